# Optimizing a Trainium2 kernel written in Bass

```python
import math
import jax, jax.numpy as jnp
from jax import lax
import numpy as np

D_MODEL = 1024
BATCH = 8
SEQ = 2048
DEPTH = 2
DEC_BATCH = 8
DEC_SEQ = 8192
PAST_LEN = 128

HEAD_DIM = 64
A_HEADS = 8
A_PATTERNS = ((128, 1), (512, 4), (2048, 16))
B_HEADS = 8
B_KV_HEADS = 2
Q_BLOCK = 128
GRID_W = 64
ROPE_THETA = 500000.0
ROPE_DIM = HEAD_DIM // 4
AXIAL_THETA = 10000.0
C_HEADS = 8
C_KEY = 128
C_VAL = 128
C_CHUNK = 64
D_FF = 2816
CONV_W = 3
ALPHA = (2 * DEPTH) ** 0.25
BETA = (8 * DEPTH) ** -0.25
LN_EPS = 1e-5
RMS_EPS = 1e-6
N_AB = (DEPTH + 1) // 2
N_C = DEPTH // 2
A_W = A_HEADS * HEAD_DIM
B_QW = B_HEADS * HEAD_DIM
B_KVW = B_KV_HEADS * HEAD_DIM
AB_IN = 3 * A_W + B_QW + 2 * B_KVW
AB_OUT = A_W + B_QW
C_W = C_HEADS * C_KEY
C_VW = C_HEADS * C_VAL
C_IN = 3 * C_W + 2 * C_VW

kernel_name = "hybrid_dilated_gqa_hgrn2_encoder"


def _layer_norm(x, g, b):
    xf = x.astype(jnp.float32)
    mu = jnp.mean(xf, -1, keepdims=True)
    var = jnp.mean(jnp.square(xf - mu), -1, keepdims=True)
    return ((xf - mu) * lax.rsqrt(var + LN_EPS) * g + b).astype(x.dtype)


def _rms_norm(x, g):
    xf = x.astype(jnp.float32)
    return (xf * lax.rsqrt(jnp.mean(xf * xf, -1, keepdims=True) + RMS_EPS) * g).astype(x.dtype)


def _rope_angles(pos, dim, theta):
    freqs = theta ** (-(jnp.arange(0, dim, 2, dtype=jnp.float32) / dim))
    ang = pos.astype(jnp.float32)[:, None] * freqs[None, :]
    return jnp.cos(ang), jnp.sin(ang)


def _apply_rope(x, cos, sin):
    half = x.shape[-1] // 2
    xf = x.astype(jnp.float32)
    x1, x2 = xf[..., :half], xf[..., half:]
    c = cos[None, :, None, :]
    s = sin[None, :, None, :]
    return jnp.concatenate([x1 * c - x2 * s, x2 * c + x1 * s], -1).astype(x.dtype)


def _dilated_branch(q, k, v, dil, n):
    B, S, H, D = q.shape
    m = S // dil
    L = n
    nb = -(-m // L)
    M = nb * L
    Bp = B * dil

    def to_cls(t):
        return t.reshape(B, m, dil, H, D).transpose(0, 2, 1, 3, 4).reshape(Bp, m, H, D)

    qc = jnp.pad(to_cls(q), ((0, 0), (0, M - m), (0, 0), (0, 0))).reshape(Bp, nb, L, H, D)

    def band(t):
        tp = jnp.pad(to_cls(t), ((0, 0), (L, M - m + L), (0, 0), (0, 0))).reshape(Bp, nb + 2, L, H, D)
        return jnp.concatenate([tp[:, :-2], tp[:, 1:-1], tp[:, 2:]], axis=2)

    kb, vb = band(k), band(v)
    s = jnp.einsum('bnqhd,bnkhd->bnhqk', qc, kb).astype(jnp.float32) * (D ** -0.5)
    qi = jnp.arange(nb)[:, None] * L + jnp.arange(L)[None, :]
    ki = jnp.arange(nb)[:, None] * L - L + jnp.arange(3 * L)[None, :]
    rel = ki[:, None, :] - qi[:, :, None]
    valid = ((jnp.abs(rel) <= n) & (ki[:, None, :] >= 0) & (ki[:, None, :] < m)) | (rel == 0)
    s = jnp.where(valid[None, :, None], s, -jnp.inf)
    lse = jax.nn.logsumexp(s, axis=-1)
    p = jnp.exp(s - lse[..., None]).astype(v.dtype)
    o = jnp.einsum('bnhqk,bnkhd->bnqhd', p, vb).reshape(Bp, M, H, D)[:, :m]
    lse = lse.transpose(0, 1, 3, 2).reshape(Bp, M, H)[:, :m]
    o = o.reshape(B, dil, m, H, D).transpose(0, 2, 1, 3, 4).reshape(B, S, H, D)
    lse = lse.reshape(B, dil, m, H).transpose(0, 2, 1, 3).reshape(B, S, H)
    return o, lse


def _dilated_attention(q, k, v):
    outs, lses = [], []
    for window, dil in A_PATTERNS:
        o, l = _dilated_branch(q, k, v, dil, window // (2 * dil))
        outs.append(o)
        lses.append(l)
    w = jax.nn.softmax(jnp.stack(lses, 0), axis=0).astype(q.dtype)
    return jnp.einsum('gbsh,gbshd->bshd', w, jnp.stack(outs, 0))


def _gqa_blocks(q, k, v):
    B, S, Hq, D = q.shape
    Hkv = k.shape[2]
    G = Hq // Hkv
    nqb = S // Q_BLOCK
    qb = q.reshape(B, nqb, Q_BLOCK, Hkv, G, D).transpose(1, 0, 2, 3, 4, 5)

    def one_block(qblk):
        s = jnp.einsum('bqkgd,bskd->bkgqs', qblk, k).astype(jnp.float32) * (D ** -0.5)
        p = jax.nn.softmax(s, axis=-1).astype(v.dtype)
        return jnp.einsum('bkgqs,bskd->bqkgd', p, v)

    o = lax.map(one_block, qb)
    return o.transpose(1, 0, 2, 3, 4, 5).reshape(B, S, Hq * D)


def _mixer_ab(x, w_in, w_out, qn, kn):
    B, S, _ = x.shape
    h = x @ w_in
    o1, o2, o3 = A_W, 2 * A_W, 3 * A_W
    o4 = o3 + B_QW
    o5 = o4 + B_KVW
    qa = h[..., :o1].reshape(B, S, A_HEADS, HEAD_DIM)
    ka = h[..., o1:o2].reshape(B, S, A_HEADS, HEAD_DIM)
    va = h[..., o2:o3].reshape(B, S, A_HEADS, HEAD_DIM)
    qb = h[..., o3:o4].reshape(B, S, B_HEADS, HEAD_DIM)
    kb = h[..., o4:o5].reshape(B, S, B_KV_HEADS, HEAD_DIM)
    vb = h[..., o5:].reshape(B, S, B_KV_HEADS, HEAD_DIM)

    cos, sin = _rope_angles(jnp.arange(S), ROPE_DIM, ROPE_THETA)

    def prope(t):
        return jnp.concatenate([_apply_rope(t[..., :ROPE_DIM], cos, sin), t[..., ROPE_DIM:]], -1)

    ya = _dilated_attention(prope(qa), prope(ka), va).reshape(B, S, A_W)

    rows = S // GRID_W
    row = jnp.broadcast_to(jnp.arange(rows)[:, None], (rows, GRID_W)).reshape(S)
    col = jnp.broadcast_to(jnp.arange(GRID_W)[None, :], (rows, GRID_W)).reshape(S)
    half = HEAD_DIM // 2
    cr, sr = _rope_angles(row, half, AXIAL_THETA)
    cc, sc = _rope_angles(col, half, AXIAL_THETA)

    def arope(t):
        return jnp.concatenate([_apply_rope(t[..., :half], cr, sr), _apply_rope(t[..., half:], cc, sc)], -1)

    qb = arope(_rms_norm(qb, qn))
    kb = arope(_rms_norm(kb, kn))
    yb = _gqa_blocks(qb, kb, vb)
    return jnp.concatenate([ya, yb], -1) @ w_out


def _hgrn2_scan(q, k, v, logf):
    B, S, H, K = q.shape
    V = v.shape[-1]
    nc = S // C_CHUNK

    def chunks(t):
        return t.reshape(B, nc, C_CHUNK, H, t.shape[-1]).swapaxes(0, 1)

    lower = jnp.tril(jnp.ones((C_CHUNK, C_CHUNK), dtype=bool))[None, :, :, None, None]

    def step(state, inp):
        qc, kc, vc, gc = inp
        b = jnp.cumsum(gc, axis=1)
        o_inter = jnp.einsum('bthk,bhkv->bthv', qc * jnp.exp(b), state)
        dec = jnp.where(lower, jnp.exp(jnp.minimum(b[:, :, None] - b[:, None, :], 0.0)), 0.0)
        attn = jnp.einsum('bthk,btshk,bshk->bths', qc, dec, kc)
        o_intra = jnp.einsum('bths,bshv->bthv', attn, vc)
        b_last = b[:, -1]
        state = jnp.exp(b_last)[..., None] * state + jnp.einsum(
            'bshk,bshv->bhkv', kc * jnp.exp(b_last[:, None] - b), vc)
        return state, o_inter + o_intra

    s0 = jnp.zeros((B, H, K, V), jnp.float32)
    _, o = lax.scan(step, s0, (chunks(q), chunks(k), chunks(v), chunks(logf)))
    return o.swapaxes(0, 1).reshape(B, S, H, V)


def _mixer_hgrn2(x, w_in, w_out, lb_fwd, lb_bwd, gn, layer):
    B, S, _ = x.shape
    h = x @ w_in
    q = h[..., :C_W].reshape(B, S, C_HEADS, C_KEY).astype(jnp.float32)
    zf = h[..., C_W:2 * C_W].reshape(B, S, C_HEADS, C_KEY).astype(jnp.float32)
    zb = h[..., 2 * C_W:3 * C_W].reshape(B, S, C_HEADS, C_KEY).astype(jnp.float32)
    i = h[..., 3 * C_W:3 * C_W + C_VW].reshape(B, S, C_HEADS, C_VAL).astype(jnp.float32)
    g = h[..., 3 * C_W + C_VW:]

    def lower_bound(tbl):
        p = jax.nn.softmax(tbl.astype(jnp.float32), axis=0)
        return (jnp.cumsum(p, axis=0) - p[0])[layer].reshape(C_HEADS, C_KEY)

    def gates(z, lb):
        f = lb + (1.0 - lb) * jax.nn.sigmoid(z)
        return jnp.log(f), 1.0 - f

    v = jax.nn.silu(i)
    logf_f, k_f = gates(zf, lower_bound(lb_fwd))
    logf_b, k_b = gates(zb, lower_bound(lb_bwd))
    o_f = _hgrn2_scan(q, k_f, v, logf_f)
    flip = lambda t: jnp.flip(t, axis=1)
    o_b = flip(_hgrn2_scan(flip(q), flip(k_b), flip(v), flip(logf_b)))
    o = _rms_norm(o_f + o_b, gn.reshape(C_HEADS, C_VAL)).astype(x.dtype)
    o = o.reshape(B, S, C_VW) * jax.nn.silu(g)
    return o @ w_out


def _conv_ffn(x, w_up, conv_w, conv_b, w_down):
    h = x @ w_up
    u, g = h[..., :D_FF], h[..., D_FF:]
    gp = jnp.pad(g, ((0, 0), (1, 1), (0, 0)))
    g = gp[:, :-2] * conv_w[0] + gp[:, 1:-1] * conv_w[1] + gp[:, 2:] * conv_w[2] + conv_b
    return (jax.nn.gelu(g) * u) @ w_down


def _trunk(x, w_in_ab, w_out_ab, qn_ab, kn_ab, w_in_c, w_out_c, lb_fwd, lb_bwd, gn_c,
           ln_mix_g, ln_mix_b, ln_ffn_g, ln_ffn_b, ffn_w_up, ffn_conv_w, ffn_conv_b, ffn_w_down):
    for l in range(DEPTH):
        j = l // 2
        if l % 2 == 0:
            y = _mixer_ab(x, w_in_ab[j], w_out_ab[j], qn_ab[j], kn_ab[j])
        else:
            y = _mixer_hgrn2(x, w_in_c[j], w_out_c[j], lb_fwd, lb_bwd, gn_c[j], l)
        x = _layer_norm(ALPHA * x + y, ln_mix_g[l], ln_mix_b[l])
        y = _conv_ffn(x, ffn_w_up[l], ffn_conv_w[l], ffn_conv_b[l], ffn_w_down[l])
        x = _layer_norm(ALPHA * x + y, ln_ffn_g[l], ln_ffn_b[l])
    return x


def setup_inputs(seed: int = 0) -> dict:
    key = jax.random.key(seed)
    ks = jax.random.split(key, 24)

    def nrm(k, shape, scale):
        return jax.random.normal(k, shape, jnp.float32) * scale

    ab_cols = jnp.ones((AB_IN,), jnp.float32).at[2 * A_W:3 * A_W].set(BETA).at[3 * A_W + B_QW + B_KVW:].set(BETA)
    c_cols = jnp.ones((C_IN,), jnp.float32).at[3 * C_W:3 * C_W + C_VW].set(BETA)
    return {
        "x_prompt": nrm(ks[0], (BATCH, SEQ, D_MODEL), 1.0),
        "x_sample": nrm(ks[1], (DEC_BATCH, DEC_SEQ, D_MODEL), 1.0),
        "w_in_ab": nrm(ks[2], (N_AB, D_MODEL, AB_IN), D_MODEL ** -0.5) * ab_cols,
        "w_out_ab": nrm(ks[3], (N_AB, AB_OUT, D_MODEL), BETA * AB_OUT ** -0.5),
        "qn_ab": 1.0 + nrm(ks[4], (N_AB, HEAD_DIM), 0.01),
        "kn_ab": 1.0 + nrm(ks[5], (N_AB, HEAD_DIM), 0.01),
        "w_in_c": nrm(ks[6], (N_C, D_MODEL, C_IN), D_MODEL ** -0.5) * c_cols,
        "w_out_c": nrm(ks[7], (N_C, C_VW, D_MODEL), BETA * C_VW ** -0.5),
        "lb_fwd": nrm(ks[8], (DEPTH, C_W), 0.1),
        "lb_bwd": nrm(ks[9], (DEPTH, C_W), 0.1),
        "gn_c": 1.0 + nrm(ks[10], (N_C, C_VW), 0.01),
        "ln_mix_g": 1.0 + nrm(ks[11], (DEPTH, D_MODEL), 0.01),
        "ln_mix_b": nrm(ks[12], (DEPTH, D_MODEL), 0.01),
        "ln_ffn_g": 1.0 + nrm(ks[13], (DEPTH, D_MODEL), 0.01),
        "ln_ffn_b": nrm(ks[14], (DEPTH, D_MODEL), 0.01),
        "ffn_w_up": nrm(ks[15], (DEPTH, D_MODEL, 2 * D_FF), D_MODEL ** -0.5),
        "ffn_conv_w": nrm(ks[16], (DEPTH, CONV_W, D_FF), CONV_W ** -0.5),
        "ffn_conv_b": nrm(ks[17], (DEPTH, D_FF), 0.01),
        "ffn_w_down": nrm(ks[18], (DEPTH, D_FF, D_MODEL), BETA * D_FF ** -0.5),
    }


def reference(x_prompt, x_sample, w_in_ab, w_out_ab, qn_ab, kn_ab, w_in_c, w_out_c, lb_fwd, lb_bwd,
              gn_c, ln_mix_g, ln_mix_b, ln_ffn_g, ln_ffn_b, ffn_w_up, ffn_conv_w, ffn_conv_b, ffn_w_down):
    y_prompt = _trunk(x_prompt, w_in_ab, w_out_ab, qn_ab, kn_ab, w_in_c, w_out_c, lb_fwd, lb_bwd, gn_c,
                      ln_mix_g, ln_mix_b, ln_ffn_g, ln_ffn_b, ffn_w_up, ffn_conv_w, ffn_conv_b, ffn_w_down)
    y_sample = _trunk(x_sample, w_in_ab, w_out_ab, qn_ab, kn_ab, w_in_c, w_out_c, lb_fwd, lb_bwd, gn_c,
                      ln_mix_g, ln_mix_b, ln_ffn_g, ln_ffn_b, ffn_w_up, ffn_conv_w, ffn_conv_b, ffn_w_down)
    return (y_prompt, y_sample)
```

```python
import math
import contextlib
import numpy as np
import ml_dtypes
import concourse.bass as bass
import concourse.mybir as mybir
from concourse.bass_utils import run_bass_kernel_spmd

F32 = mybir.dt.float32
BF16 = mybir.dt.bfloat16
AF = mybir.ActivationFunctionType
ALU = mybir.AluOpType

D = 1024
NCH = 8
DFF = 2816
NF = 22
ALPHA = 4.0 ** 0.25
LN_EPS = 1e-5
RMS_EPS = 1e-6
SAME_ENG_SYNC = True
NDMASEM = 24
STORE_Q = "pool"


class Buf:
    __slots__ = ("name", "w", "r")

    def __init__(self, name=""):
        self.name = name
        self.w = None
        self.r = []


def _flat(xs):
    out = []
    for x in xs:
        if isinstance(x, (list, tuple)):
            out.extend(_flat(x))
        else:
            out.append(x)
    return out


class Prog:
    def __init__(self, nc, marks=None):
        self.nc = nc
        self.eng = {"pe": nc.tensor, "act": nc.scalar, "dve": nc.vector, "pool": nc.gpsimd, "sp": nc.sync}
        self.marks = marks
        self.marked = []
        self.meta = []
        self.real = []
        self.ev = []
        self.dma_hist = {}
        self.dbufs = {}
        self.out_dmas = []
        if marks is not None:
            self.sems = {e: nc.alloc_semaphore("s_" + e) for e in self.eng}
            self.cnt = {e: 0 for e in self.eng}
            self.dsem = {}
            self.dcnt = {}
            self.waited = {e: {} for e in self.eng}

    def dbuf(self, *key):
        b = self.dbufs.get(key)
        if b is None:
            b = Buf(str(key))
            self.dbufs[key] = b
        return b

    def _add(self, eng, fn, reads, writes, is_dma, extra=()):
        reads = _flat(reads)
        writes = _flat(writes)
        idx = len(self.meta)
        deps = set(extra)
        for b in reads:
            if b.w is not None:
                deps.add(b.w)
        for b in writes:
            if b.w is not None:
                deps.add(b.w)
            deps.update(b.r)
        for b in writes:
            b.w = idx
            b.r = []
        for b in reads:
            if b.w != idx:
                b.r.append(idx)
        self.meta.append((eng, is_dma))
        self.real.append(fn is not None)
        fdeps = []
        for d in deps:
            de, ddma = self.meta[d]
            if (not ddma) and (not is_dma) and de == eng and (eng == "pe" or not SAME_ENG_SYNC):
                continue
            fdeps.append(d)
        if self.marks is None:
            self.marked.append(False)
            for d in fdeps:
                self.marked[d] = True
            return idx
        e = self.eng[eng]
        need = {}
        w = self.waited[eng]
        for d in fdeps:
            s, v = self.ev[d]
            key = id(s)
            if w.get(key, 0) >= v:
                continue
            if key not in need or need[key][1] < v:
                need[key] = (s, v)
        for key, (s, v) in need.items():
            e.wait_ge(s, v)
            w[key] = v
        if fn is None:
            self.ev.append(None)
            return idx
        inst = fn()
        if is_dma:
            if eng not in self.dsem:
                self.dsem[eng] = [self.nc.alloc_semaphore("d_%s_%d" % (eng, j)) for j in range(NDMASEM)]
                self.dcnt[eng] = 0
            j = self.dcnt[eng]
            self.dcnt[eng] += 1
            s = self.dsem[eng][j % NDMASEM]
            inst.then_inc(s, 16)
            self.ev.append((s, 16 * (j // NDMASEM + 1)))
        elif self.marks[idx]:
            self.cnt[eng] += 1
            inst.then_inc(self.sems[eng], 1)
            self.ev.append((self.sems[eng], self.cnt[eng]))
        else:
            self.ev.append(None)
        return idx

    def op(self, eng, fn, reads=(), writes=()):
        return self._add(eng, fn, reads, writes, False)

    def dma(self, q, fn, reads=(), writes=(), is_out=False):
        h = self.dma_hist.setdefault(q, [])
        extra = (h[-NDMASEM],) if len(h) >= NDMASEM else ()
        idx = self._add(q, fn, reads, writes, True, extra)
        h.append(idx)
        if is_out:
            self.out_dmas.append(idx)
        return idx

    def barrier(self):
        last = {}
        dmas = []
        for i, (e, isd) in enumerate(self.meta):
            if not self.real[i]:
                continue
            if isd:
                dmas.append(i)
            else:
                last[e] = i
        start = getattr(self, "_bar_from", 0)
        ex = tuple(last.values()) + tuple(d for d in dmas if d >= start)
        for e in ("pe", "act", "dve", "pool", "sp"):
            self._add(e, None, (), (), False, ex)
        self._bar_from = len(self.meta)

    def finish(self):
        self._add("sp", None, (), (), False, tuple(self.out_dmas))
        return self.marked


ALLOC = {"stack": None}


def _salloc(nc, name, shape, dtype):
    ALLOC["n"] = ALLOC.get("n", 0) + 1
    name = "%s_%d" % (name, ALLOC["n"])
    st = ALLOC["stack"]
    if st is None:
        return nc.alloc_sbuf_tensor(name, shape, dtype)
    return st.enter_context(nc.sbuf_tensor(name, shape, dtype))


class Ring:
    def __init__(self, nc, name, n, shape, dtype, psum=False, nsub=0):
        self.aps = []
        self.bufs = []
        for i in range(n):
            if psum:
                ALLOC["n"] = ALLOC.get("n", 0) + 1
                pname = "rp_%s%d_%d" % (name, i, ALLOC["n"])
                st = ALLOC["stack"]
                t = nc.alloc_psum_tensor(pname, shape, dtype) if st is None else st.enter_context(
                    nc.psum_tensor(pname, shape, dtype))
            else:
                t = _salloc(nc, "r_%s%d" % (name, i), shape, dtype)
            self.aps.append(t.ap())
            self.bufs.append([Buf("%s%d_%d" % (name, i, j)) for j in range(nsub)] if nsub else Buf("%s%d" % (name, i)))
        self.i = 0
        self.n = n

    def next(self):
        k = self.i % self.n
        self.i += 1
        return self.aps[k], self.bufs[k]


def sb(nc, name, shape, dtype, nsub=0):
    return _salloc(nc, "sb_" + name, shape, dtype).ap(), ([Buf(name + str(j)) for j in range(nsub)] if nsub else Buf(name))


def rope_perm_A():
    p = np.arange(64)
    p[0:8] = np.arange(8, 16)
    p[8:16] = np.arange(0, 8)
    return p


def rope_perm_B():
    p = np.arange(64)
    p[0:16] = np.arange(16, 32)
    p[16:32] = np.arange(0, 16)
    p[32:48] = np.arange(48, 64)
    p[48:64] = np.arange(32, 48)
    return p


def rope_tables(smax):
    t = np.arange(smax, dtype=np.float32)
    fa = (np.float32(500000.0) ** (-(np.arange(0, 16, 2, dtype=np.float32) / np.float32(16)))).astype(np.float32)
    ang = t[None, :] * fa[:, None]
    ca = np.ones((64, smax), np.float32)
    sa = np.zeros((64, smax), np.float32)
    ca[0:8] = np.cos(ang)
    ca[8:16] = np.cos(ang)
    sa[0:8] = -np.sin(ang)
    sa[8:16] = np.sin(ang)
    fb = (np.float32(10000.0) ** (-(np.arange(0, 32, 2, dtype=np.float32) / np.float32(32)))).astype(np.float32)
    row = np.floor(t / 64).astype(np.float32)
    col = (t - row * 64).astype(np.float32)
    ar = row[None, :] * fb[:, None]
    ac = col[None, :] * fb[:, None]
    cb = np.zeros((64, smax), np.float32)
    sbb = np.zeros((64, smax), np.float32)
    cb[0:16] = np.cos(ar)
    cb[16:32] = np.cos(ar)
    sbb[0:16] = -np.sin(ar)
    sbb[16:32] = np.sin(ar)
    cb[32:48] = np.cos(ac)
    cb[48:64] = np.cos(ac)
    sbb[32:48] = -np.sin(ac)
    sbb[48:64] = np.sin(ac)
    tile2 = lambda a: np.ascontiguousarray(np.concatenate([a, a], 0))
    return tile2(ca), tile2(sa), tile2(cb), tile2(sbb)


def dil_masks():
    m = np.zeros((20, 128, 512), np.float32)
    kk = np.arange(128)[:, None]
    qq = np.arange(512)[None, :]
    for i in range(20):
        dlt = (-1024 + 128 * i) + kk - qq
        a = np.abs(dlt)
        c = (a <= 64).astype(np.float32)
        c += ((dlt % 4 == 0) & (a <= 256)).astype(np.float32)
        c += ((dlt % 16 == 0) & (a <= 1024)).astype(np.float32)
        m[i] = c
    return np.ascontiguousarray(m.transpose(1, 0, 2)).astype(ml_dtypes.bfloat16)


def col128(v, nchunk):
    return np.ascontiguousarray(np.asarray(v, np.float32).reshape(nchunk, 128).T)


def ab_columns():
    A_W = 512
    qa = [h * 64 + np.arange(64) for h in range(8)]
    ka = [A_W + h * 64 + np.arange(64) for h in range(8)]
    o3 = 3 * A_W
    qb = [o3 + h * 64 + np.arange(64) for h in range(8)]
    o4 = o3 + 512
    kb = [o4 + h * 64 + np.arange(64) for h in range(2)]
    cols = []
    for c in range(4):
        cols += [qa[2 * c], qa[2 * c + 1]]
    for c in range(4):
        cols += [ka[2 * c], ka[2 * c + 1]]
    for c in range(4):
        cols += [qb[c], qb[4 + c]]
    cols += [kb[0], kb[1]]
    cols += [2 * A_W + np.arange(512)]
    o5 = o4 + 128
    cols += [o5 + np.arange(128)]
    return np.concatenate(cols)


NAB = 2304


class K:
    pass


def build(seqs, phases, debug=(), marks=None, feed=()):
    nc = bass.Bass("TRN2", target_bir_lowering=False)
    P = Prog(nc, marks)
    T = sum(seqs)
    SMAX = max(seqs)
    offs = [sum(seqs[:i]) for i in range(len(seqs))]
    k = K()
    k.nc, k.P, k.T, k.seqs, k.offs = nc, P, T, seqs, offs
    k.stq = P.eng[STORE_Q]

    def din(name, shape, dt=F32):
        return nc.dram_tensor(name, list(shape), dt, kind="ExternalInput").ap()

    def dscr(name, shape, dt):
        kind = "ExternalOutput" if name in debug else ("ExternalInput" if name in feed else "Internal")
        return nc.dram_tensor(name, list(shape), dt, kind=kind).ap()

    k.x = din("x", [T, D])
    k.w_ab = din("w_ab", [D, NAB])
    k.w_oab = din("w_oab", [D, D])
    k.w_c = din("w_c", [D, 5120])
    k.w_oc = din("w_oc", [D, D])
    k.w_up = [din("w_up%d" % l, [D, 2 * DFF]) for l in range(2)]
    k.w_dn = [din("w_dn%d" % l, [DFF, D]) for l in range(2)]
    k.t_ca = din("t_ca", [128, SMAX])
    k.t_sa = din("t_sa", [128, SMAX])
    k.t_cb = din("t_cb", [128, SMAX])
    k.t_sb = din("t_sb", [128, SMAX])
    k.masks_d = din("masks", [128, 20, 512], BF16)
    k.smallp = din("smallp", [128, NSMALL])
    k.lbrep_d = din("lbrep", [128, 4, 1024])
    k.consts_d = din("consts", [128, NCONST])
    k.y = nc.dram_tensor("y", [T, D], F32, kind="ExternalOutput").ap()
    k.xT = dscr("xT", [NCH, 128, T], F32)
    k.qaT = dscr("qaT", [4, 128, T], BF16)
    k.kaT = dscr("kaT", [4, 128, T], BF16)
    k.qbT = dscr("qbT", [4, 128, T], BF16)
    k.kbT = dscr("kbT", [1, 128, T], BF16)
    k.va = dscr("va", [T, 8 * 65], BF16)
    k.vb = dscr("vb", [T, 2 * 65], BF16)
    k.ycat = dscr("ycat", [NCH, 128, T], BF16)
    k.den = dscr("den", [16, T], F32)
    k.hq = dscr("hq", [NCH, 128, T], BF16)
    k.hv = dscr("hv", [T, 1024], BF16)
    k.x1T = dscr("x1T", [NCH, 128, T], F32)
    k.actT = dscr("actT", [NF, 128, T], BF16)
    k.x2T = dscr("x2T", [NCH, 128, T], F32)
    k.ofT = dscr("ofT", [NCH, 128, T], F32)
    k.x3T = dscr("x3T", [NCH, 128, T], F32)


    setup_consts(k)

    def run_phase(fn, *a, **kw):
        with contextlib.ExitStack() as st:
            ALLOC["stack"] = st
            if fn is not phase_att:
                k.ps = Ring(nc, "ps", 8, [128, 512], F32, psum=True)
            fn(*a, **kw)
            P.barrier()
        ALLOC["stack"] = None

    for name, fn, a, kw in phase_table(k):
        if name in phases:
            run_phase(fn, *a, **kw)
    m = P.finish()
    return nc, m


def phase_table(k):
    return [
        ("p1", phase_p1, (k,), {}),
        ("attA", phase_att, (k, "A"), {}),
        ("attB", phase_att, (k, "B"), {}),
        ("p3", phase_proj_ln, (k,), dict(src=k.ycat, resid=k.xT, dst=k.x1T, w_dram=k.w_oab, nk=8, lng=SP_LNMG0, lnb=SP_LNMB0, tag="p3", den=k.den)),
        ("f0a", phase_ffn_up, (k, 0, k.x1T), {}),
        ("f0b", phase_proj_ln, (k,), dict(src=k.actT, resid=k.x1T, dst=k.x2T, w_dram=k.w_dn[0], nk=NF, lng=SP_LNFG0, lnb=SP_LNFB0, tag="f0b")),
        ("h1", phase_hgrn, (k, True), {}),
        ("h2", phase_hgrn, (k, False), {}),
        ("p6", phase_proj_ln, (k,), dict(src=k.ycat, resid=k.x2T, dst=k.x3T, w_dram=k.w_oc, nk=8, lng=SP_LNMG1, lnb=SP_LNMB1, tag="p6")),
        ("f1a", phase_ffn_up, (k, 1, k.x3T), {}),
        ("f1b", phase_proj_ln, (k,), dict(src=k.actT, resid=k.x3T, dst=None, w_dram=k.w_dn[1], nk=NF, lng=SP_LNFG1, lnb=SP_LNFB1, tag="f1b")),
    ]


def build2(seqs, phases, debug=(), feed=()):
    _, marks = build(seqs, phases, debug, None, feed)
    nc, _ = build(seqs, phases, debug, marks, feed)
    return nc


SP_LNMG0, SP_LNMB0, SP_LNFG0, SP_LNFB0 = 0, 8, 16, 24
SP_LNMG1, SP_LNMB1, SP_LNFG1, SP_LNFB1 = 32, 40, 48, 56
SP_QN, SP_QNP, SP_KN, SP_KNP = 64, 65, 66, 67
SP_GNC = 68
SP_CW0 = 76
SP_CB0 = SP_CW0 + 66
SP_CW1 = SP_CB0 + 22
SP_CB1 = SP_CW1 + 66
SP_LBF = SP_CB1 + 22
SP_LBB = SP_LBF + 16
NSMALL = SP_LBB + 16

C_IDENT = 0
C_ONESBLK = 128
C_ONES = 256
C_TRIL = 384
C_TRIU_S = 512
C_TRIU = 640
C_TRIL_S = 768
C_ONESV = 896
C_PERMA = 1024
C_PERMB = 1152
NCONST = 1280


def host_consts():
    c = np.zeros((128, NCONST), np.float32)
    c[:, C_IDENT:C_IDENT + 128] = np.eye(128, dtype=np.float32)
    blk = np.zeros((128, 128), np.float32)
    blk[0:64, 0:64] = 1.0 / 64
    blk[64:128, 64:128] = 1.0 / 64
    c[:, C_ONESBLK:C_ONESBLK + 128] = blk
    c[:, C_ONES:C_ONES + 128] = 1.0
    c[:, C_ONESV:C_ONESV + 128] = 1.0 / 128
    for col, pm in ((C_PERMA, rope_perm_A()), (C_PERMB, rope_perm_B())):
        for m in range(128):
            c[64 * (m // 64) + pm[m % 64], col + m] = 1.0
    s = np.arange(128)[:, None]
    t = np.arange(128)[None, :]
    same = (s // 64) == (t // 64)
    c[:, C_TRIL:C_TRIL + 128] = (same & (s <= t))
    c[:, C_TRIU_S:C_TRIU_S + 128] = (same & (s > t))
    c[:, C_TRIU:C_TRIU + 128] = (same & (s >= t))
    c[:, C_TRIL_S:C_TRIL_S + 128] = (same & (s < t))
    return c


def setup_consts(k):
    nc, P = k.nc, k.P
    k.consts, k.consts_b = sb(nc, "consts", [128, NCONST], F32)
    k.small, k.small_b = sb(nc, "small", [128, NSMALL], F32)
    P.dma("sp", lambda: nc.sync.dma_start(out=k.consts, in_=k.consts_d), [], [k.consts_b])
    P.dma("sp", lambda: nc.sync.dma_start(out=k.small, in_=k.smallp), [], [k.small_b])
    k.cbf, k.cbf_b = sb(nc, "cbf", [128, NCONST], BF16)
    P.op("dve", lambda: nc.vector.tensor_copy(out=k.cbf, in_=k.consts), [k.consts_b], [k.cbf_b])
    k.onesd, k.onesd_b = sb(nc, "onesd", [128, 128], BF16)
    P.op("dve", lambda: nc.vector.memset(k.onesd, 1.0 / 1024), [], [k.onesd_b])
    k.epsr, k.epsr_b = sb(nc, "epsr", [128, 1], F32)
    P.op("dve", lambda: nc.vector.memset(k.epsr, RMS_EPS), [], [k.epsr_b])
    k.epsl, k.epsl_b = sb(nc, "epsl", [128, 1], F32)
    P.op("dve", lambda: nc.vector.memset(k.epsl, LN_EPS), [], [k.epsl_b])


def load_weight_bf16(k, name, w_dram, nk, ncols, col0=0, sw=2048):
    nc, P = k.nc, k.P
    nchunk = nk * (-(-ncols // sw))
    wt, wb = sb(nc, name, [128, nk, ncols], BF16, nsub=nchunk)
    wstage = Ring(nc, name + "_stg", 2, [128, sw], F32)
    i = 0
    for kk in range(nk):
        for c0 in range(0, ncols, sw):
            cw = min(sw, ncols - c0)
            st, stb = wstage.next()
            P.dma("sp", (lambda st=st, kk=kk, c0=c0, cw=cw: nc.sync.dma_start(
                out=st[:, 0:cw], in_=w_dram[kk * 128:(kk + 1) * 128, col0 + c0:col0 + c0 + cw])), [], [stb])
            if i % 2 == 0:
                P.op("act", (lambda st=st, kk=kk, c0=c0, cw=cw: nc.scalar.copy(
                    out=wt[:, kk, c0:c0 + cw], in_=st[:, 0:cw])), [stb], [wb[i]])
            else:
                P.op("dve", (lambda st=st, kk=kk, c0=c0, cw=cw: nc.vector.tensor_copy(
                    out=wt[:, kk, c0:c0 + cw], in_=st[:, 0:cw])), [stb], [wb[i]])
            i += 1
    return wt, wb


def tiles512(k):
    for si, (off, S) in enumerate(zip(k.offs, k.seqs)):
        for j in range(S // 512):
            yield si, off, S, j * 512, off + j * 512


def phase_p1(k):
    nc, P = k.nc, k.P
    W, Wb = load_weight_bf16(k, "w_ab_sb", k.w_ab, 8, NAB)
    hbR = Ring(nc, "p1hb", 3, [128, 512], BF16)
    permA = k.cbf[:, C_PERMA:C_PERMA + 128]
    permB = k.cbf[:, C_PERMB:C_PERMB + 128]
    xtok = Ring(nc, "xtok", 2, [128, 4, D], F32)
    xT32 = Ring(nc, "xT32", 1, [128, NCH, 512], F32, nsub=NCH)
    xTb = Ring(nc, "xTb", 2, [128, NCH, 512], BF16, nsub=NCH)
    tab = Ring(nc, "ropetab", 1, [128, 4, 512], F32)
    t1r = Ring(nc, "p1t1", 2, [128, 512], F32)
    t2r = Ring(nc, "p1t2", 2, [128, 512], F32)
    sqr = Ring(nc, "p1sq", 2, [128, 512], F32)
    rsr = Ring(nc, "p1rs", 2, [128, 512], F32)
    qkout = Ring(nc, "p1qk", 1, [128, 13, 512], BF16, nsub=13)
    vaug = Ring(nc, "p1va", 2, [128, 4, 8 * 65], BF16)
    vbug = Ring(nc, "p1vb", 2, [128, 4, 2 * 65], BF16)
    for r in (vaug, vbug):
        for ap, b in zip(r.aps, r.bufs):
            P.op("pool", (lambda ap=ap: nc.gpsimd.memset(ap, 1.0)), [], [b])
    ident = k.consts[:, C_IDENT:C_IDENT + 128]
    onesblk = k.consts[:, C_ONESBLK:C_ONESBLK + 128]
    sm = k.small
    for si, off, S, p0, g0 in tiles512(k):
        xt, xtb = xtok.next()
        P.dma("sp", (lambda xt=xt, g0=g0: nc.sync.dma_start(
            out=xt, in_=k.x[g0:g0 + 512, :].rearrange("(g p) d -> p g d", p=128))), [], [xtb])
        tb, tbb = tab.next()
        for i, src in enumerate((k.t_ca, k.t_sa, k.t_cb, k.t_sb)):
            P.dma("sp", (lambda tb=tb, i=i, src=src, p0=p0: nc.sync.dma_start(
                out=tb[:, i, :], in_=src[:, p0:p0 + 512])), [], [tbb])
        x32, x32b = xT32.next()
        xb, xbb = xTb.next()
        for c in range(NCH):
            ps, psb = k.ps.next()
            for g in range(4):
                P.op("pe", (lambda ps=ps, xt=xt, g=g, c=c: nc.tensor.matmul(
                    ps[:, g * 128:(g + 1) * 128], lhsT=xt[:, g, c * 128:(c + 1) * 128], rhs=ident,
                    start=True, stop=True)), [xtb, k.consts_b], [psb])
            P.op("act", (lambda ps=ps, x32=x32, c=c: nc.scalar.copy(out=x32[:, c, :], in_=ps)), [psb], [x32b[c]])
            P.op("dve", (lambda x32=x32, xb=xb, c=c: nc.vector.tensor_copy(out=xb[:, c, :], in_=x32[:, c, :])), [x32b[c]], [xbb[c]])
        P.dma(STORE_Q, (lambda x32=x32, g0=g0: k.stq.dma_start(
            out=k.xT[:, :, g0:g0 + 512].rearrange("c p t -> p c t"), in_=x32)),
            [x32b], [P.dbuf("xT", g0)])

        def proj(chunk):
            ps, psb = k.ps.next()
            for c in range(NCH):
                P.op("pe", (lambda ps=ps, c=c, chunk=chunk: nc.tensor.matmul(
                    ps, lhsT=W[:, c, chunk * 128:(chunk + 1) * 128], rhs=xb[:, c, :],
                    start=(c == 0), stop=(c == NCH - 1))), [Wb, xbb[c]], [psb])
            return ps, psb

        def rot(ps, psb, permM):
            hb, hbb = hbR.next()
            P.op("act", (lambda: nc.scalar.copy(out=hb, in_=ps)), [psb], [hbb])
            pp, ppb = k.ps.next()
            P.op("pe", (lambda: nc.tensor.matmul(pp, lhsT=permM, rhs=hb, start=True, stop=True)), [hbb, k.cbf_b], [ppb])
            return pp, ppb, hbb

        qo, qob = qkout.next()
        for which, base in ((0, 0), (1, 4)):
            for c in range(4):
                ps, psb = proj(base + c)
                pp, ppb, hbb = rot(ps, psb, permA)
                t1, t1b = t1r.next()
                t2, t2b = t2r.next()
                P.op("dve", (lambda t1=t1, ps=ps: nc.vector.tensor_tensor(
                    out=t1, in0=ps, in1=tb[:, 0, :], op=ALU.mult)), [psb, tbb, hbb], [t1b])
                P.op("dve", (lambda t2=t2, pp=pp: nc.vector.tensor_tensor(
                    out=t2, in0=pp, in1=tb[:, 1, :], op=ALU.mult)), [ppb, tbb], [t2b])
                slot = which * 4 + c
                P.op("pool", (lambda t1=t1, t2=t2, slot=slot: nc.gpsimd.tensor_tensor(
                    out=qo[:, slot, :], in0=t1, in1=t2, op=ALU.add)), [t1b, t2b], [qob[slot]])
        for which, base, n, gcol in ((0, 8, 4, SP_QN), (1, 12, 1, SP_KN)):
            for c in range(n):
                ps, psb = proj(base + c)
                pp, ppb, hbb = rot(ps, psb, permB)
                sq, sqb = sqr.next()
                P.op("act", (lambda sq=sq, ps=ps: nc.scalar.activation(out=sq, in_=ps, func=AF.Square)), [psb], [sqb])
                ms, msb = k.ps.next()
                P.op("pe", (lambda ms=ms, sq=sq: nc.tensor.matmul(ms, lhsT=onesblk, rhs=sq, start=True, stop=True)),
                     [sqb, k.consts_b], [msb])
                rs, rsb = rsr.next()
                P.op("act", (lambda rs=rs, ms=ms: nc.scalar.activation(
                    out=rs, in_=ms, func=AF.Ln, bias=k.epsr[:, 0:1], scale=1.0)), [msb, k.epsr_b], [rsb])
                P.op("act", (lambda rs=rs: nc.scalar.activation(out=rs, in_=rs, func=AF.Exp, scale=-0.5)), [rsb], [rsb])
                t1, t1b = t1r.next()
                t2, t2b = t2r.next()
                P.op("dve", (lambda t1=t1, ps=ps, gcol=gcol: nc.vector.scalar_tensor_tensor(
                    out=t1, in0=ps, scalar=sm[:, gcol:gcol + 1], in1=tb[:, 2, :], op0=ALU.mult, op1=ALU.mult)),
                    [psb, tbb, k.small_b, sqb, hbb], [t1b])
                P.op("dve", (lambda t2=t2, pp=pp, gcol=gcol: nc.vector.scalar_tensor_tensor(
                    out=t2, in0=pp, scalar=sm[:, gcol + 1:gcol + 2], in1=tb[:, 3, :], op0=ALU.mult, op1=ALU.mult)),
                    [ppb, tbb, k.small_b], [t2b])
                P.op("pool", (lambda t1=t1, t2=t2: nc.gpsimd.tensor_tensor(
                    out=t1, in0=t1, in1=t2, op=ALU.add)), [t1b, t2b], [t1b])
                slot = 8 + c if which == 0 else 12
                P.op("dve", (lambda t1=t1, rs=rs, slot=slot: nc.vector.tensor_tensor(
                    out=qo[:, slot, :], in0=t1, in1=rs, op=ALU.mult)), [t1b, rsb], [qob[slot]])
        for dst, s0, n in ((k.qaT, 0, 4), (k.kaT, 4, 4), (k.qbT, 8, 4), (k.kbT, 12, 1)):
            P.dma(STORE_Q, (lambda dst=dst, s0=s0, n=n, g0=g0: k.stq.dma_start(
                out=dst[:, :, g0:g0 + 512].rearrange("c p t -> p c t"), in_=qo[:, s0:s0 + n, :])),
                [qob], [P.dbuf("qk", id(dst), g0)])
        va, vab = vaug.next()
        vb, vbb = vbug.next()
        for g in range(4):
            ps, psb = k.ps.next()
            for c in range(NCH):
                P.op("pe", (lambda ps=ps, c=c, g=g: nc.tensor.matmul(
                    ps, lhsT=xb[:, c, g * 128:(g + 1) * 128], rhs=W[:, c, 1664:2176],
                    start=(c == 0), stop=(c == NCH - 1))), [Wb, xbb[c]], [psb])
            P.op("act", (lambda ps=ps, va=va, g=g: nc.scalar.copy(
                out=va[:, g, :].rearrange("p (h e) -> p h e", e=65)[:, :, 0:64],
                in_=ps.rearrange("p (h e) -> p h e", e=64))), [psb], [vab])
            ps2, ps2b = k.ps.next()
            for c in range(NCH):
                P.op("pe", (lambda ps2=ps2, c=c, g=g: nc.tensor.matmul(
                    ps2[:, 0:128], lhsT=xb[:, c, g * 128:(g + 1) * 128], rhs=W[:, c, 2176:2304],
                    start=(c == 0), stop=(c == NCH - 1))), [Wb, xbb[c]], [ps2b])
            P.op("act", (lambda ps2=ps2, vb=vb, g=g: nc.scalar.copy(
                out=vb[:, g, :].rearrange("p (h e) -> p h e", e=65)[:, :, 0:64],
                in_=ps2[:, 0:128].rearrange("p (h e) -> p h e", e=64))), [ps2b], [vbb])
        P.dma(STORE_Q, (lambda va=va, g0=g0: k.stq.dma_start(
            out=k.va[g0:g0 + 512, :].rearrange("(g p) f -> p g f", p=128), in_=va)), [vab], [P.dbuf("va", g0)])
        P.dma(STORE_Q, (lambda vb=vb, g0=g0: k.stq.dma_start(
            out=k.vb[g0:g0 + 512, :].rearrange("(g p) f -> p g f", p=128), in_=vb)), [vbb], [P.dbuf("vb", g0)])


def phase_att(k, which):
    nc, P = k.nc, k.P
    isA = which == "A"
    SMAX = max(k.seqs)
    nktm = SMAX // 128
    sR = Ring(nc, "att_s", 2, [128, 1024], F32, psum=True)
    oR = Ring(nc, "att_O", 4, [128, 512], F32, psum=True)
    qT = Ring(nc, "att_q", 2, [128, SMAX], BF16)
    kT = Ring(nc, "att_k", 2 if isA else 1, [128, SMAX], BF16)
    vR = Ring(nc, "att_v", 2 if isA else 1, [128, nktm, 130], BF16)
    pr = Ring(nc, "att_p", 3, [128, 1024], BF16)
    pm = Ring(nc, "att_pm", 3, [128, 1024], BF16, nsub=2) if isA else None
    ysb = Ring(nc, "att_y", 4, [64, 512], BF16)
    dsb = Ring(nc, "att_d", 4, [65, 512], F32)
    if isA:
        masks, masks_b = sb(nc, "masks", [128, 20, 512], BF16)
        P.dma("sp", lambda: nc.sync.dma_start(out=masks, in_=k.masks_d), [], [masks_b])
    for si, (off, S) in enumerate(zip(k.offs, k.seqs)):
        nkt = S // 128
        nqb = S // 512
        if not isA:
            kt_, ktb = kT.next()
            P.dma("sp", (lambda: nc.sync.dma_start(out=kt_[:, 0:S], in_=k.kbT[0, :, off:off + S])), [], [ktb])
            v_, vb_ = vR.next()
            P.dma("sp", (lambda: nc.sync.dma_start(
                out=v_[:, 0:nkt, :], in_=k.vb[off:off + S, :].rearrange("(t p) f -> p t f", p=128))), [], [vb_])
        for c in range(4):
            q_, qb_ = qT.next()
            qsrc = k.qaT if isA else k.qbT
            P.dma("sp", (lambda: nc.sync.dma_start(out=q_[:, 0:S], in_=qsrc[c, :, off:off + S])), [], [qb_])
            if isA:
                kt_, ktb = kT.next()
                P.dma("sp", (lambda: nc.sync.dma_start(out=kt_[:, 0:S], in_=k.kaT[c, :, off:off + S])), [], [ktb])
                v_, vb_ = vR.next()
                P.dma("sp", (lambda: nc.sync.dma_start(
                    out=v_[:, 0:nkt, :],
                    in_=k.va[off:off + S, c * 130:(c + 1) * 130].rearrange("(t p) f -> p t f", p=128))), [], [vb_])
            items = []
            for qi in range(nqb):
                q0 = qi * 512
                if isA:
                    kts = list(range(max(0, (q0 - 1024) // 128), min(nkt - 1, (q0 + 1535) // 128) + 1))
                else:
                    kts = list(range(nkt))
                for ii, kt in enumerate(kts):
                    items.append((qi, ii, kt, len(kts)))
            sq = {}

            def issue_qk(j):
                qi, ii, kt, n = items[j]
                s_, sb_ = sR.next()
                for hh in range(2):
                    pb = 64 * hh
                    P.op("pe", (lambda: nc.tensor.matmul(
                        s_[:, hh * 512:(hh + 1) * 512], lhsT=kt_[pb:pb + 64, kt * 128:kt * 128 + 128],
                        rhs=q_[pb:pb + 64, qi * 512:qi * 512 + 512], start=True, stop=True)), [ktb, qb_], [sb_])
                sq[j] = (s_, sb_)

            issue_qk(0)
            O = None
            for j, (qi, ii, kt, n) in enumerate(items):
                q0 = qi * 512
                k0 = kt * 128
                if ii == 0:
                    O = [oR.next(), oR.next()]
                s_, sb_ = sq.pop(j)
                p_, pb_ = pr.next()
                P.op("act", (lambda: nc.scalar.activation(out=p_, in_=s_, func=AF.Exp, scale=0.125)), [sb_], [pb_])
                if isA:
                    mi = (k0 - q0 + 1024) // 128
                    p2, p2b = pm.next()
                    for hh in range(2):
                        eng_, e_ = (("dve", nc.vector), ("pool", nc.gpsimd))[hh if MASK_SPLIT else 0]
                        P.op(eng_, (lambda: e_.tensor_tensor(
                            out=p2[:, hh * 512:(hh + 1) * 512], in0=p_[:, hh * 512:(hh + 1) * 512],
                            in1=masks[:, mi, :], op=ALU.mult)), [pb_, masks_b], [p2b[hh]])
                    p_, pb_ = p2, p2b
                if j + 1 < len(items):
                    issue_qk(j + 1)
                for hh in range(2):
                    P.op("pe", (lambda: nc.tensor.matmul(
                        O[hh][0][0:65, :], lhsT=v_[:, kt, hh * 65:(hh + 1) * 65], rhs=p_[:, hh * 512:(hh + 1) * 512],
                        start=(ii == 0), stop=(ii == n - 1))), [vb_, (pb_[hh] if isinstance(pb_, list) else pb_)], [O[hh][1]])
                if ii == n - 1:
                    g0 = off + q0
                    for hh in range(2):
                        if isA:
                            head = 2 * c + hh
                            ochunk, obase, drow = c, 64 * hh, head
                        else:
                            head = c + 4 * hh
                            ochunk, obase, drow = 4 + head // 2, 64 * (head % 2), 8 + head
                        ops, opsb = O[hh]
                        y_, yb_ = ysb.next()
                        d_, db_ = dsb.next()
                        P.op("act", (lambda: nc.scalar.copy(out=y_, in_=ops[0:64, :])), [opsb], [yb_])
                        P.op("act", (lambda: nc.scalar.activation(out=d_[64:65, :], in_=ops[64:65, :], func=AF.Ln)),
                             [opsb], [db_])
                        P.op("act", (lambda: nc.scalar.activation(out=d_[64:65, :], in_=d_[64:65, :], func=AF.Exp,
                                                                  scale=-1.0)), [db_], [db_])
                        P.dma(STORE_Q, (lambda: k.stq.dma_start(
                            out=k.ycat[ochunk, obase:obase + 64, g0:g0 + 512], in_=y_)), [yb_], [P.dbuf("ycat", head, g0)])
                        P.dma(STORE_Q, (lambda: k.stq.dma_start(
                            out=k.den[drow:drow + 1, g0:g0 + 512], in_=d_[64:65, :])), [db_], [P.dbuf("den", drow, g0)])


def layer_norm_tile(k, r, rb_, lng, lnb, rings, out, outb):
    nc, P = k.nc, k.P
    sm = k.small
    rbf, r2f, stat = rings
    rb16, rb16b = rbf.next()
    r2, r2b = r2f.next()
    for n in range(NCH):
        P.op("act", (lambda n=n: nc.scalar.copy(out=rb16[:, n, :], in_=r[:, n, :])), [rb_[n]], [rb16b[n]])
        P.op("act", (lambda n=n: nc.scalar.activation(out=r2[:, n, :], in_=r[:, n, :], func=AF.Square)), [rb_[n]], [r2b[n]])
    mps, mpsb = k.ps.next()
    eps_, epsb = k.ps.next()
    for n in range(NCH):
        P.op("pe", (lambda n=n: nc.tensor.matmul(mps, lhsT=k.onesd, rhs=rb16[:, n, :], start=(n == 0), stop=(n == NCH - 1))),
             [k.onesd_b, rb16b[n]], [mpsb])
    for n in range(NCH):
        P.op("pe", (lambda n=n: nc.tensor.matmul(eps_, lhsT=k.onesd, rhs=r2[:, n, :], start=(n == 0), stop=(n == NCH - 1))),
             [k.onesd_b, r2b[n]], [epsb])
    m2, m2b = stat.next()
    P.op("act", (lambda: nc.scalar.activation(out=m2, in_=mps, func=AF.Square)), [mpsb], [m2b])
    P.op("dve", (lambda: nc.vector.tensor_tensor(out=m2, in0=eps_, in1=m2, op=ALU.subtract)), [epsb, m2b], [m2b])
    P.op("act", (lambda: nc.scalar.activation(out=m2, in_=m2, func=AF.Ln, bias=k.epsl[:, 0:1], scale=1.0)),
         [m2b, k.epsl_b], [m2b])
    P.op("act", (lambda: nc.scalar.activation(out=m2, in_=m2, func=AF.Exp, scale=-0.5)), [m2b], [m2b])
    for n in range(NCH):
        P.op("dve", (lambda n=n: nc.vector.tensor_tensor(out=r[:, n, :], in0=r[:, n, :], in1=mps, op=ALU.subtract)),
             [rb_[n], mpsb], [rb_[n]])
        P.op("dve", (lambda n=n: nc.vector.scalar_tensor_tensor(
            out=r[:, n, :], in0=r[:, n, :], scalar=sm[:, lng + n:lng + n + 1], in1=m2, op0=ALU.mult, op1=ALU.mult)),
            [rb_[n], m2b, k.small_b], [rb_[n]])
        P.op("act", (lambda n=n: nc.scalar.activation(
            out=out[:, n, :], in_=r[:, n, :], func=AF.Identity, bias=sm[:, lnb + n:lnb + n + 1], scale=1.0)),
            [rb_[n], k.small_b], [outb[n]])


def ln_rings(k, tag):
    nc = k.nc
    return (Ring(nc, tag + "_rb16", 1, [128, NCH, 512], BF16, nsub=NCH), Ring(nc, tag + "_r2", 1, [128, NCH, 512], BF16, nsub=NCH),
            Ring(nc, tag + "_stat", 2, [128, 512], F32))


def phase_proj_ln(k, src, resid, dst, w_dram, nk, lng, lnb, tag, den=None):
    nc, P = k.nc, k.P
    W, Wb = load_weight_bf16(k, "w_" + tag, w_dram, nk, D)
    srcR = Ring(nc, tag + "_src", 2, [128, nk, 512], BF16)
    resR = Ring(nc, tag + "_res", 1 if nk > 8 else 2, [128, NCH, 512], F32)
    rR = Ring(nc, tag + "_r", 2, [128, NCH, 512], F32, nsub=NCH)
    outR = Ring(nc, tag + "_out", 1 if nk > 8 else 2, [128, NCH, 512], F32, nsub=NCH)
    lnr = ln_rings(k, tag)
    if den is not None:
        dnR = Ring(nc, tag + "_dn", 1, [128, nk, 512], F32)
    if dst is None:
        ytok = Ring(nc, tag + "_ytok", 2, [128, D], F32)
        ident = k.consts[:, C_IDENT:C_IDENT + 128]
    for si, off, S, p0, g0 in tiles512(k):
        s_, sb_ = srcR.next()
        P.dma("sp", (lambda: nc.sync.dma_start(out=s_, in_=src[:, :, g0:g0 + 512].rearrange("c p t -> p c t"))), [], [sb_])
        if den is not None:
            dn, dnb = dnR.next()
            for half in range(2):
                P.dma("sp", (lambda: nc.sync.dma_start(
                    out=dn[64 * half:64 * half + 64, :, :],
                    in_=den[:, g0:g0 + 512].rearrange("(n h) t -> h n t", h=2)[half:half + 1].broadcast_to([64, nk, 512]))),
                    [], [dnb])
            P.op("dve", (lambda: nc.vector.tensor_tensor(
                out=s_.rearrange("p c t -> p (c t)"), in0=s_.rearrange("p c t -> p (c t)"),
                in1=dn.rearrange("p c t -> p (c t)"), op=ALU.mult)), [sb_, dnb], [sb_])
        x_, xb_ = resR.next()
        P.dma("sp", (lambda: nc.sync.dma_start(out=x_, in_=resid[:, :, g0:g0 + 512].rearrange("c p t -> p c t"))), [], [xb_])
        r, rb_ = rR.next()
        for n in range(NCH):
            ps, psb = k.ps.next()
            for kk in range(nk):
                P.op("pe", (lambda: nc.tensor.matmul(ps, lhsT=W[:, kk, n * 128:(n + 1) * 128], rhs=s_[:, kk, :],
                                                     start=(kk == 0), stop=(kk == nk - 1))), [Wb, sb_], [psb])
            P.op("dve", (lambda: nc.vector.scalar_tensor_tensor(
                out=r[:, n, :], in0=x_[:, n, :], scalar=ALPHA, in1=ps, op0=ALU.mult, op1=ALU.add)), [xb_, psb], [rb_[n]])
        o_, ob_ = outR.next()
        layer_norm_tile(k, r, rb_, lng, lnb, lnr, o_, ob_)
        if dst is not None:
            P.dma(STORE_Q, (lambda: k.stq.dma_start(out=dst[:, :, g0:g0 + 512].rearrange("c p t -> p c t"), in_=o_)),
                  [ob_], [P.dbuf(tag, g0)])
        else:
            for g in range(4):
                yt, ytb = ytok.next()
                for half in range(2):
                    ps, psb = k.ps.next()
                    for j in range(4):
                        n = half * 4 + j
                        P.op("pe", (lambda: nc.tensor.matmul(
                            ps[:, j * 128:(j + 1) * 128], lhsT=o_[:, n, g * 128:(g + 1) * 128], rhs=ident,
                            start=True, stop=True)), [ob_[n], k.consts_b], [psb])
                    if half == 0:
                        P.op("act", (lambda: nc.scalar.copy(out=yt[:, 0:512], in_=ps)), [psb], [ytb])
                    else:
                        P.op("dve", (lambda: nc.vector.tensor_copy(out=yt[:, 512:1024], in_=ps)), [psb], [ytb])
                r0 = g0 + g * 128
                P.dma(STORE_Q, (lambda: k.stq.dma_start(out=k.y[r0:r0 + 128, :], in_=yt)), [ytb],
                      [P.dbuf("y", r0)], is_out=True)


GELU_NATIVE = True
MASK_SPLIT = False


def phase_ffn_up(k, l, xsrc):
    nc, P = k.nc, k.P
    W, Wb = load_weight_bf16(k, "w_up%d" % l, k.w_up[l], 8, 2 * DFF)
    cw = SP_CW0 if l == 0 else SP_CW1
    cbc = SP_CB0 if l == 0 else SP_CB1
    sm = k.small
    stg = Ring(nc, "fu_stg", 3, [128, 512], F32)
    xbR = Ring(nc, "fu_xb", 2, [128, NCH, 512], BF16, nsub=NCH)
    tR = Ring(nc, "fu_t", 3, [128, 512], F32)
    gR = Ring(nc, "fu_g", 2, [128, 512], F32)
    actR = Ring(nc, "fu_act", 2, [128, NF, 512], BF16, nsub=NF)
    tl = []
    for si, (off, S) in enumerate(zip(k.offs, k.seqs)):
        nt = -(-S // 510)
        wbase = -(-S // nt)
        wbase += wbase % 2
        a = 0
        while a < S:
            w = min(wbase, S - a)
            tl.append((off, S, a, w))
            a += w
    xq = {}

    def load_x(ti):
        off, S, a, w = tl[ti]
        lo, hi = max(a - 1, 0), min(a + w + 1, S)
        dcol = lo - (a - 1)
        xb, xbb = xbR.next()
        if a == 0:
            P.op("pool", (lambda: nc.gpsimd.memset(xb[:, :, 0:1], 0.0)), [], [xbb])
        if a + w == S:
            P.op("pool", (lambda: nc.gpsimd.memset(xb[:, :, w + 1:w + 2], 0.0)), [], [xbb])
        for c in range(NCH):
            st, stb = stg.next()
            P.dma("sp", (lambda: nc.sync.dma_start(out=st[:, 0:hi - lo], in_=xsrc[c, :, off + lo:off + hi])), [], [stb])
            P.op("pool", (lambda: nc.gpsimd.tensor_copy(out=xb[:, c, dcol:dcol + hi - lo], in_=st[:, 0:hi - lo])),
                 [stb], [xbb[c]])
        xq[ti] = (xb, xbb)

    load_x(0)
    for ti, (off, S, a, w) in enumerate(tl):
        if True:
            xb, xbb = xq.pop(ti)
            if ti + 1 < len(tl):
                load_x(ti + 1)
            act, actb = actR.next()
            for f in range(NF):
                gps, gpsb = k.ps.next()
                for c in range(NCH):
                    P.op("pe", (lambda: nc.tensor.matmul(
                        gps[:, 0:w + 2], lhsT=W[:, c, DFF + f * 128:DFF + (f + 1) * 128], rhs=xb[:, c, 0:w + 2],
                        start=(c == 0), stop=(c == NCH - 1))), [Wb, xbb[c]], [gpsb])
                ups, upsb = k.ps.next()
                for c in range(NCH):
                    P.op("pe", (lambda: nc.tensor.matmul(
                        ups[:, 0:w], lhsT=W[:, c, f * 128:(f + 1) * 128], rhs=xb[:, c, 1:w + 1],
                        start=(c == 0), stop=(c == NCH - 1))), [Wb, xbb[c]], [upsb])
                t, tb_ = tR.next()
                P.op("act", (lambda: nc.scalar.activation(
                    out=t[:, 0:w], in_=gps[:, 1:w + 1], func=AF.Identity,
                    scale=sm[:, cw + NF + f:cw + NF + f + 1], bias=sm[:, cbc + f:cbc + f + 1])), [gpsb, k.small_b], [tb_])
                P.op("dve", (lambda: nc.vector.scalar_tensor_tensor(
                    out=t[:, 0:w], in0=gps[:, 0:w], scalar=sm[:, cw + f:cw + f + 1], in1=t[:, 0:w],
                    op0=ALU.mult, op1=ALU.add)), [gpsb, tb_, k.small_b], [tb_])
                P.op("dve", (lambda: nc.vector.scalar_tensor_tensor(
                    out=t[:, 0:w], in0=gps[:, 2:w + 2], scalar=sm[:, cw + 2 * NF + f:cw + 2 * NF + f + 1], in1=t[:, 0:w],
                    op0=ALU.mult, op1=ALU.add)), [gpsb, tb_, k.small_b], [tb_])
                ge, geb = gR.next()
                if GELU_NATIVE:
                    P.op("act", (lambda: nc.scalar.activation(out=ge[:, 0:w], in_=t[:, 0:w], func=AF.Gelu_apprx_tanh)),
                         [tb_], [geb])
                else:
                    P.op("act", (lambda: nc.scalar.activation(out=ge[:, 0:w], in_=t[:, 0:w], func=AF.Square)), [tb_], [geb])
                    P.op("pool", (lambda: nc.gpsimd.tensor_scalar(
                        out=ge[:, 0:w], in0=ge[:, 0:w], scalar1=0.044715, scalar2=1.0, op0=ALU.mult, op1=ALU.add)),
                        [geb], [geb])
                    P.op("pool", (lambda: nc.gpsimd.tensor_tensor(out=ge[:, 0:w], in0=ge[:, 0:w], in1=t[:, 0:w], op=ALU.mult)),
                         [geb, tb_], [geb])
                    P.op("act", (lambda: nc.scalar.activation(out=ge[:, 0:w], in_=ge[:, 0:w], func=AF.Sigmoid,
                                                              scale=1.5957691216057308)), [geb], [geb])
                    P.op("pool", (lambda: nc.gpsimd.tensor_tensor(out=ge[:, 0:w], in0=ge[:, 0:w], in1=t[:, 0:w], op=ALU.mult)),
                         [geb, tb_], [geb])
                P.op("dve", (lambda: nc.vector.tensor_tensor(out=act[:, f, 0:w], in0=ge[:, 0:w], in1=ups[:, 0:w], op=ALU.mult)),
                     [geb, upsb], [actb[f]])
            g0 = off + a
            P.dma(STORE_Q, (lambda: k.stq.dma_start(
                out=k.actT[:, :, g0:g0 + w].rearrange("f p t -> p f t"), in_=act[:, :, 0:w])), [actb], [P.dbuf("actT", g0)])


def phase_hgrn(k, fwd):
    nc, P = k.nc, k.P
    sm = k.small
    Wz, Wzb = load_weight_bf16(k, "hw_z", k.w_c, 8, 1024, col0=(1024 if fwd else 2048), sw=1024)
    if fwd:
        Wq, Wqb = load_weight_bf16(k, "hw_q", k.w_c, 8, 1024, col0=0, sw=1024)
        Wi, Wib = load_weight_bf16(k, "hw_i", k.w_c, 8, 1024, col0=3072, sw=1024)
    if not fwd:
        Wg, Wgb = load_weight_bf16(k, "hw_g", k.w_c, 8, 1024, col0=4096, sw=1024)
    C = k.consts
    tri_inc = C[:, C_TRIL:C_TRIL + 128] if fwd else C[:, C_TRIU:C_TRIU + 128]
    tri_suf = C[:, C_TRIU_S:C_TRIU_S + 128] if fwd else C[:, C_TRIL_S:C_TRIL_S + 128]
    onesv = C[:, C_ONESV:C_ONESV + 128]
    lc = 63 if fwd else 0
    lbcol, lbcolb = sb(nc, "lbcol", [128, 8], F32)
    omlcol, omlcolb = sb(nc, "omlcol", [128, 8], F32)
    spb = SP_LBF if fwd else SP_LBB
    P.op("dve", (lambda: nc.vector.tensor_tensor(out=lbcol, in0=sm[:, spb + 8:spb + 16], in1=sm[:, spb:spb + 8],
                                                 op=ALU.subtract)), [k.small_b], [lbcolb])
    P.op("act", (lambda: nc.scalar.activation(out=lbcol, in_=lbcol, func=AF.Sigmoid)), [lbcolb], [lbcolb])
    P.op("dve", (lambda: nc.vector.tensor_scalar(out=omlcol, in0=lbcol, scalar1=-1.0, scalar2=1.0,
                                                 op0=ALU.mult, op1=ALU.add)), [lbcolb], [omlcolb])
    lbrep, lbrepb = sb(nc, "lbrep", [128, 1024], F32)
    omlrep, omlrepb = sb(nc, "omlrep", [128, 1024], F32)
    r0 = 0 if fwd else 2
    P.dma("sp", (lambda: nc.sync.dma_start(out=lbrep, in_=k.lbrep_d[:, r0 + 1, :])), [], [lbrepb])
    P.dma("sp", (lambda: nc.sync.dma_start(out=omlrep, in_=k.lbrep_d[:, r0, :])), [], [omlrepb])
    P.op("dve", (lambda: nc.vector.tensor_tensor(out=lbrep, in0=lbrep, in1=omlrep, op=ALU.subtract)),
         [lbrepb, omlrepb], [lbrepb])
    P.op("act", (lambda: nc.scalar.activation(out=lbrep, in_=lbrep, func=AF.Sigmoid)), [lbrepb], [lbrepb])
    P.op("dve", (lambda: nc.vector.tensor_scalar(out=omlrep, in0=lbrep, scalar1=-1.0, scalar2=1.0,
                                                 op0=ALU.mult, op1=ALU.add)), [lbrepb], [omlrepb])
    mask4, mask4b = sb(nc, "mask4", [128, 4, 128], BF16)
    for g in range(4):
        P.op("dve", (lambda g=g: nc.vector.tensor_copy(out=mask4[:, g, :], in_=tri_inc)), [k.consts_b], [mask4b])
    S32, S32b = sb(nc, "hS32", [128, 8, 128], F32, nsub=8)
    S16, S16b = sb(nc, "hS16", [128, 8, 128], BF16, nsub=8)
    stg = Ring(nc, "h_stg", 2, [128, 512], F32)
    xbR = Ring(nc, "h_xb", 2, [128, NCH, 512], BF16, nsub=NCH)
    LFr = Ring(nc, "h_LF", 1, [128, 4, 512], F32, nsub=4)
    KDr = Ring(nc, "h_KD", 1, [128, 4, 512], BF16, nsub=4)
    Vr = Ring(nc, "h_V", 1, [128, 4, 512], BF16, nsub=4)
    tA = Ring(nc, "h_tA", 4, [128, 512], F32)
    tB = Ring(nc, "h_tB", 4, [128, 512], F32)
    tC = Ring(nc, "h_tC", 2, [128, 512], F32)
    qbR = Ring(nc, "h_qb", 1, [128, 4, 512], BF16, nsub=4)
    kbR = Ring(nc, "h_kb", 4, [128, 512], BF16)
    atR = Ring(nc, "h_at", 1, [128, 4, 512], BF16, nsub=4)
    ebl = Ring(nc, "h_ebl", 1, [128, 4, 8], F32, nsub=4)
    Or = Ring(nc, "h_O", 1, [128, 4, 512], F32, nsub=4)
    qrR = Ring(nc, "h_qr", 1, [128, 4, 512], BF16, nsub=4)
    KKr = Ring(nc, "h_KK", 1, [128, 4, 512], BF16, nsub=4)
    identb = k.cbf[:, C_IDENT:C_IDENT + 128]
    if not fwd:
        ofr = Ring(nc, "h_of", 1, [128, 4, 512], F32, nsub=4)
        ogr = Ring(nc, "h_og", 1, [128, 4, 512], BF16, nsub=4)

    tiles = []
    for si, (off, S) in enumerate(zip(k.offs, k.seqs)):
        tl = list(range(S // 512))
        if not fwd:
            tl = tl[::-1]
        for n_, j in enumerate(tl):
            tiles.append((off + j * 512, n_ == 0))
    xq = {}

    def load_x(ti):
        g0 = tiles[ti][0]
        xb, xbb = xbR.next()
        for c in range(NCH):
            st, stb = stg.next()
            P.dma("sp", (lambda: nc.sync.dma_start(out=st, in_=k.x2T[c, :, g0:g0 + 512])), [], [stb])
            P.op("pool", (lambda: nc.gpsimd.tensor_copy(out=xb[:, c, :], in_=st)), [stb], [xbb[c]])
        xq[ti] = (xb, xbb)

    load_x(0)
    for ti, (g0, first) in enumerate(tiles):
        if first:
            P.op("pool", (lambda: nc.gpsimd.memset(S32, 0.0)), [], [S32b])
            P.op("pool", (lambda: nc.gpsimd.memset(S16, 0.0)), [], [S16b])
        xb, xbb = xq.pop(ti)
        for hg in range(2):
            hs = slice(hg * 512, (hg + 1) * 512)
            LF, LFb = LFr.next()
            KD, KDb = KDr.next()
            V, Vb = Vr.next()
            KK, KKb = KKr.next()
            G4 = range(4)
            tsl = [slice(g * 128, (g + 1) * 128) for g in G4]
            zp = []
            for g in G4:
                zps, zpsb = k.ps.next()
                for c in range(NCH):
                    P.op("pe", (lambda: nc.tensor.matmul(zps, lhsT=xb[:, c, tsl[g]], rhs=Wz[:, c, hs],
                                                         start=(c == 0), stop=(c == NCH - 1))), [xbb[c], Wzb], [zpsb])
                zp.append((zps, zpsb))
            aa = []
            for g in G4:
                a_, ab_ = tA.next()
                zps, zpsb = zp[g]
                P.op("act", (lambda: nc.scalar.activation(out=a_, in_=zps, func=AF.Sigmoid)), [zpsb], [ab_])
                aa.append((a_, ab_))
            for g in G4:
                a_, ab_ = aa[g]
                P.op("dve", (lambda: nc.vector.tensor_tensor(out=a_, in0=a_, in1=omlrep[:, hs], op=ALU.mult)),
                     [ab_, omlrepb], [ab_])
                P.op("pool", (lambda: nc.gpsimd.tensor_tensor(out=a_, in0=a_, in1=lbrep[:, hs], op=ALU.add)),
                     [ab_, lbrepb], [ab_])
            for g in G4:
                a_, ab_ = aa[g]
                P.op("act", (lambda: nc.scalar.activation(out=LF[:, g, :], in_=a_, func=AF.Ln)), [ab_], [LFb[g]])
            sp_ = []
            for g in G4:
                a_, ab_ = aa[g]
                P.op("pool", (lambda: nc.gpsimd.tensor_scalar(out=KK[:, g, :], in0=a_, scalar1=-1.0, scalar2=1.0,
                                                              op0=ALU.mult, op1=ALU.add)), [ab_], [KKb[g]])
                sps, spsb = k.ps.next()
                P.op("pe", (lambda: nc.tensor.matmul(sps, lhsT=tri_suf, rhs=LF[:, g, :], start=True, stop=True)),
                     [LFb[g], k.consts_b], [spsb])
                sp_.append((sps, spsb))
            bb = []
            for g in G4:
                b_, bb_ = tB.next()
                sps, spsb = sp_[g]
                P.op("act", (lambda: nc.scalar.activation(out=b_, in_=sps, func=AF.Exp)), [spsb], [bb_])
                bb.append((b_, bb_))
            for g in G4:
                a_, ab_ = aa[g]
                b_, bb_ = bb[g]
                P.op("dve", (lambda: nc.vector.tensor_tensor(out=KD[:, g, :], in0=KK[:, g, :], in1=b_, op=ALU.mult)),
                     [KKb[g], bb_], [KDb[g]])
            qb, qbb = qbR.next()
            at, atb = atR.next()
            el, elb = ebl.next()
            H4 = range(4)
            fsl = [slice((hg * 4 + hl) * 128, (hg * 4 + hl + 1) * 128) for hl in H4]
            lsl = [slice(hl * 128, (hl + 1) * 128) for hl in H4]
            btp = []
            for hl in H4:
                bt, btb = k.ps.next()
                for g in G4:
                    P.op("pe", (lambda: nc.tensor.matmul(bt[:, g * 128:(g + 1) * 128], lhsT=LF[:, g, lsl[hl]], rhs=tri_inc,
                                                         start=True, stop=True)), [LFb[g], k.consts_b], [btb])
                btp.append((bt, btb))
            eb, enb = [], []
            for hl in H4:
                bt, btb = btp[hl]
                a_, ab_ = tA.next()
                P.op("act", (lambda: nc.scalar.activation(out=a_, in_=bt, func=AF.Exp)), [btb], [ab_])
                b_, bb_ = tB.next()
                P.op("act", (lambda: nc.scalar.activation(out=b_, in_=bt, func=AF.Exp, scale=-1.0)), [btb], [bb_])
                eb.append((a_, ab_))
                enb.append((b_, bb_))
            if fwd:
                ip = []
                for g in G4:
                    ips, ipsb = k.ps.next()
                    for c in range(NCH):
                        P.op("pe", (lambda: nc.tensor.matmul(ips, lhsT=xb[:, c, tsl[g]], rhs=Wi[:, c, hs],
                                                             start=(c == 0), stop=(c == NCH - 1))), [xbb[c], Wib], [ipsb])
                    ip.append((ips, ipsb))
                for g in G4:
                    ips, ipsb = ip[g]
                    P.op("act", (lambda: nc.scalar.activation(out=V[:, g, :], in_=ips, func=AF.Silu)), [ipsb], [Vb[g]])
                P.dma(STORE_Q, (lambda: k.stq.dma_start(
                    out=k.hv[g0:g0 + 512, hs].rearrange("(g p) f -> p g f", p=128), in_=V)), [Vb], [P.dbuf("hv", hg, g0)])
            else:
                P.dma("sp", (lambda: nc.sync.dma_start(
                    out=V, in_=k.hv[g0:g0 + 512, hs].rearrange("(g p) f -> p g f", p=128))), [], [Vb])
            for hl in H4:
                a_, ab_ = eb[hl]
                P.op("pool", (lambda: nc.gpsimd.tensor_copy(
                    out=el[:, hl, :], in_=a_.rearrange("p (c t) -> p c t", t=64)[:, :, lc])), [ab_], [elb[hl]])
            qr, qrb = qrR.next()
            if not fwd:
                P.dma("sp", (lambda: nc.sync.dma_start(
                    out=qr, in_=k.hq[hg * 4:(hg + 1) * 4, :, g0:g0 + 512].rearrange("c p t -> p c t"))), [], [qrb])
            for hl in H4:
                a_, ab_ = eb[hl]
                if fwd:
                    qps, qpsb = k.ps.next()
                    for c in range(NCH):
                        P.op("pe", (lambda: nc.tensor.matmul(qps, lhsT=Wq[:, c, fsl[hl]], rhs=xb[:, c, :],
                                                             start=(c == 0), stop=(c == NCH - 1))), [xbb[c], Wqb], [qpsb])
                    P.op("act", (lambda: nc.scalar.copy(out=qr[:, hl, :], in_=qps)), [qpsb], [qrb[hl]])
                P.op("dve", (lambda: nc.vector.tensor_tensor(out=qb[:, hl, :], in0=qr[:, hl, :], in1=a_, op=ALU.mult)),
                     [qrb[hl], ab_], [qbb[hl]])
            if fwd:
                P.dma(STORE_Q, (lambda: k.stq.dma_start(
                    out=k.hq[hg * 4:(hg + 1) * 4, :, g0:g0 + 512].rearrange("c p t -> p c t"), in_=qr)),
                    [qrb], [P.dbuf("hq", hg, g0)])
            ktp = []
            for hl in H4:
                kt_, ktb_ = k.ps.next()
                for g in G4:
                    P.op("pe", (lambda: nc.tensor.matmul(kt_[:, g * 128:(g + 1) * 128], lhsT=KK[:, g, lsl[hl]], rhs=identb,
                                                         start=True, stop=True)), [KKb[g], k.cbf_b], [ktb_])
                ktp.append((kt_, ktb_))
            kbs = []
            for hl in H4:
                kt_, ktb_ = ktp[hl]
                b_, bb_ = enb[hl]
                kb, kbb = kbR.next()
                P.op("dve", (lambda: nc.vector.tensor_tensor(out=kb, in0=kt_, in1=b_, op=ALU.mult)),
                     [ktb_, bb_], [kbb])
                kbs.append((kb, kbb))
            for hl in H4:
                kb, kbb = kbs[hl]
                aps, apsb = k.ps.next()
                for g in G4:
                    gs = tsl[g]
                    P.op("pe", (lambda: nc.tensor.matmul(aps[:, gs], lhsT=kb[:, gs], rhs=qb[:, hl, gs],
                                                         start=True, stop=True)), [kbb, qbb[hl]], [apsb])
                P.op("dve", (lambda: nc.vector.tensor_tensor(
                    out=at[:, hl, :], in0=aps, in1=mask4.rearrange("p g t -> p (g t)"), op=ALU.mult)),
                    [apsb, mask4b], [atb[hl]])
            if hg == 1 and ti + 1 < len(tiles):
                load_x(ti + 1)
            O, Ob = Or.next()
            order = list(range(8)) if fwd else list(range(7, -1, -1))
            for ci in order:
                g = ci // 2
                pb = 64 * (ci % 2)
                cs = slice(ci * 64, ci * 64 + 64)
                for hl in H4:
                    h = hg * 4 + hl
                    ls = lsl[hl]
                    ops_, opsb = k.ps.next()
                    P.op("pe", (lambda: nc.tensor.matmul(ops_[:, 0:64], lhsT=S16[:, h, :], rhs=qb[:, hl, cs],
                                                         start=True, stop=False)), [S16b[h], qbb[hl]], [opsb])
                    P.op("pe", (lambda: nc.tensor.matmul(ops_[:, 0:64], lhsT=V[:, g, ls], rhs=at[:, hl, cs],
                                                         start=False, stop=True)), [Vb[g], atb[hl]], [opsb])
                    P.op("act", (lambda: nc.scalar.copy(out=O[:, hl, cs], in_=ops_[:, 0:64])), [opsb], [Ob[hl]])
                    dps, dpsb = k.ps.next()
                    P.op("pe", (lambda: nc.tensor.matmul(dps[:, 0:128], lhsT=KD[pb:pb + 64, g, ls], rhs=V[pb:pb + 64, g, ls],
                                                         start=True, stop=True)), [KDb[g], Vb[g]], [dpsb])
                    P.op("dve", (lambda: nc.vector.scalar_tensor_tensor(
                        out=S32[:, h, :], in0=S32[:, h, :], scalar=el[:, hl, ci:ci + 1], in1=dps[:, 0:128],
                        op0=ALU.mult, op1=ALU.add)), [S32b[h], elb[hl], dpsb], [S32b[h]])
                    P.op("dve", (lambda: nc.vector.tensor_copy(out=S16[:, h, :], in_=S32[:, h, :])), [S32b[h]], [S16b[h]])
            if fwd:
                P.dma(STORE_Q, (lambda: k.stq.dma_start(
                    out=k.ofT[hg * 4:(hg + 1) * 4, :, g0:g0 + 512].rearrange("c p t -> p c t"), in_=O)),
                    [Ob], [P.dbuf("ofT", hg, g0)])
            else:
                of, ofb = ofr.next()
                P.dma("sp", (lambda: nc.sync.dma_start(
                    out=of, in_=k.ofT[hg * 4:(hg + 1) * 4, :, g0:g0 + 512].rearrange("c p t -> p c t"))), [], [ofb])
                og, ogb = ogr.next()
                sqs, gpl, rs_, sgl = [], [], [], []
                for hl in H4:
                    P.op("pool", (lambda: nc.gpsimd.tensor_tensor(out=of[:, hl, :], in0=of[:, hl, :], in1=O[:, hl, :],
                                                                  op=ALU.add)), [ofb[hl], Ob[hl]], [ofb[hl]])
                for hl in H4:
                    a_, ab_ = tA.next()
                    P.op("act", (lambda: nc.scalar.activation(out=a_, in_=of[:, hl, :], func=AF.Square)), [ofb[hl]], [ab_])
                    ms, msb = k.ps.next()
                    P.op("pe", (lambda: nc.tensor.matmul(ms, lhsT=onesv, rhs=a_, start=True, stop=True)),
                         [ab_, k.consts_b], [msb])
                    sqs.append((ms, msb))
                for hl in H4:
                    ms, msb = sqs[hl]
                    b_, bb_ = tB.next()
                    P.op("act", (lambda: nc.scalar.activation(out=b_, in_=ms, func=AF.Ln, bias=k.epsr[:, 0:1],
                                                              scale=1.0)), [msb, k.epsr_b], [bb_])
                    rs_.append((b_, bb_))
                for hl in H4:
                    b_, bb_ = rs_[hl]
                    P.op("act", (lambda: nc.scalar.activation(out=b_, in_=b_, func=AF.Exp, scale=-0.5)), [bb_], [bb_])
                for hl in H4:
                    gps, gpsb = k.ps.next()
                    for c in range(NCH):
                        P.op("pe", (lambda: nc.tensor.matmul(gps, lhsT=Wg[:, c, fsl[hl]], rhs=xb[:, c, :],
                                                             start=(c == 0), stop=(c == NCH - 1))), [xbb[c], Wgb], [gpsb])
                    gpl.append((gps, gpsb))
                for hl in H4:
                    gps, gpsb = gpl[hl]
                    c_, cb_ = tC.next()
                    P.op("act", (lambda: nc.scalar.activation(out=c_, in_=gps, func=AF.Silu)), [gpsb], [cb_])
                    h = hg * 4 + hl
                    b_, bb_ = rs_[hl]
                    P.op("dve", (lambda: nc.vector.scalar_tensor_tensor(
                        out=b_, in0=of[:, hl, :], scalar=sm[:, SP_GNC + h:SP_GNC + h + 1], in1=b_,
                        op0=ALU.mult, op1=ALU.mult)), [ofb[hl], bb_, k.small_b], [bb_])
                    P.op("dve", (lambda: nc.vector.tensor_tensor(out=og[:, hl, :], in0=b_, in1=c_, op=ALU.mult)),
                         [bb_, cb_], [ogb[hl]])
                P.dma(STORE_Q, (lambda: k.stq.dma_start(
                    out=k.ycat[hg * 4:(hg + 1) * 4, :, g0:g0 + 512].rearrange("c p t -> p c t"), in_=og)),
                    [ogb], [P.dbuf("og", hg, g0)])


def host_shared(inp, smax):
    d = {}
    f32 = lambda a: np.ascontiguousarray(np.asarray(a, np.float32))
    d["w_ab"] = f32(np.asarray(inp["w_in_ab"])[0][:, ab_columns()])
    d["w_oab"] = f32(np.asarray(inp["w_out_ab"])[0])
    d["w_c"] = f32(np.asarray(inp["w_in_c"])[0])
    d["w_oc"] = f32(np.asarray(inp["w_out_c"])[0])
    for l in range(2):
        d["w_up%d" % l] = f32(np.asarray(inp["ffn_w_up"])[l])
        d["w_dn%d" % l] = f32(np.asarray(inp["ffn_w_down"])[l])
    ca, sa, cb, sbb = rope_tables(smax)
    d["t_ca"], d["t_sa"], d["t_cb"], d["t_sb"] = ca, sa, cb, sbb
    d["masks"] = dil_masks()
    sp = np.zeros((128, NSMALL), np.float32)
    for l in range(2):
        sp[:, SP_LNMG0 + 32 * l:SP_LNMG0 + 32 * l + 8] = col128(np.asarray(inp["ln_mix_g"])[l], 8)
        sp[:, SP_LNMB0 + 32 * l:SP_LNMB0 + 32 * l + 8] = col128(np.asarray(inp["ln_mix_b"])[l], 8)
        sp[:, SP_LNFG0 + 32 * l:SP_LNFG0 + 32 * l + 8] = col128(np.asarray(inp["ln_ffn_g"])[l], 8)
        sp[:, SP_LNFB0 + 32 * l:SP_LNFB0 + 32 * l + 8] = col128(np.asarray(inp["ln_ffn_b"])[l], 8)
    pb = rope_perm_B()
    qn = np.asarray(inp["qn_ab"], np.float32)[0]
    kn = np.asarray(inp["kn_ab"], np.float32)[0]
    sp[:, SP_QN] = np.concatenate([qn, qn])
    sp[:, SP_QNP] = np.concatenate([qn[pb], qn[pb]])
    sp[:, SP_KN] = np.concatenate([kn, kn])
    sp[:, SP_KNP] = np.concatenate([kn[pb], kn[pb]])
    sp[:, SP_GNC:SP_GNC + 8] = col128(np.asarray(inp["gn_c"])[0], 8)
    for l, (cw, cbias) in enumerate(((SP_CW0, SP_CB0), (SP_CW1, SP_CB1))):
        w = np.asarray(inp["ffn_conv_w"], np.float32)[l]
        for j in range(3):
            sp[:, cw + j * NF:cw + (j + 1) * NF] = col128(w[j], NF)
        sp[:, cbias:cbias + NF] = col128(np.asarray(inp["ffn_conv_b"])[l], NF)
    lbf = np.asarray(inp["lb_fwd"], np.float32)
    lbb = np.asarray(inp["lb_bwd"], np.float32)
    for l in range(2):
        sp[:, SP_LBF + 8 * l:SP_LBF + 8 * l + 8] = col128(lbf[l], 8)
        sp[:, SP_LBB + 8 * l:SP_LBB + 8 * l + 8] = col128(lbb[l], 8)
    d["smallp"] = sp
    rep = np.zeros((128, 4, 1024), np.float32)
    rep[:, 0, :] = lbf[0][None, :]
    rep[:, 1, :] = lbf[1][None, :]
    rep[:, 2, :] = lbb[0][None, :]
    rep[:, 3, :] = lbb[1][None, :]
    d["lbrep"] = rep
    d["consts"] = host_consts()
    return d


ALL_PHASES = ("p1", "attA", "attB", "p3", "f0a", "f0b", "h1", "h2", "p6", "f1a", "f1b")


def kernel(**inputs):
    seqs = [2048, 8192]
    xp = np.asarray(inputs["x_prompt"], np.float32)
    xs = np.asarray(inputs["x_sample"], np.float32)
    shared = host_shared(inputs, max(seqs))
    nc = build2(seqs, ALL_PHASES)
    in_maps = []
    for c in range(8):
        m = dict(shared)
        m["x"] = np.ascontiguousarray(np.concatenate([xp[c], xs[c]], 0))
        in_maps.append(m)
    res = run_bass_kernel_spmd(nc, in_maps, core_ids=list(range(8)))
    yp = np.stack([res.results[c]["y"][0:2048] for c in range(8)], 0)
    ys = np.stack([res.results[c]["y"][2048:] for c in range(8)], 0)
    return (yp.astype(np.float32), ys.astype(np.float32))
```

```python
import math
import contextlib
import numpy as np
import ml_dtypes
import concourse.bass as bass
import concourse.mybir as mybir
from concourse.bass_utils import run_bass_kernel_spmd

F32 = mybir.dt.float32
BF16 = mybir.dt.bfloat16
AF = mybir.ActivationFunctionType
ALU = mybir.AluOpType

D = 1024
NCH = 8
DFF = 2816
NF = 22
ALPHA = 4.0 ** 0.25
LN_EPS = 1e-5
RMS_EPS = 1e-6
SAME_ENG_SYNC = True
NDMASEM = 24
STORE_Q = "pool"


class Buf:
    __slots__ = ("name", "w", "r")

    def __init__(self, name=""):
        self.name = name
        self.w = None
        self.r = []


def _flat(xs):
    out = []
    for x in xs:
        if isinstance(x, (list, tuple)):
            out.extend(_flat(x))
        else:
            out.append(x)
    return out


class Prog:
    def __init__(self, nc, marks=None):
        self.nc = nc
        self.eng = {"pe": nc.tensor, "act": nc.scalar, "dve": nc.vector, "pool": nc.gpsimd, "sp": nc.sync}
        self.marks = marks
        self.marked = []
        self.meta = []
        self.real = []
        self.ev = []
        self.dma_hist = {}
        self.dbufs = {}
        self.out_dmas = []
        if marks is not None:
            self.sems = {e: nc.alloc_semaphore("s_" + e) for e in self.eng}
            self.cnt = {e: 0 for e in self.eng}
            self.dsem = {}
            self.dcnt = {}
            self.waited = {e: {} for e in self.eng}

    def dbuf(self, *key):
        b = self.dbufs.get(key)
        if b is None:
            b = Buf(str(key))
            self.dbufs[key] = b
        return b

    def _add(self, eng, fn, reads, writes, is_dma, extra=()):
        reads = _flat(reads)
        writes = _flat(writes)
        idx = len(self.meta)
        deps = set(extra)
        for b in reads:
            if b.w is not None:
                deps.add(b.w)
        for b in writes:
            if b.w is not None:
                deps.add(b.w)
            deps.update(b.r)
        for b in writes:
            b.w = idx
            b.r = []
        for b in reads:
            if b.w != idx:
                b.r.append(idx)
        self.meta.append((eng, is_dma))
        self.real.append(fn is not None)
        fdeps = []
        for d in deps:
            de, ddma = self.meta[d]
            if (not ddma) and (not is_dma) and de == eng and (eng == "pe" or not SAME_ENG_SYNC):
                continue
            fdeps.append(d)
        if self.marks is None:
            self.marked.append(False)
            for d in fdeps:
                self.marked[d] = True
            return idx
        e = self.eng[eng]
        need = {}
        w = self.waited[eng]
        for d in fdeps:
            s, v = self.ev[d]
            key = id(s)
            if w.get(key, 0) >= v:
                continue
            if key not in need or need[key][1] < v:
                need[key] = (s, v)
        for key, (s, v) in need.items():
            e.wait_ge(s, v)
            w[key] = v
        if fn is None:
            self.ev.append(None)
            return idx
        inst = fn()
        if is_dma:
            if eng not in self.dsem:
                self.dsem[eng] = [self.nc.alloc_semaphore("d_%s_%d" % (eng, j)) for j in range(NDMASEM)]
                self.dcnt[eng] = 0
            j = self.dcnt[eng]
            self.dcnt[eng] += 1
            s = self.dsem[eng][j % NDMASEM]
            inst.then_inc(s, 16)
            self.ev.append((s, 16 * (j // NDMASEM + 1)))
        elif self.marks[idx]:
            self.cnt[eng] += 1
            inst.then_inc(self.sems[eng], 1)
            self.ev.append((self.sems[eng], self.cnt[eng]))
        else:
            self.ev.append(None)
        return idx

    def op(self, eng, fn, reads=(), writes=()):
        return self._add(eng, fn, reads, writes, False)

    def dma(self, q, fn, reads=(), writes=(), is_out=False):
        h = self.dma_hist.setdefault(q, [])
        extra = (h[-NDMASEM],) if len(h) >= NDMASEM else ()
        idx = self._add(q, fn, reads, writes, True, extra)
        h.append(idx)
        if is_out:
            self.out_dmas.append(idx)
        return idx

    def barrier(self):
        last = {}
        dmas = []
        for i, (e, isd) in enumerate(self.meta):
            if not self.real[i]:
                continue
            if isd:
                dmas.append(i)
            else:
                last[e] = i
        start = getattr(self, "_bar_from", 0)
        ex = tuple(last.values()) + tuple(d for d in dmas if d >= start)
        for e in ("pe", "act", "dve", "pool", "sp"):
            self._add(e, None, (), (), False, ex)
        self._bar_from = len(self.meta)

    def finish(self):
        self._add("sp", None, (), (), False, tuple(self.out_dmas))
        return self.marked


ALLOC = {"stack": None}


def _salloc(nc, name, shape, dtype):
    ALLOC["n"] = ALLOC.get("n", 0) + 1
    name = "%s_%d" % (name, ALLOC["n"])
    st = ALLOC["stack"]
    if st is None:
        return nc.alloc_sbuf_tensor(name, shape, dtype)
    return st.enter_context(nc.sbuf_tensor(name, shape, dtype))


class Ring:
    def __init__(self, nc, name, n, shape, dtype, psum=False, nsub=0):
        self.aps = []
        self.bufs = []
        for i in range(n):
            if psum:
                ALLOC["n"] = ALLOC.get("n", 0) + 1
                pname = "rp_%s%d_%d" % (name, i, ALLOC["n"])
                st = ALLOC["stack"]
                t = nc.alloc_psum_tensor(pname, shape, dtype) if st is None else st.enter_context(
                    nc.psum_tensor(pname, shape, dtype))
            else:
                t = _salloc(nc, "r_%s%d" % (name, i), shape, dtype)
            self.aps.append(t.ap())
            self.bufs.append([Buf("%s%d_%d" % (name, i, j)) for j in range(nsub)] if nsub else Buf("%s%d" % (name, i)))
        self.i = 0
        self.n = n

    def next(self):
        k = self.i % self.n
        self.i += 1
        return self.aps[k], self.bufs[k]


def sb(nc, name, shape, dtype, nsub=0):
    return _salloc(nc, "sb_" + name, shape, dtype).ap(), ([Buf(name + str(j)) for j in range(nsub)] if nsub else Buf(name))


def rope_perm_A():
    p = np.arange(64)
    p[0:8] = np.arange(8, 16)
    p[8:16] = np.arange(0, 8)
    return p


def rope_perm_B():
    p = np.arange(64)
    p[0:16] = np.arange(16, 32)
    p[16:32] = np.arange(0, 16)
    p[32:48] = np.arange(48, 64)
    p[48:64] = np.arange(32, 48)
    return p


def rope_tables(smax):
    t = np.arange(smax, dtype=np.float32)
    fa = (np.float32(500000.0) ** (-(np.arange(0, 16, 2, dtype=np.float32) / np.float32(16)))).astype(np.float32)
    ang = t[None, :] * fa[:, None]
    ca = np.ones((64, smax), np.float32)
    sa = np.zeros((64, smax), np.float32)
    ca[0:8] = np.cos(ang)
    ca[8:16] = np.cos(ang)
    sa[0:8] = -np.sin(ang)
    sa[8:16] = np.sin(ang)
    fb = (np.float32(10000.0) ** (-(np.arange(0, 32, 2, dtype=np.float32) / np.float32(32)))).astype(np.float32)
    row = np.floor(t / 64).astype(np.float32)
    col = (t - row * 64).astype(np.float32)
    ar = row[None, :] * fb[:, None]
    ac = col[None, :] * fb[:, None]
    cb = np.zeros((64, smax), np.float32)
    sbb = np.zeros((64, smax), np.float32)
    cb[0:16] = np.cos(ar)
    cb[16:32] = np.cos(ar)
    sbb[0:16] = -np.sin(ar)
    sbb[16:32] = np.sin(ar)
    cb[32:48] = np.cos(ac)
    cb[48:64] = np.cos(ac)
    sbb[32:48] = -np.sin(ac)
    sbb[48:64] = np.sin(ac)
    tile2 = lambda a: np.ascontiguousarray(np.concatenate([a, a], 0))
    return tile2(ca), tile2(sa), tile2(cb), tile2(sbb)


def dil_masks():
    m = np.zeros((20, 128, 512), np.float32)
    kk = np.arange(128)[:, None]
    qq = np.arange(512)[None, :]
    for i in range(20):
        dlt = (-1024 + 128 * i) + kk - qq
        a = np.abs(dlt)
        c = (a <= 64).astype(np.float32)
        c += ((dlt % 4 == 0) & (a <= 256)).astype(np.float32)
        c += ((dlt % 16 == 0) & (a <= 1024)).astype(np.float32)
        m[i] = c
    return np.ascontiguousarray(m.transpose(1, 0, 2)).astype(ml_dtypes.bfloat16)


def col128(v, nchunk):
    return np.ascontiguousarray(np.asarray(v, np.float32).reshape(nchunk, 128).T)


def ab_columns():
    pa = rope_perm_A()
    pb = rope_perm_B()
    cols = []
    A_W = 512
    qa = [h * 64 + np.arange(64) for h in range(8)]
    qap = [h * 64 + pa for h in range(8)]
    ka = [A_W + h * 64 + np.arange(64) for h in range(8)]
    kap = [A_W + h * 64 + pa for h in range(8)]
    o3 = 3 * A_W
    qb = [o3 + h * 64 + np.arange(64) for h in range(8)]
    qbp = [o3 + h * 64 + pb for h in range(8)]
    o4 = o3 + 512
    kb = [o4 + h * 64 + np.arange(64) for h in range(2)]
    kbp = [o4 + h * 64 + pb for h in range(2)]
    for c in range(4):
        cols += [qa[2 * c], qa[2 * c + 1]]
    for c in range(4):
        cols += [qap[2 * c], qap[2 * c + 1]]
    for c in range(4):
        cols += [ka[2 * c], ka[2 * c + 1]]
    for c in range(4):
        cols += [kap[2 * c], kap[2 * c + 1]]
    for c in range(4):
        cols += [qb[c], qb[4 + c]]
    for c in range(4):
        cols += [qbp[c], qbp[4 + c]]
    cols += [kb[0], kb[1]]
    cols += [kbp[0], kbp[1]]
    cols += [2 * A_W + np.arange(512)]
    o5 = o4 + 128
    cols += [o5 + np.arange(128)]
    return np.concatenate(cols)


NAB = 3968


class K:
    pass


def build(seqs, phases, debug=(), marks=None, feed=()):
    nc = bass.Bass("TRN2", target_bir_lowering=False)
    P = Prog(nc, marks)
    T = sum(seqs)
    SMAX = max(seqs)
    offs = [sum(seqs[:i]) for i in range(len(seqs))]
    k = K()
    k.nc, k.P, k.T, k.seqs, k.offs = nc, P, T, seqs, offs
    k.stq = P.eng[STORE_Q]

    def din(name, shape, dt=F32):
        return nc.dram_tensor(name, list(shape), dt, kind="ExternalInput").ap()

    def dscr(name, shape, dt):
        kind = "ExternalOutput" if name in debug else ("ExternalInput" if name in feed else "Internal")
        return nc.dram_tensor(name, list(shape), dt, kind=kind).ap()

    k.x = din("x", [T, D])
    k.w_ab = din("w_ab", [D, NAB])
    k.w_oab = din("w_oab", [D, D])
    k.w_c = din("w_c", [D, 5120])
    k.w_oc = din("w_oc", [D, D])
    k.w_up = [din("w_up%d" % l, [D, 2 * DFF]) for l in range(2)]
    k.w_dn = [din("w_dn%d" % l, [DFF, D]) for l in range(2)]
    k.t_ca = din("t_ca", [128, SMAX])
    k.t_sa = din("t_sa", [128, SMAX])
    k.t_cb = din("t_cb", [128, SMAX])
    k.t_sb = din("t_sb", [128, SMAX])
    k.masks_d = din("masks", [128, 20, 512], BF16)
    k.smallp = din("smallp", [128, NSMALL])
    k.lbrep_d = din("lbrep", [128, 4, 1024])
    k.consts_d = din("consts", [128, NCONST])
    k.y = nc.dram_tensor("y", [T, D], F32, kind="ExternalOutput").ap()
    k.xT = dscr("xT", [NCH, 128, T], F32)
    k.qaT = dscr("qaT", [4, 128, T], BF16)
    k.kaT = dscr("kaT", [4, 128, T], BF16)
    k.qbT = dscr("qbT", [4, 128, T], BF16)
    k.kbT = dscr("kbT", [1, 128, T], BF16)
    k.va = dscr("va", [T, 8 * 65], BF16)
    k.vb = dscr("vb", [T, 2 * 65], BF16)
    k.ycat = dscr("ycat", [NCH, 128, T], BF16)
    k.den = dscr("den", [16, T], F32)
    k.hq = dscr("hq", [NCH, 128, T], BF16)
    k.hv = dscr("hv", [T, 1024], BF16)
    k.x1T = dscr("x1T", [NCH, 128, T], F32)
    k.actT = dscr("actT", [NF, 128, T], BF16)
    k.x2T = dscr("x2T", [NCH, 128, T], F32)
    k.ofT = dscr("ofT", [NCH, 128, T], F32)
    k.x3T = dscr("x3T", [NCH, 128, T], F32)


    setup_consts(k)

    def run_phase(fn, *a, **kw):
        with contextlib.ExitStack() as st:
            ALLOC["stack"] = st
            if fn is not phase_att:
                k.ps = Ring(nc, "ps", 8, [128, 512], F32, psum=True)
            fn(*a, **kw)
            P.barrier()
        ALLOC["stack"] = None

    for name, fn, a, kw in phase_table(k):
        if name in phases:
            run_phase(fn, *a, **kw)
    m = P.finish()
    return nc, m


def phase_table(k):
    return [
        ("p1", phase_p1, (k,), {}),
        ("attA", phase_att, (k, "A"), {}),
        ("attB", phase_att, (k, "B"), {}),
        ("p3", phase_proj_ln, (k,), dict(src=k.ycat, resid=k.xT, dst=k.x1T, w_dram=k.w_oab, nk=8, lng=SP_LNMG0, lnb=SP_LNMB0, tag="p3", den=k.den)),
        ("f0a", phase_ffn_up, (k, 0, k.x1T), {}),
        ("f0b", phase_proj_ln, (k,), dict(src=k.actT, resid=k.x1T, dst=k.x2T, w_dram=k.w_dn[0], nk=NF, lng=SP_LNFG0, lnb=SP_LNFB0, tag="f0b")),
        ("h1", phase_hgrn, (k, True), {}),
        ("h2", phase_hgrn, (k, False), {}),
        ("p6", phase_proj_ln, (k,), dict(src=k.ycat, resid=k.x2T, dst=k.x3T, w_dram=k.w_oc, nk=8, lng=SP_LNMG1, lnb=SP_LNMB1, tag="p6")),
        ("f1a", phase_ffn_up, (k, 1, k.x3T), {}),
        ("f1b", phase_proj_ln, (k,), dict(src=k.actT, resid=k.x3T, dst=None, w_dram=k.w_dn[1], nk=NF, lng=SP_LNFG1, lnb=SP_LNFB1, tag="f1b")),
    ]


def build2(seqs, phases, debug=(), feed=()):
    _, marks = build(seqs, phases, debug, None, feed)
    nc, _ = build(seqs, phases, debug, marks, feed)
    return nc


SP_LNMG0, SP_LNMB0, SP_LNFG0, SP_LNFB0 = 0, 8, 16, 24
SP_LNMG1, SP_LNMB1, SP_LNFG1, SP_LNFB1 = 32, 40, 48, 56
SP_QN, SP_QNP, SP_KN, SP_KNP = 64, 65, 66, 67
SP_GNC = 68
SP_CW0 = 76
SP_CB0 = SP_CW0 + 66
SP_CW1 = SP_CB0 + 22
SP_CB1 = SP_CW1 + 66
SP_LBF = SP_CB1 + 22
SP_LBB = SP_LBF + 16
NSMALL = SP_LBB + 16

C_IDENT = 0
C_ONESBLK = 128
C_ONES = 256
C_TRIL = 384
C_TRIU_S = 512
C_TRIU = 640
C_TRIL_S = 768
C_ONESV = 896
NCONST = 1024


def host_consts():
    c = np.zeros((128, NCONST), np.float32)
    c[:, C_IDENT:C_IDENT + 128] = np.eye(128, dtype=np.float32)
    blk = np.zeros((128, 128), np.float32)
    blk[0:64, 0:64] = 1.0 / 64
    blk[64:128, 64:128] = 1.0 / 64
    c[:, C_ONESBLK:C_ONESBLK + 128] = blk
    c[:, C_ONES:C_ONES + 128] = 1.0
    c[:, C_ONESV:C_ONESV + 128] = 1.0 / 128
    s = np.arange(128)[:, None]
    t = np.arange(128)[None, :]
    same = (s // 64) == (t // 64)
    c[:, C_TRIL:C_TRIL + 128] = (same & (s <= t))
    c[:, C_TRIU_S:C_TRIU_S + 128] = (same & (s > t))
    c[:, C_TRIU:C_TRIU + 128] = (same & (s >= t))
    c[:, C_TRIL_S:C_TRIL_S + 128] = (same & (s < t))
    return c


def setup_consts(k):
    nc, P = k.nc, k.P
    k.consts, k.consts_b = sb(nc, "consts", [128, NCONST], F32)
    k.small, k.small_b = sb(nc, "small", [128, NSMALL], F32)
    P.dma("sp", lambda: nc.sync.dma_start(out=k.consts, in_=k.consts_d), [], [k.consts_b])
    P.dma("sp", lambda: nc.sync.dma_start(out=k.small, in_=k.smallp), [], [k.small_b])
    k.cbf, k.cbf_b = sb(nc, "cbf", [128, NCONST], BF16)
    P.op("dve", lambda: nc.vector.tensor_copy(out=k.cbf, in_=k.consts), [k.consts_b], [k.cbf_b])
    k.onesd, k.onesd_b = sb(nc, "onesd", [128, 128], BF16)
    P.op("dve", lambda: nc.vector.memset(k.onesd, 1.0 / 1024), [], [k.onesd_b])
    k.epsr, k.epsr_b = sb(nc, "epsr", [128, 1], F32)
    P.op("dve", lambda: nc.vector.memset(k.epsr, RMS_EPS), [], [k.epsr_b])
    k.epsl, k.epsl_b = sb(nc, "epsl", [128, 1], F32)
    P.op("dve", lambda: nc.vector.memset(k.epsl, LN_EPS), [], [k.epsl_b])


def load_weight_bf16(k, name, w_dram, nk, ncols, col0=0, sw=2048):
    nc, P = k.nc, k.P
    nchunk = nk * (-(-ncols // sw))
    wt, wb = sb(nc, name, [128, nk, ncols], BF16, nsub=nchunk)
    wstage = Ring(nc, name + "_stg", 2, [128, sw], F32)
    i = 0
    for kk in range(nk):
        for c0 in range(0, ncols, sw):
            cw = min(sw, ncols - c0)
            st, stb = wstage.next()
            P.dma("sp", (lambda st=st, kk=kk, c0=c0, cw=cw: nc.sync.dma_start(
                out=st[:, 0:cw], in_=w_dram[kk * 128:(kk + 1) * 128, col0 + c0:col0 + c0 + cw])), [], [stb])
            if i % 2 == 0:
                P.op("act", (lambda st=st, kk=kk, c0=c0, cw=cw: nc.scalar.copy(
                    out=wt[:, kk, c0:c0 + cw], in_=st[:, 0:cw])), [stb], [wb[i]])
            else:
                P.op("dve", (lambda st=st, kk=kk, c0=c0, cw=cw: nc.vector.tensor_copy(
                    out=wt[:, kk, c0:c0 + cw], in_=st[:, 0:cw])), [stb], [wb[i]])
            i += 1
    return wt, wb


def tiles512(k):
    for si, (off, S) in enumerate(zip(k.offs, k.seqs)):
        for j in range(S // 512):
            yield si, off, S, j * 512, off + j * 512


def phase_p1(k):
    nc, P = k.nc, k.P
    W, Wb = load_weight_bf16(k, "w_ab_sb", k.w_ab, 8, NAB)
    xtok = Ring(nc, "xtok", 2, [128, 4, D], F32)
    xT32 = Ring(nc, "xT32", 1, [128, NCH, 512], F32, nsub=NCH)
    xTb = Ring(nc, "xTb", 2, [128, NCH, 512], BF16, nsub=NCH)
    tab = Ring(nc, "ropetab", 1, [128, 4, 512], F32)
    t1r = Ring(nc, "p1t1", 2, [128, 512], F32)
    t2r = Ring(nc, "p1t2", 2, [128, 512], F32)
    sqr = Ring(nc, "p1sq", 2, [128, 512], F32)
    rsr = Ring(nc, "p1rs", 2, [128, 512], F32)
    qkout = Ring(nc, "p1qk", 1, [128, 13, 512], BF16, nsub=13)
    vaug = Ring(nc, "p1va", 2, [128, 4, 8 * 65], BF16)
    vbug = Ring(nc, "p1vb", 2, [128, 4, 2 * 65], BF16)
    for r in (vaug, vbug):
        for ap, b in zip(r.aps, r.bufs):
            P.op("pool", (lambda ap=ap: nc.gpsimd.memset(ap, 1.0)), [], [b])
    ident = k.consts[:, C_IDENT:C_IDENT + 128]
    onesblk = k.consts[:, C_ONESBLK:C_ONESBLK + 128]
    sm = k.small
    for si, off, S, p0, g0 in tiles512(k):
        xt, xtb = xtok.next()
        P.dma("sp", (lambda xt=xt, g0=g0: nc.sync.dma_start(
            out=xt, in_=k.x[g0:g0 + 512, :].rearrange("(g p) d -> p g d", p=128))), [], [xtb])
        tb, tbb = tab.next()
        for i, src in enumerate((k.t_ca, k.t_sa, k.t_cb, k.t_sb)):
            P.dma("sp", (lambda tb=tb, i=i, src=src, p0=p0: nc.sync.dma_start(
                out=tb[:, i, :], in_=src[:, p0:p0 + 512])), [], [tbb])
        x32, x32b = xT32.next()
        xb, xbb = xTb.next()
        for c in range(NCH):
            ps, psb = k.ps.next()
            for g in range(4):
                P.op("pe", (lambda ps=ps, xt=xt, g=g, c=c: nc.tensor.matmul(
                    ps[:, g * 128:(g + 1) * 128], lhsT=xt[:, g, c * 128:(c + 1) * 128], rhs=ident,
                    start=True, stop=True)), [xtb, k.consts_b], [psb])
            P.op("act", (lambda ps=ps, x32=x32, c=c: nc.scalar.copy(out=x32[:, c, :], in_=ps)), [psb], [x32b[c]])
            P.op("dve", (lambda x32=x32, xb=xb, c=c: nc.vector.tensor_copy(out=xb[:, c, :], in_=x32[:, c, :])), [x32b[c]], [xbb[c]])
        P.dma(STORE_Q, (lambda x32=x32, g0=g0: k.stq.dma_start(
            out=k.xT[:, :, g0:g0 + 512].rearrange("c p t -> p c t"), in_=x32)),
            [x32b], [P.dbuf("xT", g0)])

        def proj(chunk):
            ps, psb = k.ps.next()
            for c in range(NCH):
                P.op("pe", (lambda ps=ps, c=c, chunk=chunk: nc.tensor.matmul(
                    ps, lhsT=W[:, c, chunk * 128:(chunk + 1) * 128], rhs=xb[:, c, :],
                    start=(c == 0), stop=(c == NCH - 1))), [Wb, xbb[c]], [psb])
            return ps, psb

        qo, qob = qkout.next()
        for which, base in ((0, 0), (1, 8)):
            for c in range(4):
                ps, psb = proj(base + c)
                pp, ppb = proj(base + 4 + c)
                t1, t1b = t1r.next()
                t2, t2b = t2r.next()
                P.op("dve", (lambda t1=t1, ps=ps: nc.vector.tensor_tensor(
                    out=t1, in0=ps, in1=tb[:, 0, :], op=ALU.mult)), [psb, tbb], [t1b])
                P.op("dve", (lambda t2=t2, pp=pp: nc.vector.tensor_tensor(
                    out=t2, in0=pp, in1=tb[:, 1, :], op=ALU.mult)), [ppb, tbb], [t2b])
                slot = which * 4 + c
                P.op("pool", (lambda t1=t1, t2=t2, slot=slot: nc.gpsimd.tensor_tensor(
                    out=qo[:, slot, :], in0=t1, in1=t2, op=ALU.add)), [t1b, t2b], [qob[slot]])
        for which, base, n, gcol in ((0, 16, 4, SP_QN), (1, 24, 1, SP_KN)):
            for c in range(n):
                ps, psb = proj(base + c)
                pp, ppb = proj(base + n + c)
                sq, sqb = sqr.next()
                P.op("act", (lambda sq=sq, ps=ps: nc.scalar.activation(out=sq, in_=ps, func=AF.Square)), [psb], [sqb])
                ms, msb = k.ps.next()
                P.op("pe", (lambda ms=ms, sq=sq: nc.tensor.matmul(ms, lhsT=onesblk, rhs=sq, start=True, stop=True)),
                     [sqb, k.consts_b], [msb])
                rs, rsb = rsr.next()
                P.op("act", (lambda rs=rs, ms=ms: nc.scalar.activation(
                    out=rs, in_=ms, func=AF.Ln, bias=k.epsr[:, 0:1], scale=1.0)), [msb, k.epsr_b], [rsb])
                P.op("act", (lambda rs=rs: nc.scalar.activation(out=rs, in_=rs, func=AF.Exp, scale=-0.5)), [rsb], [rsb])
                t1, t1b = t1r.next()
                t2, t2b = t2r.next()
                P.op("dve", (lambda t1=t1, ps=ps, gcol=gcol: nc.vector.scalar_tensor_tensor(
                    out=t1, in0=ps, scalar=sm[:, gcol:gcol + 1], in1=tb[:, 2, :], op0=ALU.mult, op1=ALU.mult)),
                    [psb, tbb, k.small_b, sqb], [t1b])
                P.op("dve", (lambda t2=t2, pp=pp, gcol=gcol: nc.vector.scalar_tensor_tensor(
                    out=t2, in0=pp, scalar=sm[:, gcol + 1:gcol + 2], in1=tb[:, 3, :], op0=ALU.mult, op1=ALU.mult)),
                    [ppb, tbb, k.small_b], [t2b])
                P.op("pool", (lambda t1=t1, t2=t2: nc.gpsimd.tensor_tensor(
                    out=t1, in0=t1, in1=t2, op=ALU.add)), [t1b, t2b], [t1b])
                slot = 8 + c if which == 0 else 12
                P.op("dve", (lambda t1=t1, rs=rs, slot=slot: nc.vector.tensor_tensor(
                    out=qo[:, slot, :], in0=t1, in1=rs, op=ALU.mult)), [t1b, rsb], [qob[slot]])
        for dst, s0, n in ((k.qaT, 0, 4), (k.kaT, 4, 4), (k.qbT, 8, 4), (k.kbT, 12, 1)):
            P.dma(STORE_Q, (lambda dst=dst, s0=s0, n=n, g0=g0: k.stq.dma_start(
                out=dst[:, :, g0:g0 + 512].rearrange("c p t -> p c t"), in_=qo[:, s0:s0 + n, :])),
                [qob], [P.dbuf("qk", id(dst), g0)])
        va, vab = vaug.next()
        vb, vbb = vbug.next()
        for g in range(4):
            ps, psb = k.ps.next()
            for c in range(NCH):
                P.op("pe", (lambda ps=ps, c=c, g=g: nc.tensor.matmul(
                    ps, lhsT=xb[:, c, g * 128:(g + 1) * 128], rhs=W[:, c, 3328:3840],
                    start=(c == 0), stop=(c == NCH - 1))), [Wb, xbb[c]], [psb])
            P.op("act", (lambda ps=ps, va=va, g=g: nc.scalar.copy(
                out=va[:, g, :].rearrange("p (h e) -> p h e", e=65)[:, :, 0:64],
                in_=ps.rearrange("p (h e) -> p h e", e=64))), [psb], [vab])
            ps2, ps2b = k.ps.next()
            for c in range(NCH):
                P.op("pe", (lambda ps2=ps2, c=c, g=g: nc.tensor.matmul(
                    ps2[:, 0:128], lhsT=xb[:, c, g * 128:(g + 1) * 128], rhs=W[:, c, 3840:3968],
                    start=(c == 0), stop=(c == NCH - 1))), [Wb, xbb[c]], [ps2b])
            P.op("act", (lambda ps2=ps2, vb=vb, g=g: nc.scalar.copy(
                out=vb[:, g, :].rearrange("p (h e) -> p h e", e=65)[:, :, 0:64],
                in_=ps2[:, 0:128].rearrange("p (h e) -> p h e", e=64))), [ps2b], [vbb])
        P.dma(STORE_Q, (lambda va=va, g0=g0: k.stq.dma_start(
            out=k.va[g0:g0 + 512, :].rearrange("(g p) f -> p g f", p=128), in_=va)), [vab], [P.dbuf("va", g0)])
        P.dma(STORE_Q, (lambda vb=vb, g0=g0: k.stq.dma_start(
            out=k.vb[g0:g0 + 512, :].rearrange("(g p) f -> p g f", p=128), in_=vb)), [vbb], [P.dbuf("vb", g0)])


def phase_att(k, which):
    nc, P = k.nc, k.P
    isA = which == "A"
    SMAX = max(k.seqs)
    nktm = SMAX // 128
    sR = Ring(nc, "att_s", 2, [128, 1024], F32, psum=True)
    oR = Ring(nc, "att_O", 4, [128, 512], F32, psum=True)
    qT = Ring(nc, "att_q", 2, [128, SMAX], BF16)
    kT = Ring(nc, "att_k", 2 if isA else 1, [128, SMAX], BF16)
    vR = Ring(nc, "att_v", 2 if isA else 1, [128, nktm, 130], BF16)
    pr = Ring(nc, "att_p", 3, [128, 1024], BF16)
    pm = Ring(nc, "att_pm", 3, [128, 1024], BF16, nsub=2) if isA else None
    ysb = Ring(nc, "att_y", 4, [64, 512], BF16)
    dsb = Ring(nc, "att_d", 4, [65, 512], F32)
    if isA:
        masks, masks_b = sb(nc, "masks", [128, 20, 512], BF16)
        P.dma("sp", lambda: nc.sync.dma_start(out=masks, in_=k.masks_d), [], [masks_b])
    for si, (off, S) in enumerate(zip(k.offs, k.seqs)):
        nkt = S // 128
        nqb = S // 512
        if not isA:
            kt_, ktb = kT.next()
            P.dma("sp", (lambda: nc.sync.dma_start(out=kt_[:, 0:S], in_=k.kbT[0, :, off:off + S])), [], [ktb])
            v_, vb_ = vR.next()
            P.dma("sp", (lambda: nc.sync.dma_start(
                out=v_[:, 0:nkt, :], in_=k.vb[off:off + S, :].rearrange("(t p) f -> p t f", p=128))), [], [vb_])
        for c in range(4):
            q_, qb_ = qT.next()
            qsrc = k.qaT if isA else k.qbT
            P.dma("sp", (lambda: nc.sync.dma_start(out=q_[:, 0:S], in_=qsrc[c, :, off:off + S])), [], [qb_])
            if isA:
                kt_, ktb = kT.next()
                P.dma("sp", (lambda: nc.sync.dma_start(out=kt_[:, 0:S], in_=k.kaT[c, :, off:off + S])), [], [ktb])
                v_, vb_ = vR.next()
                P.dma("sp", (lambda: nc.sync.dma_start(
                    out=v_[:, 0:nkt, :],
                    in_=k.va[off:off + S, c * 130:(c + 1) * 130].rearrange("(t p) f -> p t f", p=128))), [], [vb_])
            items = []
            for qi in range(nqb):
                q0 = qi * 512
                if isA:
                    kts = list(range(max(0, (q0 - 1024) // 128), min(nkt - 1, (q0 + 1535) // 128) + 1))
                else:
                    kts = list(range(nkt))
                for ii, kt in enumerate(kts):
                    items.append((qi, ii, kt, len(kts)))
            sq = {}

            def issue_qk(j):
                qi, ii, kt, n = items[j]
                s_, sb_ = sR.next()
                for hh in range(2):
                    pb = 64 * hh
                    P.op("pe", (lambda: nc.tensor.matmul(
                        s_[:, hh * 512:(hh + 1) * 512], lhsT=kt_[pb:pb + 64, kt * 128:kt * 128 + 128],
                        rhs=q_[pb:pb + 64, qi * 512:qi * 512 + 512], start=True, stop=True)), [ktb, qb_], [sb_])
                sq[j] = (s_, sb_)

            issue_qk(0)
            O = None
            for j, (qi, ii, kt, n) in enumerate(items):
                q0 = qi * 512
                k0 = kt * 128
                if ii == 0:
                    O = [oR.next(), oR.next()]
                s_, sb_ = sq.pop(j)
                p_, pb_ = pr.next()
                P.op("act", (lambda: nc.scalar.activation(out=p_, in_=s_, func=AF.Exp, scale=0.125)), [sb_], [pb_])
                if isA:
                    mi = (k0 - q0 + 1024) // 128
                    p2, p2b = pm.next()
                    for hh in range(2):
                        eng_, e_ = (("dve", nc.vector), ("pool", nc.gpsimd))[hh if MASK_SPLIT else 0]
                        P.op(eng_, (lambda: e_.tensor_tensor(
                            out=p2[:, hh * 512:(hh + 1) * 512], in0=p_[:, hh * 512:(hh + 1) * 512],
                            in1=masks[:, mi, :], op=ALU.mult)), [pb_, masks_b], [p2b[hh]])
                    p_, pb_ = p2, p2b
                if j + 1 < len(items):
                    issue_qk(j + 1)
                for hh in range(2):
                    P.op("pe", (lambda: nc.tensor.matmul(
                        O[hh][0][0:65, :], lhsT=v_[:, kt, hh * 65:(hh + 1) * 65], rhs=p_[:, hh * 512:(hh + 1) * 512],
                        start=(ii == 0), stop=(ii == n - 1))), [vb_, (pb_[hh] if isinstance(pb_, list) else pb_)], [O[hh][1]])
                if ii == n - 1:
                    g0 = off + q0
                    for hh in range(2):
                        if isA:
                            head = 2 * c + hh
                            ochunk, obase, drow = c, 64 * hh, head
                        else:
                            head = c + 4 * hh
                            ochunk, obase, drow = 4 + head // 2, 64 * (head % 2), 8 + head
                        ops, opsb = O[hh]
                        y_, yb_ = ysb.next()
                        d_, db_ = dsb.next()
                        P.op("act", (lambda: nc.scalar.copy(out=y_, in_=ops[0:64, :])), [opsb], [yb_])
                        P.op("act", (lambda: nc.scalar.activation(out=d_[64:65, :], in_=ops[64:65, :], func=AF.Ln)),
                             [opsb], [db_])
                        P.op("act", (lambda: nc.scalar.activation(out=d_[64:65, :], in_=d_[64:65, :], func=AF.Exp,
                                                                  scale=-1.0)), [db_], [db_])
                        P.dma(STORE_Q, (lambda: k.stq.dma_start(
                            out=k.ycat[ochunk, obase:obase + 64, g0:g0 + 512], in_=y_)), [yb_], [P.dbuf("ycat", head, g0)])
                        P.dma(STORE_Q, (lambda: k.stq.dma_start(
                            out=k.den[drow:drow + 1, g0:g0 + 512], in_=d_[64:65, :])), [db_], [P.dbuf("den", drow, g0)])


def layer_norm_tile(k, r, rb_, lng, lnb, rings, out, outb):
    nc, P = k.nc, k.P
    sm = k.small
    rbf, r2f, stat = rings
    rb16, rb16b = rbf.next()
    r2, r2b = r2f.next()
    for n in range(NCH):
        P.op("act", (lambda n=n: nc.scalar.copy(out=rb16[:, n, :], in_=r[:, n, :])), [rb_[n]], [rb16b[n]])
        P.op("act", (lambda n=n: nc.scalar.activation(out=r2[:, n, :], in_=r[:, n, :], func=AF.Square)), [rb_[n]], [r2b[n]])
    mps, mpsb = k.ps.next()
    eps_, epsb = k.ps.next()
    for n in range(NCH):
        P.op("pe", (lambda n=n: nc.tensor.matmul(mps, lhsT=k.onesd, rhs=rb16[:, n, :], start=(n == 0), stop=(n == NCH - 1))),
             [k.onesd_b, rb16b[n]], [mpsb])
    for n in range(NCH):
        P.op("pe", (lambda n=n: nc.tensor.matmul(eps_, lhsT=k.onesd, rhs=r2[:, n, :], start=(n == 0), stop=(n == NCH - 1))),
             [k.onesd_b, r2b[n]], [epsb])
    m2, m2b = stat.next()
    P.op("act", (lambda: nc.scalar.activation(out=m2, in_=mps, func=AF.Square)), [mpsb], [m2b])
    P.op("dve", (lambda: nc.vector.tensor_tensor(out=m2, in0=eps_, in1=m2, op=ALU.subtract)), [epsb, m2b], [m2b])
    P.op("act", (lambda: nc.scalar.activation(out=m2, in_=m2, func=AF.Ln, bias=k.epsl[:, 0:1], scale=1.0)),
         [m2b, k.epsl_b], [m2b])
    P.op("act", (lambda: nc.scalar.activation(out=m2, in_=m2, func=AF.Exp, scale=-0.5)), [m2b], [m2b])
    for n in range(NCH):
        P.op("dve", (lambda n=n: nc.vector.tensor_tensor(out=r[:, n, :], in0=r[:, n, :], in1=mps, op=ALU.subtract)),
             [rb_[n], mpsb], [rb_[n]])
        P.op("dve", (lambda n=n: nc.vector.scalar_tensor_tensor(
            out=r[:, n, :], in0=r[:, n, :], scalar=sm[:, lng + n:lng + n + 1], in1=m2, op0=ALU.mult, op1=ALU.mult)),
            [rb_[n], m2b, k.small_b], [rb_[n]])
        P.op("act", (lambda n=n: nc.scalar.activation(
            out=out[:, n, :], in_=r[:, n, :], func=AF.Identity, bias=sm[:, lnb + n:lnb + n + 1], scale=1.0)),
            [rb_[n], k.small_b], [outb[n]])


def ln_rings(k, tag):
    nc = k.nc
    return (Ring(nc, tag + "_rb16", 1, [128, NCH, 512], BF16, nsub=NCH), Ring(nc, tag + "_r2", 1, [128, NCH, 512], BF16, nsub=NCH),
            Ring(nc, tag + "_stat", 2, [128, 512], F32))


def phase_proj_ln(k, src, resid, dst, w_dram, nk, lng, lnb, tag, den=None):
    nc, P = k.nc, k.P
    W, Wb = load_weight_bf16(k, "w_" + tag, w_dram, nk, D)
    srcR = Ring(nc, tag + "_src", 2, [128, nk, 512], BF16)
    resR = Ring(nc, tag + "_res", 1 if nk > 8 else 2, [128, NCH, 512], F32)
    rR = Ring(nc, tag + "_r", 2, [128, NCH, 512], F32, nsub=NCH)
    outR = Ring(nc, tag + "_out", 1 if nk > 8 else 2, [128, NCH, 512], F32, nsub=NCH)
    lnr = ln_rings(k, tag)
    if den is not None:
        dnR = Ring(nc, tag + "_dn", 1, [128, nk, 512], F32)
    if dst is None:
        ytok = Ring(nc, tag + "_ytok", 2, [128, D], F32)
        ident = k.consts[:, C_IDENT:C_IDENT + 128]
    for si, off, S, p0, g0 in tiles512(k):
        s_, sb_ = srcR.next()
        P.dma("sp", (lambda: nc.sync.dma_start(out=s_, in_=src[:, :, g0:g0 + 512].rearrange("c p t -> p c t"))), [], [sb_])
        if den is not None:
            dn, dnb = dnR.next()
            for half in range(2):
                P.dma("sp", (lambda: nc.sync.dma_start(
                    out=dn[64 * half:64 * half + 64, :, :],
                    in_=den[:, g0:g0 + 512].rearrange("(n h) t -> h n t", h=2)[half:half + 1].broadcast_to([64, nk, 512]))),
                    [], [dnb])
            P.op("dve", (lambda: nc.vector.tensor_tensor(
                out=s_.rearrange("p c t -> p (c t)"), in0=s_.rearrange("p c t -> p (c t)"),
                in1=dn.rearrange("p c t -> p (c t)"), op=ALU.mult)), [sb_, dnb], [sb_])
        x_, xb_ = resR.next()
        P.dma("sp", (lambda: nc.sync.dma_start(out=x_, in_=resid[:, :, g0:g0 + 512].rearrange("c p t -> p c t"))), [], [xb_])
        r, rb_ = rR.next()
        for n in range(NCH):
            ps, psb = k.ps.next()
            for kk in range(nk):
                P.op("pe", (lambda: nc.tensor.matmul(ps, lhsT=W[:, kk, n * 128:(n + 1) * 128], rhs=s_[:, kk, :],
                                                     start=(kk == 0), stop=(kk == nk - 1))), [Wb, sb_], [psb])
            P.op("dve", (lambda: nc.vector.scalar_tensor_tensor(
                out=r[:, n, :], in0=x_[:, n, :], scalar=ALPHA, in1=ps, op0=ALU.mult, op1=ALU.add)), [xb_, psb], [rb_[n]])
        o_, ob_ = outR.next()
        layer_norm_tile(k, r, rb_, lng, lnb, lnr, o_, ob_)
        if dst is not None:
            P.dma(STORE_Q, (lambda: k.stq.dma_start(out=dst[:, :, g0:g0 + 512].rearrange("c p t -> p c t"), in_=o_)),
                  [ob_], [P.dbuf(tag, g0)])
        else:
            for g in range(4):
                yt, ytb = ytok.next()
                for half in range(2):
                    ps, psb = k.ps.next()
                    for j in range(4):
                        n = half * 4 + j
                        P.op("pe", (lambda: nc.tensor.matmul(
                            ps[:, j * 128:(j + 1) * 128], lhsT=o_[:, n, g * 128:(g + 1) * 128], rhs=ident,
                            start=True, stop=True)), [ob_[n], k.consts_b], [psb])
                    if half == 0:
                        P.op("act", (lambda: nc.scalar.copy(out=yt[:, 0:512], in_=ps)), [psb], [ytb])
                    else:
                        P.op("dve", (lambda: nc.vector.tensor_copy(out=yt[:, 512:1024], in_=ps)), [psb], [ytb])
                r0 = g0 + g * 128
                P.dma(STORE_Q, (lambda: k.stq.dma_start(out=k.y[r0:r0 + 128, :], in_=yt)), [ytb],
                      [P.dbuf("y", r0)], is_out=True)


GELU_NATIVE = True
MASK_SPLIT = False


def phase_ffn_up(k, l, xsrc):
    nc, P = k.nc, k.P
    W, Wb = load_weight_bf16(k, "w_up%d" % l, k.w_up[l], 8, 2 * DFF)
    cw = SP_CW0 if l == 0 else SP_CW1
    cbc = SP_CB0 if l == 0 else SP_CB1
    sm = k.small
    stg = Ring(nc, "fu_stg", 3, [128, 512], F32)
    xbR = Ring(nc, "fu_xb", 2, [128, NCH, 512], BF16, nsub=NCH)
    tR = Ring(nc, "fu_t", 3, [128, 512], F32)
    gR = Ring(nc, "fu_g", 2, [128, 512], F32)
    actR = Ring(nc, "fu_act", 2, [128, NF, 512], BF16, nsub=NF)
    tl = []
    for si, (off, S) in enumerate(zip(k.offs, k.seqs)):
        nt = -(-S // 510)
        wbase = -(-S // nt)
        wbase += wbase % 2
        a = 0
        while a < S:
            w = min(wbase, S - a)
            tl.append((off, S, a, w))
            a += w
    xq = {}

    def load_x(ti):
        off, S, a, w = tl[ti]
        lo, hi = max(a - 1, 0), min(a + w + 1, S)
        dcol = lo - (a - 1)
        xb, xbb = xbR.next()
        if a == 0:
            P.op("pool", (lambda: nc.gpsimd.memset(xb[:, :, 0:1], 0.0)), [], [xbb])
        if a + w == S:
            P.op("pool", (lambda: nc.gpsimd.memset(xb[:, :, w + 1:w + 2], 0.0)), [], [xbb])
        for c in range(NCH):
            st, stb = stg.next()
            P.dma("sp", (lambda: nc.sync.dma_start(out=st[:, 0:hi - lo], in_=xsrc[c, :, off + lo:off + hi])), [], [stb])
            P.op("pool", (lambda: nc.gpsimd.tensor_copy(out=xb[:, c, dcol:dcol + hi - lo], in_=st[:, 0:hi - lo])),
                 [stb], [xbb[c]])
        xq[ti] = (xb, xbb)

    load_x(0)
    for ti, (off, S, a, w) in enumerate(tl):
        if True:
            xb, xbb = xq.pop(ti)
            if ti + 1 < len(tl):
                load_x(ti + 1)
            act, actb = actR.next()
            for f in range(NF):
                gps, gpsb = k.ps.next()
                for c in range(NCH):
                    P.op("pe", (lambda: nc.tensor.matmul(
                        gps[:, 0:w + 2], lhsT=W[:, c, DFF + f * 128:DFF + (f + 1) * 128], rhs=xb[:, c, 0:w + 2],
                        start=(c == 0), stop=(c == NCH - 1))), [Wb, xbb[c]], [gpsb])
                ups, upsb = k.ps.next()
                for c in range(NCH):
                    P.op("pe", (lambda: nc.tensor.matmul(
                        ups[:, 0:w], lhsT=W[:, c, f * 128:(f + 1) * 128], rhs=xb[:, c, 1:w + 1],
                        start=(c == 0), stop=(c == NCH - 1))), [Wb, xbb[c]], [upsb])
                t, tb_ = tR.next()
                P.op("act", (lambda: nc.scalar.activation(
                    out=t[:, 0:w], in_=gps[:, 1:w + 1], func=AF.Identity,
                    scale=sm[:, cw + NF + f:cw + NF + f + 1], bias=sm[:, cbc + f:cbc + f + 1])), [gpsb, k.small_b], [tb_])
                P.op("dve", (lambda: nc.vector.scalar_tensor_tensor(
                    out=t[:, 0:w], in0=gps[:, 0:w], scalar=sm[:, cw + f:cw + f + 1], in1=t[:, 0:w],
                    op0=ALU.mult, op1=ALU.add)), [gpsb, tb_, k.small_b], [tb_])
                P.op("dve", (lambda: nc.vector.scalar_tensor_tensor(
                    out=t[:, 0:w], in0=gps[:, 2:w + 2], scalar=sm[:, cw + 2 * NF + f:cw + 2 * NF + f + 1], in1=t[:, 0:w],
                    op0=ALU.mult, op1=ALU.add)), [gpsb, tb_, k.small_b], [tb_])
                ge, geb = gR.next()
                if GELU_NATIVE:
                    P.op("act", (lambda: nc.scalar.activation(out=ge[:, 0:w], in_=t[:, 0:w], func=AF.Gelu_apprx_tanh)),
                         [tb_], [geb])
                else:
                    P.op("act", (lambda: nc.scalar.activation(out=ge[:, 0:w], in_=t[:, 0:w], func=AF.Square)), [tb_], [geb])
                    P.op("pool", (lambda: nc.gpsimd.tensor_scalar(
                        out=ge[:, 0:w], in0=ge[:, 0:w], scalar1=0.044715, scalar2=1.0, op0=ALU.mult, op1=ALU.add)),
                        [geb], [geb])
                    P.op("pool", (lambda: nc.gpsimd.tensor_tensor(out=ge[:, 0:w], in0=ge[:, 0:w], in1=t[:, 0:w], op=ALU.mult)),
                         [geb, tb_], [geb])
                    P.op("act", (lambda: nc.scalar.activation(out=ge[:, 0:w], in_=ge[:, 0:w], func=AF.Sigmoid,
                                                              scale=1.5957691216057308)), [geb], [geb])
                    P.op("pool", (lambda: nc.gpsimd.tensor_tensor(out=ge[:, 0:w], in0=ge[:, 0:w], in1=t[:, 0:w], op=ALU.mult)),
                         [geb, tb_], [geb])
                P.op("dve", (lambda: nc.vector.tensor_tensor(out=act[:, f, 0:w], in0=ge[:, 0:w], in1=ups[:, 0:w], op=ALU.mult)),
                     [geb, upsb], [actb[f]])
            g0 = off + a
            P.dma(STORE_Q, (lambda: k.stq.dma_start(
                out=k.actT[:, :, g0:g0 + w].rearrange("f p t -> p f t"), in_=act[:, :, 0:w])), [actb], [P.dbuf("actT", g0)])


def phase_hgrn(k, fwd):
    nc, P = k.nc, k.P
    sm = k.small
    Wz, Wzb = load_weight_bf16(k, "hw_z", k.w_c, 8, 1024, col0=(1024 if fwd else 2048), sw=1024)
    if fwd:
        Wq, Wqb = load_weight_bf16(k, "hw_q", k.w_c, 8, 1024, col0=0, sw=1024)
        Wi, Wib = load_weight_bf16(k, "hw_i", k.w_c, 8, 1024, col0=3072, sw=1024)
    if not fwd:
        Wg, Wgb = load_weight_bf16(k, "hw_g", k.w_c, 8, 1024, col0=4096, sw=1024)
    C = k.consts
    tri_inc = C[:, C_TRIL:C_TRIL + 128] if fwd else C[:, C_TRIU:C_TRIU + 128]
    tri_suf = C[:, C_TRIU_S:C_TRIU_S + 128] if fwd else C[:, C_TRIL_S:C_TRIL_S + 128]
    onesv = C[:, C_ONESV:C_ONESV + 128]
    lc = 63 if fwd else 0
    lbcol, lbcolb = sb(nc, "lbcol", [128, 8], F32)
    omlcol, omlcolb = sb(nc, "omlcol", [128, 8], F32)
    spb = SP_LBF if fwd else SP_LBB
    P.op("dve", (lambda: nc.vector.tensor_tensor(out=lbcol, in0=sm[:, spb + 8:spb + 16], in1=sm[:, spb:spb + 8],
                                                 op=ALU.subtract)), [k.small_b], [lbcolb])
    P.op("act", (lambda: nc.scalar.activation(out=lbcol, in_=lbcol, func=AF.Sigmoid)), [lbcolb], [lbcolb])
    P.op("dve", (lambda: nc.vector.tensor_scalar(out=omlcol, in0=lbcol, scalar1=-1.0, scalar2=1.0,
                                                 op0=ALU.mult, op1=ALU.add)), [lbcolb], [omlcolb])
    lbrep, lbrepb = sb(nc, "lbrep", [128, 1024], F32)
    omlrep, omlrepb = sb(nc, "omlrep", [128, 1024], F32)
    r0 = 0 if fwd else 2
    P.dma("sp", (lambda: nc.sync.dma_start(out=lbrep, in_=k.lbrep_d[:, r0 + 1, :])), [], [lbrepb])
    P.dma("sp", (lambda: nc.sync.dma_start(out=omlrep, in_=k.lbrep_d[:, r0, :])), [], [omlrepb])
    P.op("dve", (lambda: nc.vector.tensor_tensor(out=lbrep, in0=lbrep, in1=omlrep, op=ALU.subtract)),
         [lbrepb, omlrepb], [lbrepb])
    P.op("act", (lambda: nc.scalar.activation(out=lbrep, in_=lbrep, func=AF.Sigmoid)), [lbrepb], [lbrepb])
    P.op("dve", (lambda: nc.vector.tensor_scalar(out=omlrep, in0=lbrep, scalar1=-1.0, scalar2=1.0,
                                                 op0=ALU.mult, op1=ALU.add)), [lbrepb], [omlrepb])
    mask4, mask4b = sb(nc, "mask4", [128, 4, 128], BF16)
    for g in range(4):
        P.op("dve", (lambda g=g: nc.vector.tensor_copy(out=mask4[:, g, :], in_=tri_inc)), [k.consts_b], [mask4b])
    S32, S32b = sb(nc, "hS32", [128, 8, 128], F32, nsub=8)
    S16, S16b = sb(nc, "hS16", [128, 8, 128], BF16, nsub=8)
    stg = Ring(nc, "h_stg", 2, [128, 512], F32)
    xbR = Ring(nc, "h_xb", 2, [128, NCH, 512], BF16, nsub=NCH)
    LFr = Ring(nc, "h_LF", 1, [128, 4, 512], F32, nsub=4)
    KDr = Ring(nc, "h_KD", 1, [128, 4, 512], BF16, nsub=4)
    Vr = Ring(nc, "h_V", 1, [128, 4, 512], BF16, nsub=4)
    tA = Ring(nc, "h_tA", 4, [128, 512], F32)
    tB = Ring(nc, "h_tB", 4, [128, 512], F32)
    tC = Ring(nc, "h_tC", 2, [128, 512], F32)
    qbR = Ring(nc, "h_qb", 1, [128, 4, 512], BF16, nsub=4)
    kbR = Ring(nc, "h_kb", 4, [128, 512], BF16)
    atR = Ring(nc, "h_at", 1, [128, 4, 512], BF16, nsub=4)
    ebl = Ring(nc, "h_ebl", 1, [128, 4, 8], F32, nsub=4)
    Or = Ring(nc, "h_O", 1, [128, 4, 512], F32, nsub=4)
    qrR = Ring(nc, "h_qr", 1, [128, 4, 512], BF16, nsub=4)
    KKr = Ring(nc, "h_KK", 1, [128, 4, 512], BF16, nsub=4)
    identb = k.cbf[:, C_IDENT:C_IDENT + 128]
    if not fwd:
        ofr = Ring(nc, "h_of", 1, [128, 4, 512], F32, nsub=4)
        ogr = Ring(nc, "h_og", 1, [128, 4, 512], BF16, nsub=4)

    tiles = []
    for si, (off, S) in enumerate(zip(k.offs, k.seqs)):
        tl = list(range(S // 512))
        if not fwd:
            tl = tl[::-1]
        for n_, j in enumerate(tl):
            tiles.append((off + j * 512, n_ == 0))
    xq = {}

    def load_x(ti):
        g0 = tiles[ti][0]
        xb, xbb = xbR.next()
        for c in range(NCH):
            st, stb = stg.next()
            P.dma("sp", (lambda: nc.sync.dma_start(out=st, in_=k.x2T[c, :, g0:g0 + 512])), [], [stb])
            P.op("pool", (lambda: nc.gpsimd.tensor_copy(out=xb[:, c, :], in_=st)), [stb], [xbb[c]])
        xq[ti] = (xb, xbb)

    load_x(0)
    for ti, (g0, first) in enumerate(tiles):
        if first:
            P.op("pool", (lambda: nc.gpsimd.memset(S32, 0.0)), [], [S32b])
            P.op("pool", (lambda: nc.gpsimd.memset(S16, 0.0)), [], [S16b])
        xb, xbb = xq.pop(ti)
        for hg in range(2):
            hs = slice(hg * 512, (hg + 1) * 512)
            LF, LFb = LFr.next()
            KD, KDb = KDr.next()
            V, Vb = Vr.next()
            KK, KKb = KKr.next()
            G4 = range(4)
            tsl = [slice(g * 128, (g + 1) * 128) for g in G4]
            zp = []
            for g in G4:
                zps, zpsb = k.ps.next()
                for c in range(NCH):
                    P.op("pe", (lambda: nc.tensor.matmul(zps, lhsT=xb[:, c, tsl[g]], rhs=Wz[:, c, hs],
                                                         start=(c == 0), stop=(c == NCH - 1))), [xbb[c], Wzb], [zpsb])
                zp.append((zps, zpsb))
            aa = []
            for g in G4:
                a_, ab_ = tA.next()
                zps, zpsb = zp[g]
                P.op("act", (lambda: nc.scalar.activation(out=a_, in_=zps, func=AF.Sigmoid)), [zpsb], [ab_])
                aa.append((a_, ab_))
            for g in G4:
                a_, ab_ = aa[g]
                P.op("dve", (lambda: nc.vector.tensor_tensor(out=a_, in0=a_, in1=omlrep[:, hs], op=ALU.mult)),
                     [ab_, omlrepb], [ab_])
                P.op("pool", (lambda: nc.gpsimd.tensor_tensor(out=a_, in0=a_, in1=lbrep[:, hs], op=ALU.add)),
                     [ab_, lbrepb], [ab_])
            for g in G4:
                a_, ab_ = aa[g]
                P.op("act", (lambda: nc.scalar.activation(out=LF[:, g, :], in_=a_, func=AF.Ln)), [ab_], [LFb[g]])
            sp_ = []
            for g in G4:
                a_, ab_ = aa[g]
                P.op("pool", (lambda: nc.gpsimd.tensor_scalar(out=KK[:, g, :], in0=a_, scalar1=-1.0, scalar2=1.0,
                                                              op0=ALU.mult, op1=ALU.add)), [ab_], [KKb[g]])
                sps, spsb = k.ps.next()
                P.op("pe", (lambda: nc.tensor.matmul(sps, lhsT=tri_suf, rhs=LF[:, g, :], start=True, stop=True)),
                     [LFb[g], k.consts_b], [spsb])
                sp_.append((sps, spsb))
            bb = []
            for g in G4:
                b_, bb_ = tB.next()
                sps, spsb = sp_[g]
                P.op("act", (lambda: nc.scalar.activation(out=b_, in_=sps, func=AF.Exp)), [spsb], [bb_])
                bb.append((b_, bb_))
            for g in G4:
                a_, ab_ = aa[g]
                b_, bb_ = bb[g]
                P.op("dve", (lambda: nc.vector.tensor_tensor(out=KD[:, g, :], in0=KK[:, g, :], in1=b_, op=ALU.mult)),
                     [KKb[g], bb_], [KDb[g]])
            qb, qbb = qbR.next()
            at, atb = atR.next()
            el, elb = ebl.next()
            H4 = range(4)
            fsl = [slice((hg * 4 + hl) * 128, (hg * 4 + hl + 1) * 128) for hl in H4]
            lsl = [slice(hl * 128, (hl + 1) * 128) for hl in H4]
            btp = []
            for hl in H4:
                bt, btb = k.ps.next()
                for g in G4:
                    P.op("pe", (lambda: nc.tensor.matmul(bt[:, g * 128:(g + 1) * 128], lhsT=LF[:, g, lsl[hl]], rhs=tri_inc,
                                                         start=True, stop=True)), [LFb[g], k.consts_b], [btb])
                btp.append((bt, btb))
            eb, enb = [], []
            for hl in H4:
                bt, btb = btp[hl]
                a_, ab_ = tA.next()
                P.op("act", (lambda: nc.scalar.activation(out=a_, in_=bt, func=AF.Exp)), [btb], [ab_])
                b_, bb_ = tB.next()
                P.op("act", (lambda: nc.scalar.activation(out=b_, in_=bt, func=AF.Exp, scale=-1.0)), [btb], [bb_])
                eb.append((a_, ab_))
                enb.append((b_, bb_))
            if fwd:
                ip = []
                for g in G4:
                    ips, ipsb = k.ps.next()
                    for c in range(NCH):
                        P.op("pe", (lambda: nc.tensor.matmul(ips, lhsT=xb[:, c, tsl[g]], rhs=Wi[:, c, hs],
                                                             start=(c == 0), stop=(c == NCH - 1))), [xbb[c], Wib], [ipsb])
                    ip.append((ips, ipsb))
                for g in G4:
                    ips, ipsb = ip[g]
                    P.op("act", (lambda: nc.scalar.activation(out=V[:, g, :], in_=ips, func=AF.Silu)), [ipsb], [Vb[g]])
                P.dma(STORE_Q, (lambda: k.stq.dma_start(
                    out=k.hv[g0:g0 + 512, hs].rearrange("(g p) f -> p g f", p=128), in_=V)), [Vb], [P.dbuf("hv", hg, g0)])
            else:
                P.dma("sp", (lambda: nc.sync.dma_start(
                    out=V, in_=k.hv[g0:g0 + 512, hs].rearrange("(g p) f -> p g f", p=128))), [], [Vb])
            for hl in H4:
                a_, ab_ = eb[hl]
                P.op("pool", (lambda: nc.gpsimd.tensor_copy(
                    out=el[:, hl, :], in_=a_.rearrange("p (c t) -> p c t", t=64)[:, :, lc])), [ab_], [elb[hl]])
            qr, qrb = qrR.next()
            if not fwd:
                P.dma("sp", (lambda: nc.sync.dma_start(
                    out=qr, in_=k.hq[hg * 4:(hg + 1) * 4, :, g0:g0 + 512].rearrange("c p t -> p c t"))), [], [qrb])
            for hl in H4:
                a_, ab_ = eb[hl]
                if fwd:
                    qps, qpsb = k.ps.next()
                    for c in range(NCH):
                        P.op("pe", (lambda: nc.tensor.matmul(qps, lhsT=Wq[:, c, fsl[hl]], rhs=xb[:, c, :],
                                                             start=(c == 0), stop=(c == NCH - 1))), [xbb[c], Wqb], [qpsb])
                    P.op("act", (lambda: nc.scalar.copy(out=qr[:, hl, :], in_=qps)), [qpsb], [qrb[hl]])
                P.op("dve", (lambda: nc.vector.tensor_tensor(out=qb[:, hl, :], in0=qr[:, hl, :], in1=a_, op=ALU.mult)),
                     [qrb[hl], ab_], [qbb[hl]])
            if fwd:
                P.dma(STORE_Q, (lambda: k.stq.dma_start(
                    out=k.hq[hg * 4:(hg + 1) * 4, :, g0:g0 + 512].rearrange("c p t -> p c t"), in_=qr)),
                    [qrb], [P.dbuf("hq", hg, g0)])
            ktp = []
            for hl in H4:
                kt_, ktb_ = k.ps.next()
                for g in G4:
                    P.op("pe", (lambda: nc.tensor.matmul(kt_[:, g * 128:(g + 1) * 128], lhsT=KK[:, g, lsl[hl]], rhs=identb,
                                                         start=True, stop=True)), [KKb[g], k.cbf_b], [ktb_])
                ktp.append((kt_, ktb_))
            kbs = []
            for hl in H4:
                kt_, ktb_ = ktp[hl]
                b_, bb_ = enb[hl]
                kb, kbb = kbR.next()
                P.op("dve", (lambda: nc.vector.tensor_tensor(out=kb, in0=kt_, in1=b_, op=ALU.mult)),
                     [ktb_, bb_], [kbb])
                kbs.append((kb, kbb))
            for hl in H4:
                kb, kbb = kbs[hl]
                aps, apsb = k.ps.next()
                for g in G4:
                    gs = tsl[g]
                    P.op("pe", (lambda: nc.tensor.matmul(aps[:, gs], lhsT=kb[:, gs], rhs=qb[:, hl, gs],
                                                         start=True, stop=True)), [kbb, qbb[hl]], [apsb])
                P.op("dve", (lambda: nc.vector.tensor_tensor(
                    out=at[:, hl, :], in0=aps, in1=mask4.rearrange("p g t -> p (g t)"), op=ALU.mult)),
                    [apsb, mask4b], [atb[hl]])
            if hg == 1 and ti + 1 < len(tiles):
                load_x(ti + 1)
            O, Ob = Or.next()
            order = list(range(8)) if fwd else list(range(7, -1, -1))
            for ci in order:
                g = ci // 2
                pb = 64 * (ci % 2)
                cs = slice(ci * 64, ci * 64 + 64)
                for hl in H4:
                    h = hg * 4 + hl
                    ls = lsl[hl]
                    ops_, opsb = k.ps.next()
                    P.op("pe", (lambda: nc.tensor.matmul(ops_[:, 0:64], lhsT=S16[:, h, :], rhs=qb[:, hl, cs],
                                                         start=True, stop=False)), [S16b[h], qbb[hl]], [opsb])
                    P.op("pe", (lambda: nc.tensor.matmul(ops_[:, 0:64], lhsT=V[:, g, ls], rhs=at[:, hl, cs],
                                                         start=False, stop=True)), [Vb[g], atb[hl]], [opsb])
                    P.op("act", (lambda: nc.scalar.copy(out=O[:, hl, cs], in_=ops_[:, 0:64])), [opsb], [Ob[hl]])
                    dps, dpsb = k.ps.next()
                    P.op("pe", (lambda: nc.tensor.matmul(dps[:, 0:128], lhsT=KD[pb:pb + 64, g, ls], rhs=V[pb:pb + 64, g, ls],
                                                         start=True, stop=True)), [KDb[g], Vb[g]], [dpsb])
                    P.op("dve", (lambda: nc.vector.scalar_tensor_tensor(
                        out=S32[:, h, :], in0=S32[:, h, :], scalar=el[:, hl, ci:ci + 1], in1=dps[:, 0:128],
                        op0=ALU.mult, op1=ALU.add)), [S32b[h], elb[hl], dpsb], [S32b[h]])
                    P.op("dve", (lambda: nc.vector.tensor_copy(out=S16[:, h, :], in_=S32[:, h, :])), [S32b[h]], [S16b[h]])
            if fwd:
                P.dma(STORE_Q, (lambda: k.stq.dma_start(
                    out=k.ofT[hg * 4:(hg + 1) * 4, :, g0:g0 + 512].rearrange("c p t -> p c t"), in_=O)),
                    [Ob], [P.dbuf("ofT", hg, g0)])
            else:
                of, ofb = ofr.next()
                P.dma("sp", (lambda: nc.sync.dma_start(
                    out=of, in_=k.ofT[hg * 4:(hg + 1) * 4, :, g0:g0 + 512].rearrange("c p t -> p c t"))), [], [ofb])
                og, ogb = ogr.next()
                sqs, gpl, rs_, sgl = [], [], [], []
                for hl in H4:
                    P.op("pool", (lambda: nc.gpsimd.tensor_tensor(out=of[:, hl, :], in0=of[:, hl, :], in1=O[:, hl, :],
                                                                  op=ALU.add)), [ofb[hl], Ob[hl]], [ofb[hl]])
                for hl in H4:
                    a_, ab_ = tA.next()
                    P.op("act", (lambda: nc.scalar.activation(out=a_, in_=of[:, hl, :], func=AF.Square)), [ofb[hl]], [ab_])
                    ms, msb = k.ps.next()
                    P.op("pe", (lambda: nc.tensor.matmul(ms, lhsT=onesv, rhs=a_, start=True, stop=True)),
                         [ab_, k.consts_b], [msb])
                    sqs.append((ms, msb))
                for hl in H4:
                    ms, msb = sqs[hl]
                    b_, bb_ = tB.next()
                    P.op("act", (lambda: nc.scalar.activation(out=b_, in_=ms, func=AF.Ln, bias=k.epsr[:, 0:1],
                                                              scale=1.0)), [msb, k.epsr_b], [bb_])
                    rs_.append((b_, bb_))
                for hl in H4:
                    b_, bb_ = rs_[hl]
                    P.op("act", (lambda: nc.scalar.activation(out=b_, in_=b_, func=AF.Exp, scale=-0.5)), [bb_], [bb_])
                for hl in H4:
                    gps, gpsb = k.ps.next()
                    for c in range(NCH):
                        P.op("pe", (lambda: nc.tensor.matmul(gps, lhsT=Wg[:, c, fsl[hl]], rhs=xb[:, c, :],
                                                             start=(c == 0), stop=(c == NCH - 1))), [xbb[c], Wgb], [gpsb])
                    gpl.append((gps, gpsb))
                for hl in H4:
                    gps, gpsb = gpl[hl]
                    c_, cb_ = tC.next()
                    P.op("act", (lambda: nc.scalar.activation(out=c_, in_=gps, func=AF.Silu)), [gpsb], [cb_])
                    h = hg * 4 + hl
                    b_, bb_ = rs_[hl]
                    P.op("dve", (lambda: nc.vector.scalar_tensor_tensor(
                        out=b_, in0=of[:, hl, :], scalar=sm[:, SP_GNC + h:SP_GNC + h + 1], in1=b_,
                        op0=ALU.mult, op1=ALU.mult)), [ofb[hl], bb_, k.small_b], [bb_])
                    P.op("dve", (lambda: nc.vector.tensor_tensor(out=og[:, hl, :], in0=b_, in1=c_, op=ALU.mult)),
                         [bb_, cb_], [ogb[hl]])
                P.dma(STORE_Q, (lambda: k.stq.dma_start(
                    out=k.ycat[hg * 4:(hg + 1) * 4, :, g0:g0 + 512].rearrange("c p t -> p c t"), in_=og)),
                    [ogb], [P.dbuf("og", hg, g0)])


def host_shared(inp, smax):
    d = {}
    f32 = lambda a: np.ascontiguousarray(np.asarray(a, np.float32))
    d["w_ab"] = f32(np.asarray(inp["w_in_ab"])[0][:, ab_columns()])
    d["w_oab"] = f32(np.asarray(inp["w_out_ab"])[0])
    d["w_c"] = f32(np.asarray(inp["w_in_c"])[0])
    d["w_oc"] = f32(np.asarray(inp["w_out_c"])[0])
    for l in range(2):
        d["w_up%d" % l] = f32(np.asarray(inp["ffn_w_up"])[l])
        d["w_dn%d" % l] = f32(np.asarray(inp["ffn_w_down"])[l])
    ca, sa, cb, sbb = rope_tables(smax)
    d["t_ca"], d["t_sa"], d["t_cb"], d["t_sb"] = ca, sa, cb, sbb
    d["masks"] = dil_masks()
    sp = np.zeros((128, NSMALL), np.float32)
    for l in range(2):
        sp[:, SP_LNMG0 + 32 * l:SP_LNMG0 + 32 * l + 8] = col128(np.asarray(inp["ln_mix_g"])[l], 8)
        sp[:, SP_LNMB0 + 32 * l:SP_LNMB0 + 32 * l + 8] = col128(np.asarray(inp["ln_mix_b"])[l], 8)
        sp[:, SP_LNFG0 + 32 * l:SP_LNFG0 + 32 * l + 8] = col128(np.asarray(inp["ln_ffn_g"])[l], 8)
        sp[:, SP_LNFB0 + 32 * l:SP_LNFB0 + 32 * l + 8] = col128(np.asarray(inp["ln_ffn_b"])[l], 8)
    pb = rope_perm_B()
    qn = np.asarray(inp["qn_ab"], np.float32)[0]
    kn = np.asarray(inp["kn_ab"], np.float32)[0]
    sp[:, SP_QN] = np.concatenate([qn, qn])
    sp[:, SP_QNP] = np.concatenate([qn[pb], qn[pb]])
    sp[:, SP_KN] = np.concatenate([kn, kn])
    sp[:, SP_KNP] = np.concatenate([kn[pb], kn[pb]])
    sp[:, SP_GNC:SP_GNC + 8] = col128(np.asarray(inp["gn_c"])[0], 8)
    for l, (cw, cbias) in enumerate(((SP_CW0, SP_CB0), (SP_CW1, SP_CB1))):
        w = np.asarray(inp["ffn_conv_w"], np.float32)[l]
        for j in range(3):
            sp[:, cw + j * NF:cw + (j + 1) * NF] = col128(w[j], NF)
        sp[:, cbias:cbias + NF] = col128(np.asarray(inp["ffn_conv_b"])[l], NF)
    lbf = np.asarray(inp["lb_fwd"], np.float32)
    lbb = np.asarray(inp["lb_bwd"], np.float32)
    for l in range(2):
        sp[:, SP_LBF + 8 * l:SP_LBF + 8 * l + 8] = col128(lbf[l], 8)
        sp[:, SP_LBB + 8 * l:SP_LBB + 8 * l + 8] = col128(lbb[l], 8)
    d["smallp"] = sp
    rep = np.zeros((128, 4, 1024), np.float32)
    rep[:, 0, :] = lbf[0][None, :]
    rep[:, 1, :] = lbf[1][None, :]
    rep[:, 2, :] = lbb[0][None, :]
    rep[:, 3, :] = lbb[1][None, :]
    d["lbrep"] = rep
    d["consts"] = host_consts()
    return d


ALL_PHASES = ("p1", "attA", "attB", "p3", "f0a", "f0b", "h1", "h2", "p6", "f1a", "f1b")


def kernel(**inputs):
    seqs = [2048, 8192]
    xp = np.asarray(inputs["x_prompt"], np.float32)
    xs = np.asarray(inputs["x_sample"], np.float32)
    shared = host_shared(inputs, max(seqs))
    nc = build2(seqs, ALL_PHASES)
    in_maps = []
    for c in range(8):
        m = dict(shared)
        m["x"] = np.ascontiguousarray(np.concatenate([xp[c], xs[c]], 0))
        in_maps.append(m)
    res = run_bass_kernel_spmd(nc, in_maps, core_ids=list(range(8)))
    yp = np.stack([res.results[c]["y"][0:2048] for c in range(8)], 0)
    ys = np.stack([res.results[c]["y"][2048:] for c in range(8)], 0)
    return (yp.astype(np.float32), ys.astype(np.float32))
```

```python
import math
import contextlib
import numpy as np
import ml_dtypes
import concourse.bass as bass
import concourse.mybir as mybir
from concourse.bass_utils import run_bass_kernel_spmd

F32 = mybir.dt.float32
BF16 = mybir.dt.bfloat16
AF = mybir.ActivationFunctionType
ALU = mybir.AluOpType

D = 1024
NCH = 8
DFF = 2816
NF = 22
ALPHA = 4.0 ** 0.25
LN_EPS = 1e-5
RMS_EPS = 1e-6
SAME_ENG_SYNC = True
NDMASEM = 24
STORE_Q = "pool"


class Buf:
    __slots__ = ("name", "w", "r")

    def __init__(self, name=""):
        self.name = name
        self.w = None
        self.r = []


def _flat(xs):
    out = []
    for x in xs:
        if isinstance(x, (list, tuple)):
            out.extend(_flat(x))
        else:
            out.append(x)
    return out


class Prog:
    def __init__(self, nc, marks=None):
        self.nc = nc
        self.eng = {"pe": nc.tensor, "act": nc.scalar, "dve": nc.vector, "pool": nc.gpsimd, "sp": nc.sync}
        self.marks = marks
        self.marked = []
        self.meta = []
        self.real = []
        self.ev = []
        self.dma_hist = {}
        self.dbufs = {}
        self.out_dmas = []
        if marks is not None:
            self.sems = {e: nc.alloc_semaphore("s_" + e) for e in self.eng}
            self.cnt = {e: 0 for e in self.eng}
            self.dsem = {}
            self.dcnt = {}
            self.waited = {e: {} for e in self.eng}

    def dbuf(self, *key):
        b = self.dbufs.get(key)
        if b is None:
            b = Buf(str(key))
            self.dbufs[key] = b
        return b

    def _add(self, eng, fn, reads, writes, is_dma, extra=()):
        reads = _flat(reads)
        writes = _flat(writes)
        idx = len(self.meta)
        deps = set(extra)
        for b in reads:
            if b.w is not None:
                deps.add(b.w)
        for b in writes:
            if b.w is not None:
                deps.add(b.w)
            deps.update(b.r)
        for b in writes:
            b.w = idx
            b.r = []
        for b in reads:
            if b.w != idx:
                b.r.append(idx)
        self.meta.append((eng, is_dma))
        self.real.append(fn is not None)
        fdeps = []
        for d in deps:
            de, ddma = self.meta[d]
            if (not ddma) and (not is_dma) and de == eng and (eng == "pe" or not SAME_ENG_SYNC):
                continue
            fdeps.append(d)
        if self.marks is None:
            self.marked.append(False)
            for d in fdeps:
                self.marked[d] = True
            return idx
        e = self.eng[eng]
        need = {}
        w = self.waited[eng]
        for d in fdeps:
            s, v = self.ev[d]
            key = id(s)
            if w.get(key, 0) >= v:
                continue
            if key not in need or need[key][1] < v:
                need[key] = (s, v)
        for key, (s, v) in need.items():
            e.wait_ge(s, v)
            w[key] = v
        if fn is None:
            self.ev.append(None)
            return idx
        inst = fn()
        if is_dma:
            if eng not in self.dsem:
                self.dsem[eng] = [self.nc.alloc_semaphore("d_%s_%d" % (eng, j)) for j in range(NDMASEM)]
                self.dcnt[eng] = 0
            j = self.dcnt[eng]
            self.dcnt[eng] += 1
            s = self.dsem[eng][j % NDMASEM]
            inst.then_inc(s, 16)
            self.ev.append((s, 16 * (j // NDMASEM + 1)))
        elif self.marks[idx]:
            self.cnt[eng] += 1
            inst.then_inc(self.sems[eng], 1)
            self.ev.append((self.sems[eng], self.cnt[eng]))
        else:
            self.ev.append(None)
        return idx

    def op(self, eng, fn, reads=(), writes=()):
        return self._add(eng, fn, reads, writes, False)

    def dma(self, q, fn, reads=(), writes=(), is_out=False):
        h = self.dma_hist.setdefault(q, [])
        extra = (h[-NDMASEM],) if len(h) >= NDMASEM else ()
        idx = self._add(q, fn, reads, writes, True, extra)
        h.append(idx)
        if is_out:
            self.out_dmas.append(idx)
        return idx

    def barrier(self):
        last = {}
        dmas = []
        for i, (e, isd) in enumerate(self.meta):
            if not self.real[i]:
                continue
            if isd:
                dmas.append(i)
            else:
                last[e] = i
        start = getattr(self, "_bar_from", 0)
        ex = tuple(last.values()) + tuple(d for d in dmas if d >= start)
        for e in ("pe", "act", "dve", "pool", "sp"):
            self._add(e, None, (), (), False, ex)
        self._bar_from = len(self.meta)

    def finish(self):
        self._add("sp", None, (), (), False, tuple(self.out_dmas))
        return self.marked


ALLOC = {"stack": None}


def _salloc(nc, name, shape, dtype):
    ALLOC["n"] = ALLOC.get("n", 0) + 1
    name = "%s_%d" % (name, ALLOC["n"])
    st = ALLOC["stack"]
    if st is None:
        return nc.alloc_sbuf_tensor(name, shape, dtype)
    return st.enter_context(nc.sbuf_tensor(name, shape, dtype))


class Ring:
    def __init__(self, nc, name, n, shape, dtype, psum=False, nsub=0):
        self.aps = []
        self.bufs = []
        for i in range(n):
            if psum:
                ALLOC["n"] = ALLOC.get("n", 0) + 1
                pname = "rp_%s%d_%d" % (name, i, ALLOC["n"])
                st = ALLOC["stack"]
                t = nc.alloc_psum_tensor(pname, shape, dtype) if st is None else st.enter_context(
                    nc.psum_tensor(pname, shape, dtype))
            else:
                t = _salloc(nc, "r_%s%d" % (name, i), shape, dtype)
            self.aps.append(t.ap())
            self.bufs.append([Buf("%s%d_%d" % (name, i, j)) for j in range(nsub)] if nsub else Buf("%s%d" % (name, i)))
        self.i = 0
        self.n = n

    def next(self):
        k = self.i % self.n
        self.i += 1
        return self.aps[k], self.bufs[k]


def sb(nc, name, shape, dtype, nsub=0):
    return _salloc(nc, "sb_" + name, shape, dtype).ap(), ([Buf(name + str(j)) for j in range(nsub)] if nsub else Buf(name))


def rope_perm_A():
    p = np.arange(64)
    p[0:8] = np.arange(8, 16)
    p[8:16] = np.arange(0, 8)
    return p


def rope_perm_B():
    p = np.arange(64)
    p[0:16] = np.arange(16, 32)
    p[16:32] = np.arange(0, 16)
    p[32:48] = np.arange(48, 64)
    p[48:64] = np.arange(32, 48)
    return p


def rope_tables(smax):
    t = np.arange(smax, dtype=np.float32)
    fa = (np.float32(500000.0) ** (-(np.arange(0, 16, 2, dtype=np.float32) / np.float32(16)))).astype(np.float32)
    ang = t[None, :] * fa[:, None]
    ca = np.ones((64, smax), np.float32)
    sa = np.zeros((64, smax), np.float32)
    ca[0:8] = np.cos(ang)
    ca[8:16] = np.cos(ang)
    sa[0:8] = -np.sin(ang)
    sa[8:16] = np.sin(ang)
    fb = (np.float32(10000.0) ** (-(np.arange(0, 32, 2, dtype=np.float32) / np.float32(32)))).astype(np.float32)
    row = np.floor(t / 64).astype(np.float32)
    col = (t - row * 64).astype(np.float32)
    ar = row[None, :] * fb[:, None]
    ac = col[None, :] * fb[:, None]
    cb = np.zeros((64, smax), np.float32)
    sbb = np.zeros((64, smax), np.float32)
    cb[0:16] = np.cos(ar)
    cb[16:32] = np.cos(ar)
    sbb[0:16] = -np.sin(ar)
    sbb[16:32] = np.sin(ar)
    cb[32:48] = np.cos(ac)
    cb[48:64] = np.cos(ac)
    sbb[32:48] = -np.sin(ac)
    sbb[48:64] = np.sin(ac)
    tile2 = lambda a: np.ascontiguousarray(np.concatenate([a, a], 0))
    return tile2(ca), tile2(sa), tile2(cb), tile2(sbb)


def dil_masks():
    m = np.zeros((20, 128, 512), np.float32)
    kk = np.arange(128)[:, None]
    qq = np.arange(512)[None, :]
    for i in range(20):
        dlt = (-1024 + 128 * i) + kk - qq
        a = np.abs(dlt)
        c = (a <= 64).astype(np.float32)
        c += ((dlt % 4 == 0) & (a <= 256)).astype(np.float32)
        c += ((dlt % 16 == 0) & (a <= 1024)).astype(np.float32)
        m[i] = c
    return np.ascontiguousarray(m.transpose(1, 0, 2)).astype(ml_dtypes.bfloat16)


def col128(v, nchunk):
    return np.ascontiguousarray(np.asarray(v, np.float32).reshape(nchunk, 128).T)


def ab_columns():
    A_W = 512
    qa = [h * 64 + np.arange(64) for h in range(8)]
    ka = [A_W + h * 64 + np.arange(64) for h in range(8)]
    o3 = 3 * A_W
    qb = [o3 + h * 64 + np.arange(64) for h in range(8)]
    o4 = o3 + 512
    kb = [o4 + h * 64 + np.arange(64) for h in range(2)]
    cols = []
    for c in range(4):
        cols += [qa[2 * c], qa[2 * c + 1]]
    for c in range(4):
        cols += [ka[2 * c], ka[2 * c + 1]]
    for c in range(4):
        cols += [qb[c], qb[4 + c]]
    cols += [kb[0], kb[1]]
    cols += [2 * A_W + np.arange(512)]
    o5 = o4 + 128
    cols += [o5 + np.arange(128)]
    return np.concatenate(cols)


NAB = 2304


class K:
    pass


def build(seqs, phases, debug=(), marks=None, feed=()):
    nc = bass.Bass("TRN2", target_bir_lowering=False)
    P = Prog(nc, marks)
    T = sum(seqs)
    SMAX = max(seqs)
    offs = [sum(seqs[:i]) for i in range(len(seqs))]
    k = K()
    k.nc, k.P, k.T, k.seqs, k.offs = nc, P, T, seqs, offs
    k.stq = P.eng[STORE_Q]

    def din(name, shape, dt=F32):
        return nc.dram_tensor(name, list(shape), dt, kind="ExternalInput").ap()

    def dscr(name, shape, dt):
        kind = "ExternalOutput" if name in debug else ("ExternalInput" if name in feed else "Internal")
        return nc.dram_tensor(name, list(shape), dt, kind=kind).ap()

    k.x = din("x", [T, D])
    k.w_ab = din("w_ab", [D, NAB])
    k.w_oab = din("w_oab", [D, D])
    k.w_c = din("w_c", [D, 5120])
    k.w_oc = din("w_oc", [D, D])
    k.w_up = [din("w_up%d" % l, [D, 2 * DFF]) for l in range(2)]
    k.w_dn = [din("w_dn%d" % l, [DFF, D]) for l in range(2)]
    k.t_ca = din("t_ca", [128, SMAX])
    k.t_sa = din("t_sa", [128, SMAX])
    k.t_cb = din("t_cb", [128, SMAX])
    k.t_sb = din("t_sb", [128, SMAX])
    k.masks_d = din("masks", [128, 20, 512], BF16)
    k.smallp = din("smallp", [128, NSMALL])
    k.lbrep_d = din("lbrep", [128, 4, 1024])
    k.consts_d = din("consts", [128, NCONST])
    k.y = nc.dram_tensor("y", [T, D], F32, kind="ExternalOutput").ap()
    k.xT = dscr("xT", [NCH, 128, T], F32)
    k.qaT = dscr("qaT", [4, 128, T], BF16)
    k.kaT = dscr("kaT", [4, 128, T], BF16)
    k.qbT = dscr("qbT", [4, 128, T], BF16)
    k.kbT = dscr("kbT", [1, 128, T], BF16)
    k.va = dscr("va", [T, 8 * 65], BF16)
    k.vb = dscr("vb", [T, 2 * 65], BF16)
    k.ycat = dscr("ycat", [NCH, 128, T], BF16)
    k.den = dscr("den", [16, T], F32)
    k.hq = dscr("hq", [NCH, 128, T], BF16)
    k.hv = dscr("hv", [T, 1024], BF16)
    k.x1T = dscr("x1T", [NCH, 128, T], F32)
    k.actT = dscr("actT", [NF, 128, T], BF16)
    k.x2T = dscr("x2T", [NCH, 128, T], F32)
    k.ofT = dscr("ofT", [NCH, 128, T], F32)
    k.x3T = dscr("x3T", [NCH, 128, T], F32)


    setup_consts(k)

    def run_phase(fn, *a, **kw):
        with contextlib.ExitStack() as st:
            ALLOC["stack"] = st
            if fn is not phase_att:
                k.ps = Ring(nc, "ps", 8, [128, 512], F32, psum=True)
            fn(*a, **kw)
            P.barrier()
        ALLOC["stack"] = None

    for name, fn, a, kw in phase_table(k):
        if name in phases:
            run_phase(fn, *a, **kw)
    m = P.finish()
    return nc, m


def phase_table(k):
    return [
        ("p1", phase_p1, (k,), {}),
        ("attA", phase_att, (k, "A"), {}),
        ("attB", phase_att, (k, "B"), {}),
        ("p3", phase_proj_ln, (k,), dict(src=k.ycat, resid=k.xT, dst=k.x1T, w_dram=k.w_oab, nk=8, lng=SP_LNMG0, lnb=SP_LNMB0, tag="p3", den=k.den)),
        ("f0a", phase_ffn_up, (k, 0, k.x1T), {}),
        ("f0b", phase_proj_ln, (k,), dict(src=k.actT, resid=k.x1T, dst=k.x2T, w_dram=k.w_dn[0], nk=NF, lng=SP_LNFG0, lnb=SP_LNFB0, tag="f0b")),
        ("h1", phase_hgrn, (k, True), {}),
        ("h2", phase_hgrn, (k, False), {}),
        ("p6", phase_proj_ln, (k,), dict(src=k.ycat, resid=k.x2T, dst=k.x3T, w_dram=k.w_oc, nk=8, lng=SP_LNMG1, lnb=SP_LNMB1, tag="p6")),
        ("f1a", phase_ffn_up, (k, 1, k.x3T), {}),
        ("f1b", phase_proj_ln, (k,), dict(src=k.actT, resid=k.x3T, dst=None, w_dram=k.w_dn[1], nk=NF, lng=SP_LNFG1, lnb=SP_LNFB1, tag="f1b")),
    ]


def build2(seqs, phases, debug=(), feed=()):
    _, marks = build(seqs, phases, debug, None, feed)
    nc, _ = build(seqs, phases, debug, marks, feed)
    return nc


SP_LNMG0, SP_LNMB0, SP_LNFG0, SP_LNFB0 = 0, 8, 16, 24
SP_LNMG1, SP_LNMB1, SP_LNFG1, SP_LNFB1 = 32, 40, 48, 56
SP_QN, SP_QNP, SP_KN, SP_KNP = 64, 65, 66, 67
SP_GNC = 68
SP_CW0 = 76
SP_CB0 = SP_CW0 + 66
SP_CW1 = SP_CB0 + 22
SP_CB1 = SP_CW1 + 66
SP_LBF = SP_CB1 + 22
SP_LBB = SP_LBF + 16
NSMALL = SP_LBB + 16

C_IDENT = 0
C_ONESBLK = 128
C_ONES = 256
C_TRIL = 384
C_TRIU_S = 512
C_TRIU = 640
C_TRIL_S = 768
C_ONESV = 896
C_PERMA = 1024
C_PERMB = 1152
NCONST = 1280


def host_consts():
    c = np.zeros((128, NCONST), np.float32)
    c[:, C_IDENT:C_IDENT + 128] = np.eye(128, dtype=np.float32)
    blk = np.zeros((128, 128), np.float32)
    blk[0:64, 0:64] = 1.0 / 64
    blk[64:128, 64:128] = 1.0 / 64
    c[:, C_ONESBLK:C_ONESBLK + 128] = blk
    c[:, C_ONES:C_ONES + 128] = 1.0
    c[:, C_ONESV:C_ONESV + 128] = 1.0 / 128
    for col, pm in ((C_PERMA, rope_perm_A()), (C_PERMB, rope_perm_B())):
        for m in range(128):
            c[64 * (m // 64) + pm[m % 64], col + m] = 1.0
    s = np.arange(128)[:, None]
    t = np.arange(128)[None, :]
    same = (s // 64) == (t // 64)
    c[:, C_TRIL:C_TRIL + 128] = (same & (s <= t))
    c[:, C_TRIU_S:C_TRIU_S + 128] = (same & (s > t))
    c[:, C_TRIU:C_TRIU + 128] = (same & (s >= t))
    c[:, C_TRIL_S:C_TRIL_S + 128] = (same & (s < t))
    return c


def setup_consts(k):
    nc, P = k.nc, k.P
    k.consts, k.consts_b = sb(nc, "consts", [128, NCONST], F32)
    k.small, k.small_b = sb(nc, "small", [128, NSMALL], F32)
    P.dma("sp", lambda: nc.sync.dma_start(out=k.consts, in_=k.consts_d), [], [k.consts_b])
    P.dma("sp", lambda: nc.sync.dma_start(out=k.small, in_=k.smallp), [], [k.small_b])
    k.cbf, k.cbf_b = sb(nc, "cbf", [128, NCONST], BF16)
    P.op("dve", lambda: nc.vector.tensor_copy(out=k.cbf, in_=k.consts), [k.consts_b], [k.cbf_b])
    k.onesd, k.onesd_b = sb(nc, "onesd", [128, 128], BF16)
    P.op("dve", lambda: nc.vector.memset(k.onesd, 1.0 / 1024), [], [k.onesd_b])
    k.epsr, k.epsr_b = sb(nc, "epsr", [128, 1], F32)
    P.op("dve", lambda: nc.vector.memset(k.epsr, RMS_EPS), [], [k.epsr_b])
    k.epsl, k.epsl_b = sb(nc, "epsl", [128, 1], F32)
    P.op("dve", lambda: nc.vector.memset(k.epsl, LN_EPS), [], [k.epsl_b])


def load_weight_bf16(k, name, w_dram, nk, ncols, col0=0, sw=2048):
    nc, P = k.nc, k.P
    nchunk = nk * (-(-ncols // sw))
    wt, wb = sb(nc, name, [128, nk, ncols], BF16, nsub=nchunk)
    wstage = Ring(nc, name + "_stg", 2, [128, sw], F32)
    i = 0
    for kk in range(nk):
        for c0 in range(0, ncols, sw):
            cw = min(sw, ncols - c0)
            st, stb = wstage.next()
            P.dma("sp", (lambda st=st, kk=kk, c0=c0, cw=cw: nc.sync.dma_start(
                out=st[:, 0:cw], in_=w_dram[kk * 128:(kk + 1) * 128, col0 + c0:col0 + c0 + cw])), [], [stb])
            if i % 2 == 0:
                P.op("act", (lambda st=st, kk=kk, c0=c0, cw=cw: nc.scalar.copy(
                    out=wt[:, kk, c0:c0 + cw], in_=st[:, 0:cw])), [stb], [wb[i]])
            else:
                P.op("dve", (lambda st=st, kk=kk, c0=c0, cw=cw: nc.vector.tensor_copy(
                    out=wt[:, kk, c0:c0 + cw], in_=st[:, 0:cw])), [stb], [wb[i]])
            i += 1
    return wt, wb


def tiles512(k):
    for si, (off, S) in enumerate(zip(k.offs, k.seqs)):
        for j in range(S // 512):
            yield si, off, S, j * 512, off + j * 512


def phase_p1(k):
    nc, P = k.nc, k.P
    W, Wb = load_weight_bf16(k, "w_ab_sb", k.w_ab, 8, NAB)
    hbR = Ring(nc, "p1hb", 3, [128, 512], BF16)
    permA = k.cbf[:, C_PERMA:C_PERMA + 128]
    permB = k.cbf[:, C_PERMB:C_PERMB + 128]
    xtok = Ring(nc, "xtok", 2, [128, 4, D], F32)
    xT32 = Ring(nc, "xT32", 1, [128, NCH, 512], F32, nsub=NCH)
    xTb = Ring(nc, "xTb", 2, [128, NCH, 512], BF16, nsub=NCH)
    tab = Ring(nc, "ropetab", 1, [128, 4, 512], F32)
    t1r = Ring(nc, "p1t1", 2, [128, 512], F32)
    t2r = Ring(nc, "p1t2", 2, [128, 512], F32)
    sqr = Ring(nc, "p1sq", 2, [128, 512], F32)
    rsr = Ring(nc, "p1rs", 2, [128, 512], F32)
    qkout = Ring(nc, "p1qk", 1, [128, 13, 512], BF16, nsub=13)
    vaug = Ring(nc, "p1va", 2, [128, 4, 8 * 65], BF16)
    vbug = Ring(nc, "p1vb", 2, [128, 4, 2 * 65], BF16)
    for r in (vaug, vbug):
        for ap, b in zip(r.aps, r.bufs):
            P.op("pool", (lambda ap=ap: nc.gpsimd.memset(ap, 1.0)), [], [b])
    ident = k.consts[:, C_IDENT:C_IDENT + 128]
    onesblk = k.consts[:, C_ONESBLK:C_ONESBLK + 128]
    sm = k.small
    for si, off, S, p0, g0 in tiles512(k):
        xt, xtb = xtok.next()
        P.dma("sp", (lambda xt=xt, g0=g0: nc.sync.dma_start(
            out=xt, in_=k.x[g0:g0 + 512, :].rearrange("(g p) d -> p g d", p=128))), [], [xtb])
        tb, tbb = tab.next()
        for i, src in enumerate((k.t_ca, k.t_sa, k.t_cb, k.t_sb)):
            P.dma("sp", (lambda tb=tb, i=i, src=src, p0=p0: nc.sync.dma_start(
                out=tb[:, i, :], in_=src[:, p0:p0 + 512])), [], [tbb])
        x32, x32b = xT32.next()
        xb, xbb = xTb.next()
        for c in range(NCH):
            ps, psb = k.ps.next()
            for g in range(4):
                P.op("pe", (lambda ps=ps, xt=xt, g=g, c=c: nc.tensor.transpose(
                    out=ps[:, g * 128:(g + 1) * 128], in_=xt[:, g, c * 128:(c + 1) * 128], identity=ident)),
                    [xtb, k.consts_b], [psb])
            P.op("act", (lambda ps=ps, x32=x32, c=c: nc.scalar.copy(out=x32[:, c, :], in_=ps)), [psb], [x32b[c]])
            P.op("dve", (lambda x32=x32, xb=xb, c=c: nc.vector.tensor_copy(out=xb[:, c, :], in_=x32[:, c, :])), [x32b[c]], [xbb[c]])
        P.dma(STORE_Q, (lambda x32=x32, g0=g0: k.stq.dma_start(
            out=k.xT[:, :, g0:g0 + 512].rearrange("c p t -> p c t"), in_=x32)),
            [x32b], [P.dbuf("xT", g0)])

        def proj(chunk):
            ps, psb = k.ps.next()
            for c in range(NCH):
                P.op("pe", (lambda ps=ps, c=c, chunk=chunk: nc.tensor.matmul(
                    ps, lhsT=W[:, c, chunk * 128:(chunk + 1) * 128], rhs=xb[:, c, :],
                    start=(c == 0), stop=(c == NCH - 1))), [Wb, xbb[c]], [psb])
            return ps, psb

        def rot(ps, psb, permM):
            hb, hbb = hbR.next()
            P.op("act", (lambda: nc.scalar.copy(out=hb, in_=ps)), [psb], [hbb])
            pp, ppb = k.ps.next()
            P.op("pe", (lambda: nc.tensor.matmul(pp, lhsT=permM, rhs=hb, start=True, stop=True)), [hbb, k.cbf_b], [ppb])
            return pp, ppb, hbb

        qo, qob = qkout.next()
        for which, base in ((0, 0), (1, 4)):
            for c in range(4):
                ps, psb = proj(base + c)
                pp, ppb, hbb = rot(ps, psb, permA)
                t1, t1b = t1r.next()
                t2, t2b = t2r.next()
                P.op("dve", (lambda t1=t1, ps=ps: nc.vector.tensor_tensor(
                    out=t1, in0=ps, in1=tb[:, 0, :], op=ALU.mult)), [psb, tbb, hbb], [t1b])
                P.op("dve", (lambda t2=t2, pp=pp: nc.vector.tensor_tensor(
                    out=t2, in0=pp, in1=tb[:, 1, :], op=ALU.mult)), [ppb, tbb], [t2b])
                slot = which * 4 + c
                P.op("pool", (lambda t1=t1, t2=t2, slot=slot: nc.gpsimd.tensor_tensor(
                    out=qo[:, slot, :], in0=t1, in1=t2, op=ALU.add)), [t1b, t2b], [qob[slot]])
        for which, base, n, gcol in ((0, 8, 4, SP_QN), (1, 12, 1, SP_KN)):
            for c in range(n):
                ps, psb = proj(base + c)
                pp, ppb, hbb = rot(ps, psb, permB)
                sq, sqb = sqr.next()
                P.op("act", (lambda sq=sq, ps=ps: nc.scalar.activation(out=sq, in_=ps, func=AF.Square)), [psb], [sqb])
                ms, msb = k.ps.next()
                P.op("pe", (lambda ms=ms, sq=sq: nc.tensor.matmul(ms, lhsT=onesblk, rhs=sq, start=True, stop=True)),
                     [sqb, k.consts_b], [msb])
                rs, rsb = rsr.next()
                P.op("act", (lambda rs=rs, ms=ms: nc.scalar.activation(
                    out=rs, in_=ms, func=AF.Ln, bias=k.epsr[:, 0:1], scale=1.0)), [msb, k.epsr_b], [rsb])
                P.op("act", (lambda rs=rs: nc.scalar.activation(out=rs, in_=rs, func=AF.Exp, scale=-0.5)), [rsb], [rsb])
                t1, t1b = t1r.next()
                t2, t2b = t2r.next()
                P.op("dve", (lambda t1=t1, ps=ps, gcol=gcol: nc.vector.scalar_tensor_tensor(
                    out=t1, in0=ps, scalar=sm[:, gcol:gcol + 1], in1=tb[:, 2, :], op0=ALU.mult, op1=ALU.mult)),
                    [psb, tbb, k.small_b, sqb, hbb], [t1b])
                P.op("dve", (lambda t2=t2, pp=pp, gcol=gcol: nc.vector.scalar_tensor_tensor(
                    out=t2, in0=pp, scalar=sm[:, gcol + 1:gcol + 2], in1=tb[:, 3, :], op0=ALU.mult, op1=ALU.mult)),
                    [ppb, tbb, k.small_b], [t2b])
                P.op("pool", (lambda t1=t1, t2=t2: nc.gpsimd.tensor_tensor(
                    out=t1, in0=t1, in1=t2, op=ALU.add)), [t1b, t2b], [t1b])
                slot = 8 + c if which == 0 else 12
                P.op("dve", (lambda t1=t1, rs=rs, slot=slot: nc.vector.tensor_tensor(
                    out=qo[:, slot, :], in0=t1, in1=rs, op=ALU.mult)), [t1b, rsb], [qob[slot]])
        for dst, s0, n in ((k.qaT, 0, 4), (k.kaT, 4, 4), (k.qbT, 8, 4), (k.kbT, 12, 1)):
            P.dma(STORE_Q, (lambda dst=dst, s0=s0, n=n, g0=g0: k.stq.dma_start(
                out=dst[:, :, g0:g0 + 512].rearrange("c p t -> p c t"), in_=qo[:, s0:s0 + n, :])),
                [qob], [P.dbuf("qk", id(dst), g0)])
        va, vab = vaug.next()
        vb, vbb = vbug.next()
        for g in range(4):
            ps, psb = k.ps.next()
            for c in range(NCH):
                P.op("pe", (lambda ps=ps, c=c, g=g: nc.tensor.matmul(
                    ps, lhsT=xb[:, c, g * 128:(g + 1) * 128], rhs=W[:, c, 1664:2176],
                    start=(c == 0), stop=(c == NCH - 1))), [Wb, xbb[c]], [psb])
            P.op("act", (lambda ps=ps, va=va, g=g: nc.scalar.copy(
                out=va[:, g, :].rearrange("p (h e) -> p h e", e=65)[:, :, 0:64],
                in_=ps.rearrange("p (h e) -> p h e", e=64))), [psb], [vab])
            ps2, ps2b = k.ps.next()
            for c in range(NCH):
                P.op("pe", (lambda ps2=ps2, c=c, g=g: nc.tensor.matmul(
                    ps2[:, 0:128], lhsT=xb[:, c, g * 128:(g + 1) * 128], rhs=W[:, c, 2176:2304],
                    start=(c == 0), stop=(c == NCH - 1))), [Wb, xbb[c]], [ps2b])
            P.op("act", (lambda ps2=ps2, vb=vb, g=g: nc.scalar.copy(
                out=vb[:, g, :].rearrange("p (h e) -> p h e", e=65)[:, :, 0:64],
                in_=ps2[:, 0:128].rearrange("p (h e) -> p h e", e=64))), [ps2b], [vbb])
        P.dma(STORE_Q, (lambda va=va, g0=g0: k.stq.dma_start(
            out=k.va[g0:g0 + 512, :].rearrange("(g p) f -> p g f", p=128), in_=va)), [vab], [P.dbuf("va", g0)])
        P.dma(STORE_Q, (lambda vb=vb, g0=g0: k.stq.dma_start(
            out=k.vb[g0:g0 + 512, :].rearrange("(g p) f -> p g f", p=128), in_=vb)), [vbb], [P.dbuf("vb", g0)])


def phase_att(k, which):
    nc, P = k.nc, k.P
    isA = which == "A"
    SMAX = max(k.seqs)
    nktm = SMAX // 128
    sR = Ring(nc, "att_s", 2, [128, 1024], F32, psum=True)
    oR = Ring(nc, "att_O", 4, [128, 512], F32, psum=True)
    qT = Ring(nc, "att_q", 2, [128, SMAX], BF16)
    kT = Ring(nc, "att_k", 2 if isA else 1, [128, SMAX], BF16)
    vR = Ring(nc, "att_v", 2 if isA else 1, [128, nktm, 130], BF16)
    pr = Ring(nc, "att_p", 3, [128, 1024], BF16)
    pm = Ring(nc, "att_pm", 3, [128, 1024], BF16, nsub=2) if isA else None
    ysb = Ring(nc, "att_y", 4, [64, 512], BF16)
    dsb = Ring(nc, "att_d", 4, [65, 512], F32)
    if isA:
        masks, masks_b = sb(nc, "masks", [128, 20, 512], BF16)
        P.dma("sp", lambda: nc.sync.dma_start(out=masks, in_=k.masks_d), [], [masks_b])
    for si, (off, S) in enumerate(zip(k.offs, k.seqs)):
        nkt = S // 128
        nqb = S // 512
        if not isA:
            kt_, ktb = kT.next()
            P.dma("sp", (lambda: nc.sync.dma_start(out=kt_[:, 0:S], in_=k.kbT[0, :, off:off + S])), [], [ktb])
            v_, vb_ = vR.next()
            P.dma("sp", (lambda: nc.sync.dma_start(
                out=v_[:, 0:nkt, :], in_=k.vb[off:off + S, :].rearrange("(t p) f -> p t f", p=128))), [], [vb_])
        for c in range(4):
            q_, qb_ = qT.next()
            qsrc = k.qaT if isA else k.qbT
            P.dma("sp", (lambda: nc.sync.dma_start(out=q_[:, 0:S], in_=qsrc[c, :, off:off + S])), [], [qb_])
            if isA:
                kt_, ktb = kT.next()
                P.dma("sp", (lambda: nc.sync.dma_start(out=kt_[:, 0:S], in_=k.kaT[c, :, off:off + S])), [], [ktb])
                v_, vb_ = vR.next()
                P.dma("sp", (lambda: nc.sync.dma_start(
                    out=v_[:, 0:nkt, :],
                    in_=k.va[off:off + S, c * 130:(c + 1) * 130].rearrange("(t p) f -> p t f", p=128))), [], [vb_])
            items = []
            for qi in range(nqb):
                q0 = qi * 512
                if isA:
                    kts = list(range(max(0, (q0 - 1024) // 128), min(nkt - 1, (q0 + 1535) // 128) + 1))
                else:
                    kts = list(range(nkt))
                for ii, kt in enumerate(kts):
                    items.append((qi, ii, kt, len(kts)))
            sq = {}

            def issue_qk(j):
                qi, ii, kt, n = items[j]
                s_, sb_ = sR.next()
                for hh in range(2):
                    pb = 64 * hh
                    P.op("pe", (lambda: nc.tensor.matmul(
                        s_[:, hh * 512:(hh + 1) * 512], lhsT=kt_[pb:pb + 64, kt * 128:kt * 128 + 128],
                        rhs=q_[pb:pb + 64, qi * 512:qi * 512 + 512], start=True, stop=True)), [ktb, qb_], [sb_])
                sq[j] = (s_, sb_)

            issue_qk(0)
            O = None
            for j, (qi, ii, kt, n) in enumerate(items):
                q0 = qi * 512
                k0 = kt * 128
                if ii == 0:
                    O = [oR.next(), oR.next()]
                s_, sb_ = sq.pop(j)
                p_, pb_ = pr.next()
                P.op("act", (lambda: nc.scalar.activation(out=p_, in_=s_, func=AF.Exp, scale=0.125)), [sb_], [pb_])
                if isA:
                    mi = (k0 - q0 + 1024) // 128
                    p2, p2b = pm.next()
                    for hh in range(2):
                        eng_, e_ = (("dve", nc.vector), ("pool", nc.gpsimd))[hh if MASK_SPLIT else 0]
                        P.op(eng_, (lambda: e_.tensor_tensor(
                            out=p2[:, hh * 512:(hh + 1) * 512], in0=p_[:, hh * 512:(hh + 1) * 512],
                            in1=masks[:, mi, :], op=ALU.mult)), [pb_, masks_b], [p2b[hh]])
                    p_, pb_ = p2, p2b
                if j + 1 < len(items):
                    issue_qk(j + 1)
                for hh in range(2):
                    P.op("pe", (lambda: nc.tensor.matmul(
                        O[hh][0][0:65, :], lhsT=v_[:, kt, hh * 65:(hh + 1) * 65], rhs=p_[:, hh * 512:(hh + 1) * 512],
                        start=(ii == 0), stop=(ii == n - 1))), [vb_, (pb_[hh] if isinstance(pb_, list) else pb_)], [O[hh][1]])
                if ii == n - 1:
                    g0 = off + q0
                    for hh in range(2):
                        if isA:
                            head = 2 * c + hh
                            ochunk, obase, drow = c, 64 * hh, head
                        else:
                            head = c + 4 * hh
                            ochunk, obase, drow = 4 + head // 2, 64 * (head % 2), 8 + head
                        ops, opsb = O[hh]
                        y_, yb_ = ysb.next()
                        d_, db_ = dsb.next()
                        P.op("act", (lambda: nc.scalar.copy(out=y_, in_=ops[0:64, :])), [opsb], [yb_])
                        P.op("act", (lambda: nc.scalar.activation(out=d_[64:65, :], in_=ops[64:65, :], func=AF.Ln)),
                             [opsb], [db_])
                        P.op("act", (lambda: nc.scalar.activation(out=d_[64:65, :], in_=d_[64:65, :], func=AF.Exp,
                                                                  scale=-1.0)), [db_], [db_])
                        P.dma(STORE_Q, (lambda: k.stq.dma_start(
                            out=k.ycat[ochunk, obase:obase + 64, g0:g0 + 512], in_=y_)), [yb_], [P.dbuf("ycat", head, g0)])
                        P.dma(STORE_Q, (lambda: k.stq.dma_start(
                            out=k.den[drow:drow + 1, g0:g0 + 512], in_=d_[64:65, :])), [db_], [P.dbuf("den", drow, g0)])


def layer_norm_tile(k, r, rb_, lng, lnb, rings, out, outb):
    nc, P = k.nc, k.P
    sm = k.small
    rbf, r2f, stat = rings
    rb16, rb16b = rbf.next()
    r2, r2b = r2f.next()
    for n in range(NCH):
        P.op("act", (lambda n=n: nc.scalar.copy(out=rb16[:, n, :], in_=r[:, n, :])), [rb_[n]], [rb16b[n]])
        P.op("act", (lambda n=n: nc.scalar.activation(out=r2[:, n, :], in_=r[:, n, :], func=AF.Square)), [rb_[n]], [r2b[n]])
    mps, mpsb = k.ps.next()
    eps_, epsb = k.ps.next()
    for n in range(NCH):
        P.op("pe", (lambda n=n: nc.tensor.matmul(mps, lhsT=k.onesd, rhs=rb16[:, n, :], start=(n == 0), stop=(n == NCH - 1))),
             [k.onesd_b, rb16b[n]], [mpsb])
    for n in range(NCH):
        P.op("pe", (lambda n=n: nc.tensor.matmul(eps_, lhsT=k.onesd, rhs=r2[:, n, :], start=(n == 0), stop=(n == NCH - 1))),
             [k.onesd_b, r2b[n]], [epsb])
    m2, m2b = stat.next()
    P.op("act", (lambda: nc.scalar.activation(out=m2, in_=mps, func=AF.Square)), [mpsb], [m2b])
    P.op("dve", (lambda: nc.vector.tensor_tensor(out=m2, in0=eps_, in1=m2, op=ALU.subtract)), [epsb, m2b], [m2b])
    P.op("act", (lambda: nc.scalar.activation(out=m2, in_=m2, func=AF.Ln, bias=k.epsl[:, 0:1], scale=1.0)),
         [m2b, k.epsl_b], [m2b])
    P.op("act", (lambda: nc.scalar.activation(out=m2, in_=m2, func=AF.Exp, scale=-0.5)), [m2b], [m2b])
    for n in range(NCH):
        P.op("dve", (lambda n=n: nc.vector.tensor_tensor(out=r[:, n, :], in0=r[:, n, :], in1=mps, op=ALU.subtract)),
             [rb_[n], mpsb], [rb_[n]])
        P.op("dve", (lambda n=n: nc.vector.scalar_tensor_tensor(
            out=r[:, n, :], in0=r[:, n, :], scalar=sm[:, lng + n:lng + n + 1], in1=m2, op0=ALU.mult, op1=ALU.mult)),
            [rb_[n], m2b, k.small_b], [rb_[n]])
        P.op("act", (lambda n=n: nc.scalar.activation(
            out=out[:, n, :], in_=r[:, n, :], func=AF.Identity, bias=sm[:, lnb + n:lnb + n + 1], scale=1.0)),
            [rb_[n], k.small_b], [outb[n]])


def ln_rings(k, tag):
    nc = k.nc
    return (Ring(nc, tag + "_rb16", 1, [128, NCH, 512], BF16, nsub=NCH), Ring(nc, tag + "_r2", 1, [128, NCH, 512], BF16, nsub=NCH),
            Ring(nc, tag + "_stat", 2, [128, 512], F32))


def phase_proj_ln(k, src, resid, dst, w_dram, nk, lng, lnb, tag, den=None):
    nc, P = k.nc, k.P
    W, Wb = load_weight_bf16(k, "w_" + tag, w_dram, nk, D)
    srcR = Ring(nc, tag + "_src", 2, [128, nk, 512], BF16)
    resR = Ring(nc, tag + "_res", 1 if nk > 8 else 2, [128, NCH, 512], F32)
    rR = Ring(nc, tag + "_r", 2, [128, NCH, 512], F32, nsub=NCH)
    outR = Ring(nc, tag + "_out", 1 if nk > 8 else 2, [128, NCH, 512], F32, nsub=NCH)
    lnr = ln_rings(k, tag)
    if den is not None:
        dnR = Ring(nc, tag + "_dn", 1, [128, nk, 512], F32)
    if dst is None:
        ytok = Ring(nc, tag + "_ytok", 2, [128, D], F32)
        ident = k.consts[:, C_IDENT:C_IDENT + 128]
    for si, off, S, p0, g0 in tiles512(k):
        s_, sb_ = srcR.next()
        P.dma("sp", (lambda: nc.sync.dma_start(out=s_, in_=src[:, :, g0:g0 + 512].rearrange("c p t -> p c t"))), [], [sb_])
        if den is not None:
            dn, dnb = dnR.next()
            for half in range(2):
                P.dma("sp", (lambda: nc.sync.dma_start(
                    out=dn[64 * half:64 * half + 64, :, :],
                    in_=den[:, g0:g0 + 512].rearrange("(n h) t -> h n t", h=2)[half:half + 1].broadcast_to([64, nk, 512]))),
                    [], [dnb])
            P.op("dve", (lambda: nc.vector.tensor_tensor(
                out=s_.rearrange("p c t -> p (c t)"), in0=s_.rearrange("p c t -> p (c t)"),
                in1=dn.rearrange("p c t -> p (c t)"), op=ALU.mult)), [sb_, dnb], [sb_])
        x_, xb_ = resR.next()
        P.dma("sp", (lambda: nc.sync.dma_start(out=x_, in_=resid[:, :, g0:g0 + 512].rearrange("c p t -> p c t"))), [], [xb_])
        r, rb_ = rR.next()
        for n in range(NCH):
            ps, psb = k.ps.next()
            for kk in range(nk):
                P.op("pe", (lambda: nc.tensor.matmul(ps, lhsT=W[:, kk, n * 128:(n + 1) * 128], rhs=s_[:, kk, :],
                                                     start=(kk == 0), stop=(kk == nk - 1))), [Wb, sb_], [psb])
            P.op("dve", (lambda: nc.vector.scalar_tensor_tensor(
                out=r[:, n, :], in0=x_[:, n, :], scalar=ALPHA, in1=ps, op0=ALU.mult, op1=ALU.add)), [xb_, psb], [rb_[n]])
        o_, ob_ = outR.next()
        layer_norm_tile(k, r, rb_, lng, lnb, lnr, o_, ob_)
        if dst is not None:
            P.dma(STORE_Q, (lambda: k.stq.dma_start(out=dst[:, :, g0:g0 + 512].rearrange("c p t -> p c t"), in_=o_)),
                  [ob_], [P.dbuf(tag, g0)])
        else:
            for g in range(4):
                yt, ytb = ytok.next()
                for half in range(2):
                    ps, psb = k.ps.next()
                    for j in range(4):
                        n = half * 4 + j
                        P.op("pe", (lambda: nc.tensor.transpose(
                            out=ps[:, j * 128:(j + 1) * 128], in_=o_[:, n, g * 128:(g + 1) * 128], identity=ident)),
                            [ob_[n], k.consts_b], [psb])
                    if half == 0:
                        P.op("act", (lambda: nc.scalar.copy(out=yt[:, 0:512], in_=ps)), [psb], [ytb])
                    else:
                        P.op("dve", (lambda: nc.vector.tensor_copy(out=yt[:, 512:1024], in_=ps)), [psb], [ytb])
                r0 = g0 + g * 128
                P.dma(STORE_Q, (lambda: k.stq.dma_start(out=k.y[r0:r0 + 128, :], in_=yt)), [ytb],
                      [P.dbuf("y", r0)], is_out=True)


GELU_NATIVE = True
MASK_SPLIT = False


def phase_ffn_up(k, l, xsrc):
    nc, P = k.nc, k.P
    W, Wb = load_weight_bf16(k, "w_up%d" % l, k.w_up[l], 8, 2 * DFF)
    cw = SP_CW0 if l == 0 else SP_CW1
    cbc = SP_CB0 if l == 0 else SP_CB1
    sm = k.small
    stg = Ring(nc, "fu_stg", 3, [128, 512], F32)
    xbR = Ring(nc, "fu_xb", 2, [128, NCH, 512], BF16, nsub=NCH)
    tR = Ring(nc, "fu_t", 3, [128, 512], F32)
    gR = Ring(nc, "fu_g", 2, [128, 512], F32)
    actR = Ring(nc, "fu_act", 2, [128, NF, 512], BF16, nsub=NF)
    tl = []
    for si, (off, S) in enumerate(zip(k.offs, k.seqs)):
        nt = -(-S // 510)
        wbase = -(-S // nt)
        wbase += wbase % 2
        a = 0
        while a < S:
            w = min(wbase, S - a)
            tl.append((off, S, a, w))
            a += w
    xq = {}

    def load_x(ti):
        off, S, a, w = tl[ti]
        lo, hi = max(a - 1, 0), min(a + w + 1, S)
        dcol = lo - (a - 1)
        xb, xbb = xbR.next()
        if a == 0:
            P.op("pool", (lambda: nc.gpsimd.memset(xb[:, :, 0:1], 0.0)), [], [xbb])
        if a + w == S:
            P.op("pool", (lambda: nc.gpsimd.memset(xb[:, :, w + 1:w + 2], 0.0)), [], [xbb])
        for c in range(NCH):
            st, stb = stg.next()
            P.dma("sp", (lambda: nc.sync.dma_start(out=st[:, 0:hi - lo], in_=xsrc[c, :, off + lo:off + hi])), [], [stb])
            P.op("pool", (lambda: nc.gpsimd.tensor_copy(out=xb[:, c, dcol:dcol + hi - lo], in_=st[:, 0:hi - lo])),
                 [stb], [xbb[c]])
        xq[ti] = (xb, xbb)

    load_x(0)
    for ti, (off, S, a, w) in enumerate(tl):
        if True:
            xb, xbb = xq.pop(ti)
            if ti + 1 < len(tl):
                load_x(ti + 1)
            act, actb = actR.next()
            for f in range(NF):
                gps, gpsb = k.ps.next()
                for c in range(NCH):
                    P.op("pe", (lambda: nc.tensor.matmul(
                        gps[:, 0:w + 2], lhsT=W[:, c, DFF + f * 128:DFF + (f + 1) * 128], rhs=xb[:, c, 0:w + 2],
                        start=(c == 0), stop=(c == NCH - 1))), [Wb, xbb[c]], [gpsb])
                ups, upsb = k.ps.next()
                for c in range(NCH):
                    P.op("pe", (lambda: nc.tensor.matmul(
                        ups[:, 0:w], lhsT=W[:, c, f * 128:(f + 1) * 128], rhs=xb[:, c, 1:w + 1],
                        start=(c == 0), stop=(c == NCH - 1))), [Wb, xbb[c]], [upsb])
                t, tb_ = tR.next()
                P.op("act", (lambda: nc.scalar.activation(
                    out=t[:, 0:w], in_=gps[:, 1:w + 1], func=AF.Identity,
                    scale=sm[:, cw + NF + f:cw + NF + f + 1], bias=sm[:, cbc + f:cbc + f + 1])), [gpsb, k.small_b], [tb_])
                P.op("dve", (lambda: nc.vector.scalar_tensor_tensor(
                    out=t[:, 0:w], in0=gps[:, 0:w], scalar=sm[:, cw + f:cw + f + 1], in1=t[:, 0:w],
                    op0=ALU.mult, op1=ALU.add)), [gpsb, tb_, k.small_b], [tb_])
                P.op("dve", (lambda: nc.vector.scalar_tensor_tensor(
                    out=t[:, 0:w], in0=gps[:, 2:w + 2], scalar=sm[:, cw + 2 * NF + f:cw + 2 * NF + f + 1], in1=t[:, 0:w],
                    op0=ALU.mult, op1=ALU.add)), [gpsb, tb_, k.small_b], [tb_])
                ge, geb = gR.next()
                if GELU_NATIVE:
                    P.op("act", (lambda: nc.scalar.activation(out=ge[:, 0:w], in_=t[:, 0:w], func=AF.Gelu_apprx_tanh)),
                         [tb_], [geb])
                else:
                    P.op("act", (lambda: nc.scalar.activation(out=ge[:, 0:w], in_=t[:, 0:w], func=AF.Square)), [tb_], [geb])
                    P.op("pool", (lambda: nc.gpsimd.tensor_scalar(
                        out=ge[:, 0:w], in0=ge[:, 0:w], scalar1=0.044715, scalar2=1.0, op0=ALU.mult, op1=ALU.add)),
                        [geb], [geb])
                    P.op("pool", (lambda: nc.gpsimd.tensor_tensor(out=ge[:, 0:w], in0=ge[:, 0:w], in1=t[:, 0:w], op=ALU.mult)),
                         [geb, tb_], [geb])
                    P.op("act", (lambda: nc.scalar.activation(out=ge[:, 0:w], in_=ge[:, 0:w], func=AF.Sigmoid,
                                                              scale=1.5957691216057308)), [geb], [geb])
                    P.op("pool", (lambda: nc.gpsimd.tensor_tensor(out=ge[:, 0:w], in0=ge[:, 0:w], in1=t[:, 0:w], op=ALU.mult)),
                         [geb, tb_], [geb])
                P.op("dve", (lambda: nc.vector.tensor_tensor(out=act[:, f, 0:w], in0=ge[:, 0:w], in1=ups[:, 0:w], op=ALU.mult)),
                     [geb, upsb], [actb[f]])
            g0 = off + a
            P.dma(STORE_Q, (lambda: k.stq.dma_start(
                out=k.actT[:, :, g0:g0 + w].rearrange("f p t -> p f t"), in_=act[:, :, 0:w])), [actb], [P.dbuf("actT", g0)])


def phase_hgrn(k, fwd):
    nc, P = k.nc, k.P
    sm = k.small
    Wz, Wzb = load_weight_bf16(k, "hw_z", k.w_c, 8, 1024, col0=(1024 if fwd else 2048), sw=1024)
    if fwd:
        Wq, Wqb = load_weight_bf16(k, "hw_q", k.w_c, 8, 1024, col0=0, sw=1024)
        Wi, Wib = load_weight_bf16(k, "hw_i", k.w_c, 8, 1024, col0=3072, sw=1024)
    if not fwd:
        Wg, Wgb = load_weight_bf16(k, "hw_g", k.w_c, 8, 1024, col0=4096, sw=1024)
    C = k.consts
    tri_inc = C[:, C_TRIL:C_TRIL + 128] if fwd else C[:, C_TRIU:C_TRIU + 128]
    tri_suf = C[:, C_TRIU_S:C_TRIU_S + 128] if fwd else C[:, C_TRIL_S:C_TRIL_S + 128]
    onesv = C[:, C_ONESV:C_ONESV + 128]
    lc = 63 if fwd else 0
    lbcol, lbcolb = sb(nc, "lbcol", [128, 8], F32)
    omlcol, omlcolb = sb(nc, "omlcol", [128, 8], F32)
    spb = SP_LBF if fwd else SP_LBB
    P.op("dve", (lambda: nc.vector.tensor_tensor(out=lbcol, in0=sm[:, spb + 8:spb + 16], in1=sm[:, spb:spb + 8],
                                                 op=ALU.subtract)), [k.small_b], [lbcolb])
    P.op("act", (lambda: nc.scalar.activation(out=lbcol, in_=lbcol, func=AF.Sigmoid)), [lbcolb], [lbcolb])
    P.op("dve", (lambda: nc.vector.tensor_scalar(out=omlcol, in0=lbcol, scalar1=-1.0, scalar2=1.0,
                                                 op0=ALU.mult, op1=ALU.add)), [lbcolb], [omlcolb])
    lbrep, lbrepb = sb(nc, "lbrep", [128, 1024], F32)
    omlrep, omlrepb = sb(nc, "omlrep", [128, 1024], F32)
    r0 = 0 if fwd else 2
    P.dma("sp", (lambda: nc.sync.dma_start(out=lbrep, in_=k.lbrep_d[:, r0 + 1, :])), [], [lbrepb])
    P.dma("sp", (lambda: nc.sync.dma_start(out=omlrep, in_=k.lbrep_d[:, r0, :])), [], [omlrepb])
    P.op("dve", (lambda: nc.vector.tensor_tensor(out=lbrep, in0=lbrep, in1=omlrep, op=ALU.subtract)),
         [lbrepb, omlrepb], [lbrepb])
    P.op("act", (lambda: nc.scalar.activation(out=lbrep, in_=lbrep, func=AF.Sigmoid)), [lbrepb], [lbrepb])
    P.op("dve", (lambda: nc.vector.tensor_scalar(out=omlrep, in0=lbrep, scalar1=-1.0, scalar2=1.0,
                                                 op0=ALU.mult, op1=ALU.add)), [lbrepb], [omlrepb])
    mask4, mask4b = sb(nc, "mask4", [128, 4, 128], BF16)
    for g in range(4):
        P.op("dve", (lambda g=g: nc.vector.tensor_copy(out=mask4[:, g, :], in_=tri_inc)), [k.consts_b], [mask4b])
    S32, S32b = sb(nc, "hS32", [128, 8, 128], F32, nsub=8)
    S16, S16b = sb(nc, "hS16", [128, 8, 128], BF16, nsub=8)
    stg = Ring(nc, "h_stg", 2, [128, 512], F32)
    xbR = Ring(nc, "h_xb", 2, [128, NCH, 512], BF16, nsub=NCH)
    LFr = Ring(nc, "h_LF", 1, [128, 4, 512], F32, nsub=4)
    KDr = Ring(nc, "h_KD", 1, [128, 4, 512], BF16, nsub=4)
    Vr = Ring(nc, "h_V", 1, [128, 4, 512], BF16, nsub=4)
    tA = Ring(nc, "h_tA", 4, [128, 512], F32)
    tB = Ring(nc, "h_tB", 4, [128, 512], F32)
    tC = Ring(nc, "h_tC", 2, [128, 512], F32)
    qbR = Ring(nc, "h_qb", 1, [128, 4, 512], BF16, nsub=4)
    kbR = Ring(nc, "h_kb", 4, [128, 512], BF16)
    atR = Ring(nc, "h_at", 1, [128, 4, 512], BF16, nsub=4)
    ebl = Ring(nc, "h_ebl", 1, [128, 4, 8], F32, nsub=4)
    Or = Ring(nc, "h_O", 1, [128, 4, 512], F32, nsub=4)
    qrR = Ring(nc, "h_qr", 1, [128, 4, 512], BF16, nsub=4)
    KKr = Ring(nc, "h_KK", 1, [128, 4, 512], BF16, nsub=4)
    identb = k.cbf[:, C_IDENT:C_IDENT + 128]
    if not fwd:
        ofr = Ring(nc, "h_of", 1, [128, 4, 512], F32, nsub=4)
        ogr = Ring(nc, "h_og", 1, [128, 4, 512], BF16, nsub=4)

    tiles = []
    for si, (off, S) in enumerate(zip(k.offs, k.seqs)):
        tl = list(range(S // 512))
        if not fwd:
            tl = tl[::-1]
        for n_, j in enumerate(tl):
            tiles.append((off + j * 512, n_ == 0))
    xq = {}

    def load_x(ti):
        g0 = tiles[ti][0]
        xb, xbb = xbR.next()
        for c in range(NCH):
            st, stb = stg.next()
            P.dma("sp", (lambda: nc.sync.dma_start(out=st, in_=k.x2T[c, :, g0:g0 + 512])), [], [stb])
            P.op("pool", (lambda: nc.gpsimd.tensor_copy(out=xb[:, c, :], in_=st)), [stb], [xbb[c]])
        xq[ti] = (xb, xbb)

    load_x(0)
    for ti, (g0, first) in enumerate(tiles):
        if first:
            P.op("pool", (lambda: nc.gpsimd.memset(S32, 0.0)), [], [S32b])
            P.op("pool", (lambda: nc.gpsimd.memset(S16, 0.0)), [], [S16b])
        xb, xbb = xq.pop(ti)
        for hg in range(2):
            hs = slice(hg * 512, (hg + 1) * 512)
            LF, LFb = LFr.next()
            KD, KDb = KDr.next()
            V, Vb = Vr.next()
            KK, KKb = KKr.next()
            G4 = range(4)
            tsl = [slice(g * 128, (g + 1) * 128) for g in G4]
            zp = []
            for g in G4:
                zps, zpsb = k.ps.next()
                for c in range(NCH):
                    P.op("pe", (lambda: nc.tensor.matmul(zps, lhsT=xb[:, c, tsl[g]], rhs=Wz[:, c, hs],
                                                         start=(c == 0), stop=(c == NCH - 1))), [xbb[c], Wzb], [zpsb])
                zp.append((zps, zpsb))
            aa = []
            for g in G4:
                a_, ab_ = tA.next()
                zps, zpsb = zp[g]
                P.op("act", (lambda: nc.scalar.activation(out=a_, in_=zps, func=AF.Sigmoid)), [zpsb], [ab_])
                aa.append((a_, ab_))
            for g in G4:
                a_, ab_ = aa[g]
                P.op("dve", (lambda: nc.vector.tensor_tensor(out=a_, in0=a_, in1=omlrep[:, hs], op=ALU.mult)),
                     [ab_, omlrepb], [ab_])
                P.op("pool", (lambda: nc.gpsimd.tensor_tensor(out=a_, in0=a_, in1=lbrep[:, hs], op=ALU.add)),
                     [ab_, lbrepb], [ab_])
            for g in G4:
                a_, ab_ = aa[g]
                P.op("act", (lambda: nc.scalar.activation(out=LF[:, g, :], in_=a_, func=AF.Ln)), [ab_], [LFb[g]])
            sp_ = []
            for g in G4:
                a_, ab_ = aa[g]
                P.op("pool", (lambda: nc.gpsimd.tensor_scalar(out=KK[:, g, :], in0=a_, scalar1=-1.0, scalar2=1.0,
                                                              op0=ALU.mult, op1=ALU.add)), [ab_], [KKb[g]])
                sps, spsb = k.ps.next()
                P.op("pe", (lambda: nc.tensor.matmul(sps, lhsT=tri_suf, rhs=LF[:, g, :], start=True, stop=True)),
                     [LFb[g], k.consts_b], [spsb])
                sp_.append((sps, spsb))
            bb = []
            for g in G4:
                b_, bb_ = tB.next()
                sps, spsb = sp_[g]
                P.op("act", (lambda: nc.scalar.activation(out=b_, in_=sps, func=AF.Exp)), [spsb], [bb_])
                bb.append((b_, bb_))
            for g in G4:
                a_, ab_ = aa[g]
                b_, bb_ = bb[g]
                P.op("dve", (lambda: nc.vector.tensor_tensor(out=KD[:, g, :], in0=KK[:, g, :], in1=b_, op=ALU.mult)),
                     [KKb[g], bb_], [KDb[g]])
            qb, qbb = qbR.next()
            at, atb = atR.next()
            el, elb = ebl.next()
            H4 = range(4)
            fsl = [slice((hg * 4 + hl) * 128, (hg * 4 + hl + 1) * 128) for hl in H4]
            lsl = [slice(hl * 128, (hl + 1) * 128) for hl in H4]
            btp = []
            for hl in H4:
                bt, btb = k.ps.next()
                for g in G4:
                    P.op("pe", (lambda: nc.tensor.matmul(bt[:, g * 128:(g + 1) * 128], lhsT=LF[:, g, lsl[hl]], rhs=tri_inc,
                                                         start=True, stop=True)), [LFb[g], k.consts_b], [btb])
                btp.append((bt, btb))
            eb, enb = [], []
            for hl in H4:
                bt, btb = btp[hl]
                a_, ab_ = tA.next()
                P.op("act", (lambda: nc.scalar.activation(out=a_, in_=bt, func=AF.Exp)), [btb], [ab_])
                b_, bb_ = tB.next()
                P.op("act", (lambda: nc.scalar.activation(out=b_, in_=bt, func=AF.Exp, scale=-1.0)), [btb], [bb_])
                eb.append((a_, ab_))
                enb.append((b_, bb_))
            if fwd:
                ip = []
                for g in G4:
                    ips, ipsb = k.ps.next()
                    for c in range(NCH):
                        P.op("pe", (lambda: nc.tensor.matmul(ips, lhsT=xb[:, c, tsl[g]], rhs=Wi[:, c, hs],
                                                             start=(c == 0), stop=(c == NCH - 1))), [xbb[c], Wib], [ipsb])
                    ip.append((ips, ipsb))
                for g in G4:
                    ips, ipsb = ip[g]
                    P.op("act", (lambda: nc.scalar.activation(out=V[:, g, :], in_=ips, func=AF.Silu)), [ipsb], [Vb[g]])
                P.dma(STORE_Q, (lambda: k.stq.dma_start(
                    out=k.hv[g0:g0 + 512, hs].rearrange("(g p) f -> p g f", p=128), in_=V)), [Vb], [P.dbuf("hv", hg, g0)])
            else:
                P.dma("sp", (lambda: nc.sync.dma_start(
                    out=V, in_=k.hv[g0:g0 + 512, hs].rearrange("(g p) f -> p g f", p=128))), [], [Vb])
            for hl in H4:
                a_, ab_ = eb[hl]
                P.op("pool", (lambda: nc.gpsimd.tensor_copy(
                    out=el[:, hl, :], in_=a_.rearrange("p (c t) -> p c t", t=64)[:, :, lc])), [ab_], [elb[hl]])
            qr, qrb = qrR.next()
            if not fwd:
                P.dma("sp", (lambda: nc.sync.dma_start(
                    out=qr, in_=k.hq[hg * 4:(hg + 1) * 4, :, g0:g0 + 512].rearrange("c p t -> p c t"))), [], [qrb])
            for hl in H4:
                a_, ab_ = eb[hl]
                if fwd:
                    qps, qpsb = k.ps.next()
                    for c in range(NCH):
                        P.op("pe", (lambda: nc.tensor.matmul(qps, lhsT=Wq[:, c, fsl[hl]], rhs=xb[:, c, :],
                                                             start=(c == 0), stop=(c == NCH - 1))), [xbb[c], Wqb], [qpsb])
                    P.op("act", (lambda: nc.scalar.copy(out=qr[:, hl, :], in_=qps)), [qpsb], [qrb[hl]])
                P.op("dve", (lambda: nc.vector.tensor_tensor(out=qb[:, hl, :], in0=qr[:, hl, :], in1=a_, op=ALU.mult)),
                     [qrb[hl], ab_], [qbb[hl]])
            if fwd:
                P.dma(STORE_Q, (lambda: k.stq.dma_start(
                    out=k.hq[hg * 4:(hg + 1) * 4, :, g0:g0 + 512].rearrange("c p t -> p c t"), in_=qr)),
                    [qrb], [P.dbuf("hq", hg, g0)])
            ktp = []
            for hl in H4:
                kt_, ktb_ = k.ps.next()
                for g in G4:
                    P.op("pe", (lambda: nc.tensor.matmul(kt_[:, g * 128:(g + 1) * 128], lhsT=KK[:, g, lsl[hl]], rhs=identb,
                                                         start=True, stop=True)), [KKb[g], k.cbf_b], [ktb_])
                ktp.append((kt_, ktb_))
            kbs = []
            for hl in H4:
                kt_, ktb_ = ktp[hl]
                b_, bb_ = enb[hl]
                kb, kbb = kbR.next()
                P.op("dve", (lambda: nc.vector.tensor_tensor(out=kb, in0=kt_, in1=b_, op=ALU.mult)),
                     [ktb_, bb_], [kbb])
                kbs.append((kb, kbb))
            for hl in H4:
                kb, kbb = kbs[hl]
                aps, apsb = k.ps.next()
                for g in G4:
                    gs = tsl[g]
                    P.op("pe", (lambda: nc.tensor.matmul(aps[:, gs], lhsT=kb[:, gs], rhs=qb[:, hl, gs],
                                                         start=True, stop=True)), [kbb, qbb[hl]], [apsb])
                P.op("dve", (lambda: nc.vector.tensor_tensor(
                    out=at[:, hl, :], in0=aps, in1=mask4.rearrange("p g t -> p (g t)"), op=ALU.mult)),
                    [apsb, mask4b], [atb[hl]])
            if hg == 1 and ti + 1 < len(tiles):
                load_x(ti + 1)
            O, Ob = Or.next()
            order = list(range(8)) if fwd else list(range(7, -1, -1))
            for ci in order:
                g = ci // 2
                pb = 64 * (ci % 2)
                cs = slice(ci * 64, ci * 64 + 64)
                for hl in H4:
                    h = hg * 4 + hl
                    ls = lsl[hl]
                    ops_, opsb = k.ps.next()
                    P.op("pe", (lambda: nc.tensor.matmul(ops_[:, 0:64], lhsT=S16[:, h, :], rhs=qb[:, hl, cs],
                                                         start=True, stop=False)), [S16b[h], qbb[hl]], [opsb])
                    P.op("pe", (lambda: nc.tensor.matmul(ops_[:, 0:64], lhsT=V[:, g, ls], rhs=at[:, hl, cs],
                                                         start=False, stop=True)), [Vb[g], atb[hl]], [opsb])
                    P.op("act", (lambda: nc.scalar.copy(out=O[:, hl, cs], in_=ops_[:, 0:64])), [opsb], [Ob[hl]])
                    dps, dpsb = k.ps.next()
                    P.op("pe", (lambda: nc.tensor.matmul(dps[:, 0:128], lhsT=KD[pb:pb + 64, g, ls], rhs=V[pb:pb + 64, g, ls],
                                                         start=True, stop=True)), [KDb[g], Vb[g]], [dpsb])
                    P.op("dve", (lambda: nc.vector.scalar_tensor_tensor(
                        out=S32[:, h, :], in0=S32[:, h, :], scalar=el[:, hl, ci:ci + 1], in1=dps[:, 0:128],
                        op0=ALU.mult, op1=ALU.add)), [S32b[h], elb[hl], dpsb], [S32b[h]])
                    P.op("dve", (lambda: nc.vector.tensor_copy(out=S16[:, h, :], in_=S32[:, h, :])), [S32b[h]], [S16b[h]])
            if fwd:
                P.dma(STORE_Q, (lambda: k.stq.dma_start(
                    out=k.ofT[hg * 4:(hg + 1) * 4, :, g0:g0 + 512].rearrange("c p t -> p c t"), in_=O)),
                    [Ob], [P.dbuf("ofT", hg, g0)])
            else:
                of, ofb = ofr.next()
                P.dma("sp", (lambda: nc.sync.dma_start(
                    out=of, in_=k.ofT[hg * 4:(hg + 1) * 4, :, g0:g0 + 512].rearrange("c p t -> p c t"))), [], [ofb])
                og, ogb = ogr.next()
                sqs, gpl, rs_, sgl = [], [], [], []
                for hl in H4:
                    P.op("pool", (lambda: nc.gpsimd.tensor_tensor(out=of[:, hl, :], in0=of[:, hl, :], in1=O[:, hl, :],
                                                                  op=ALU.add)), [ofb[hl], Ob[hl]], [ofb[hl]])
                for hl in H4:
                    a_, ab_ = tA.next()
                    P.op("act", (lambda: nc.scalar.activation(out=a_, in_=of[:, hl, :], func=AF.Square)), [ofb[hl]], [ab_])
                    ms, msb = k.ps.next()
                    P.op("pe", (lambda: nc.tensor.matmul(ms, lhsT=onesv, rhs=a_, start=True, stop=True)),
                         [ab_, k.consts_b], [msb])
                    sqs.append((ms, msb))
                for hl in H4:
                    ms, msb = sqs[hl]
                    b_, bb_ = tB.next()
                    P.op("act", (lambda: nc.scalar.activation(out=b_, in_=ms, func=AF.Ln, bias=k.epsr[:, 0:1],
                                                              scale=1.0)), [msb, k.epsr_b], [bb_])
                    rs_.append((b_, bb_))
                for hl in H4:
                    b_, bb_ = rs_[hl]
                    P.op("act", (lambda: nc.scalar.activation(out=b_, in_=b_, func=AF.Exp, scale=-0.5)), [bb_], [bb_])
                for hl in H4:
                    gps, gpsb = k.ps.next()
                    for c in range(NCH):
                        P.op("pe", (lambda: nc.tensor.matmul(gps, lhsT=Wg[:, c, fsl[hl]], rhs=xb[:, c, :],
                                                             start=(c == 0), stop=(c == NCH - 1))), [xbb[c], Wgb], [gpsb])
                    gpl.append((gps, gpsb))
                for hl in H4:
                    gps, gpsb = gpl[hl]
                    c_, cb_ = tC.next()
                    P.op("act", (lambda: nc.scalar.activation(out=c_, in_=gps, func=AF.Silu)), [gpsb], [cb_])
                    h = hg * 4 + hl
                    b_, bb_ = rs_[hl]
                    P.op("dve", (lambda: nc.vector.scalar_tensor_tensor(
                        out=b_, in0=of[:, hl, :], scalar=sm[:, SP_GNC + h:SP_GNC + h + 1], in1=b_,
                        op0=ALU.mult, op1=ALU.mult)), [ofb[hl], bb_, k.small_b], [bb_])
                    P.op("dve", (lambda: nc.vector.tensor_tensor(out=og[:, hl, :], in0=b_, in1=c_, op=ALU.mult)),
                         [bb_, cb_], [ogb[hl]])
                P.dma(STORE_Q, (lambda: k.stq.dma_start(
                    out=k.ycat[hg * 4:(hg + 1) * 4, :, g0:g0 + 512].rearrange("c p t -> p c t"), in_=og)),
                    [ogb], [P.dbuf("og", hg, g0)])


def host_shared(inp, smax):
    d = {}
    f32 = lambda a: np.ascontiguousarray(np.asarray(a, np.float32))
    d["w_ab"] = f32(np.asarray(inp["w_in_ab"])[0][:, ab_columns()])
    d["w_oab"] = f32(np.asarray(inp["w_out_ab"])[0])
    d["w_c"] = f32(np.asarray(inp["w_in_c"])[0])
    d["w_oc"] = f32(np.asarray(inp["w_out_c"])[0])
    for l in range(2):
        d["w_up%d" % l] = f32(np.asarray(inp["ffn_w_up"])[l])
        d["w_dn%d" % l] = f32(np.asarray(inp["ffn_w_down"])[l])
    ca, sa, cb, sbb = rope_tables(smax)
    d["t_ca"], d["t_sa"], d["t_cb"], d["t_sb"] = ca, sa, cb, sbb
    d["masks"] = dil_masks()
    sp = np.zeros((128, NSMALL), np.float32)
    for l in range(2):
        sp[:, SP_LNMG0 + 32 * l:SP_LNMG0 + 32 * l + 8] = col128(np.asarray(inp["ln_mix_g"])[l], 8)
        sp[:, SP_LNMB0 + 32 * l:SP_LNMB0 + 32 * l + 8] = col128(np.asarray(inp["ln_mix_b"])[l], 8)
        sp[:, SP_LNFG0 + 32 * l:SP_LNFG0 + 32 * l + 8] = col128(np.asarray(inp["ln_ffn_g"])[l], 8)
        sp[:, SP_LNFB0 + 32 * l:SP_LNFB0 + 32 * l + 8] = col128(np.asarray(inp["ln_ffn_b"])[l], 8)
    pb = rope_perm_B()
    qn = np.asarray(inp["qn_ab"], np.float32)[0]
    kn = np.asarray(inp["kn_ab"], np.float32)[0]
    sp[:, SP_QN] = np.concatenate([qn, qn])
    sp[:, SP_QNP] = np.concatenate([qn[pb], qn[pb]])
    sp[:, SP_KN] = np.concatenate([kn, kn])
    sp[:, SP_KNP] = np.concatenate([kn[pb], kn[pb]])
    sp[:, SP_GNC:SP_GNC + 8] = col128(np.asarray(inp["gn_c"])[0], 8)
    for l, (cw, cbias) in enumerate(((SP_CW0, SP_CB0), (SP_CW1, SP_CB1))):
        w = np.asarray(inp["ffn_conv_w"], np.float32)[l]
        for j in range(3):
            sp[:, cw + j * NF:cw + (j + 1) * NF] = col128(w[j], NF)
        sp[:, cbias:cbias + NF] = col128(np.asarray(inp["ffn_conv_b"])[l], NF)
    lbf = np.asarray(inp["lb_fwd"], np.float32)
    lbb = np.asarray(inp["lb_bwd"], np.float32)
    for l in range(2):
        sp[:, SP_LBF + 8 * l:SP_LBF + 8 * l + 8] = col128(lbf[l], 8)
        sp[:, SP_LBB + 8 * l:SP_LBB + 8 * l + 8] = col128(lbb[l], 8)
    d["smallp"] = sp
    rep = np.zeros((128, 4, 1024), np.float32)
    rep[:, 0, :] = lbf[0][None, :]
    rep[:, 1, :] = lbf[1][None, :]
    rep[:, 2, :] = lbb[0][None, :]
    rep[:, 3, :] = lbb[1][None, :]
    d["lbrep"] = rep
    d["consts"] = host_consts()
    return d


ALL_PHASES = ("p1", "attA", "attB", "p3", "f0a", "f0b", "h1", "h2", "p6", "f1a", "f1b")


def kernel(**inputs):
    seqs = [2048, 8192]
    xp = np.asarray(inputs["x_prompt"], np.float32)
    xs = np.asarray(inputs["x_sample"], np.float32)
    shared = host_shared(inputs, max(seqs))
    nc = build2(seqs, ALL_PHASES)
    in_maps = []
    for c in range(8):
        m = dict(shared)
        m["x"] = np.ascontiguousarray(np.concatenate([xp[c], xs[c]], 0))
        in_maps.append(m)
    res = run_bass_kernel_spmd(nc, in_maps, core_ids=list(range(8)))
    yp = np.stack([res.results[c]["y"][0:2048] for c in range(8)], 0)
    ys = np.stack([res.results[c]["y"][2048:] for c in range(8)], 0)
    return (yp.astype(np.float32), ys.astype(np.float32))
```

```python
import math
import contextlib
import numpy as np
import ml_dtypes
import concourse.bass as bass
import concourse.mybir as mybir
from concourse.bass_utils import run_bass_kernel_spmd

F32 = mybir.dt.float32
BF16 = mybir.dt.bfloat16
AF = mybir.ActivationFunctionType
ALU = mybir.AluOpType

D = 1024
NCH = 8
DFF = 2816
NF = 22
ALPHA = 4.0 ** 0.25
LN_EPS = 1e-5
RMS_EPS = 1e-6
SAME_ENG_SYNC = True
NDMASEM = 24
STORE_Q = "pool"


class Buf:
    __slots__ = ("name", "w", "r")

    def __init__(self, name=""):
        self.name = name
        self.w = None
        self.r = []


def _flat(xs):
    out = []
    for x in xs:
        if isinstance(x, (list, tuple)):
            out.extend(_flat(x))
        else:
            out.append(x)
    return out


class Prog:
    def __init__(self, nc, marks=None):
        self.nc = nc
        self.eng = {"pe": nc.tensor, "act": nc.scalar, "dve": nc.vector, "pool": nc.gpsimd, "sp": nc.sync}
        self.marks = marks
        self.marked = []
        self.meta = []
        self.real = []
        self.ev = []
        self.dma_hist = {}
        self.dbufs = {}
        self.out_dmas = []
        if marks is not None:
            self.sems = {e: nc.alloc_semaphore("s_" + e) for e in self.eng}
            self.cnt = {e: 0 for e in self.eng}
            self.dsem = {}
            self.dcnt = {}
            self.waited = {e: {} for e in self.eng}

    def dbuf(self, *key):
        b = self.dbufs.get(key)
        if b is None:
            b = Buf(str(key))
            self.dbufs[key] = b
        return b

    def _add(self, eng, fn, reads, writes, is_dma, extra=()):
        reads = _flat(reads)
        writes = _flat(writes)
        idx = len(self.meta)
        deps = set(extra)
        for b in reads:
            if b.w is not None:
                deps.add(b.w)
        for b in writes:
            if b.w is not None:
                deps.add(b.w)
            deps.update(b.r)
        for b in writes:
            b.w = idx
            b.r = []
        for b in reads:
            if b.w != idx:
                b.r.append(idx)
        self.meta.append((eng, is_dma))
        self.real.append(fn is not None)
        fdeps = []
        for d in deps:
            de, ddma = self.meta[d]
            if (not ddma) and (not is_dma) and de == eng and (eng == "pe" or not SAME_ENG_SYNC):
                continue
            fdeps.append(d)
        if self.marks is None:
            self.marked.append(False)
            for d in fdeps:
                self.marked[d] = True
            return idx
        e = self.eng[eng]
        need = {}
        w = self.waited[eng]
        for d in fdeps:
            s, v = self.ev[d]
            key = id(s)
            if w.get(key, 0) >= v:
                continue
            if key not in need or need[key][1] < v:
                need[key] = (s, v)
        for key, (s, v) in need.items():
            e.wait_ge(s, v)
            w[key] = v
        if fn is None:
            self.ev.append(None)
            return idx
        inst = fn()
        if is_dma:
            if eng not in self.dsem:
                self.dsem[eng] = [self.nc.alloc_semaphore("d_%s_%d" % (eng, j)) for j in range(NDMASEM)]
                self.dcnt[eng] = 0
            j = self.dcnt[eng]
            self.dcnt[eng] += 1
            s = self.dsem[eng][j % NDMASEM]
            inst.then_inc(s, 16)
            self.ev.append((s, 16 * (j // NDMASEM + 1)))
        elif self.marks[idx]:
            self.cnt[eng] += 1
            inst.then_inc(self.sems[eng], 1)
            self.ev.append((self.sems[eng], self.cnt[eng]))
        else:
            self.ev.append(None)
        return idx

    def op(self, eng, fn, reads=(), writes=()):
        return self._add(eng, fn, reads, writes, False)

    def dma(self, q, fn, reads=(), writes=(), is_out=False):
        h = self.dma_hist.setdefault(q, [])
        extra = (h[-NDMASEM],) if len(h) >= NDMASEM else ()
        idx = self._add(q, fn, reads, writes, True, extra)
        h.append(idx)
        if is_out:
            self.out_dmas.append(idx)
        return idx

    def barrier(self):
        last = {}
        dmas = []
        for i, (e, isd) in enumerate(self.meta):
            if not self.real[i]:
                continue
            if isd:
                dmas.append(i)
            else:
                last[e] = i
        start = getattr(self, "_bar_from", 0)
        ex = tuple(last.values()) + tuple(d for d in dmas if d >= start)
        for e in ("pe", "act", "dve", "pool", "sp"):
            self._add(e, None, (), (), False, ex)
        self._bar_from = len(self.meta)

    def finish(self):
        self._add("sp", None, (), (), False, tuple(self.out_dmas))
        return self.marked


ALLOC = {"stack": None}


def _salloc(nc, name, shape, dtype):
    ALLOC["n"] = ALLOC.get("n", 0) + 1
    name = "%s_%d" % (name, ALLOC["n"])
    st = ALLOC["stack"]
    if st is None:
        return nc.alloc_sbuf_tensor(name, shape, dtype)
    return st.enter_context(nc.sbuf_tensor(name, shape, dtype))


class Ring:
    def __init__(self, nc, name, n, shape, dtype, psum=False, nsub=0):
        self.aps = []
        self.bufs = []
        for i in range(n):
            if psum:
                ALLOC["n"] = ALLOC.get("n", 0) + 1
                pname = "rp_%s%d_%d" % (name, i, ALLOC["n"])
                st = ALLOC["stack"]
                t = nc.alloc_psum_tensor(pname, shape, dtype) if st is None else st.enter_context(
                    nc.psum_tensor(pname, shape, dtype))
            else:
                t = _salloc(nc, "r_%s%d" % (name, i), shape, dtype)
            self.aps.append(t.ap())
            self.bufs.append([Buf("%s%d_%d" % (name, i, j)) for j in range(nsub)] if nsub else Buf("%s%d" % (name, i)))
        self.i = 0
        self.n = n

    def next(self):
        k = self.i % self.n
        self.i += 1
        return self.aps[k], self.bufs[k]


def sb(nc, name, shape, dtype, nsub=0):
    return _salloc(nc, "sb_" + name, shape, dtype).ap(), ([Buf(name + str(j)) for j in range(nsub)] if nsub else Buf(name))


def rope_perm_A():
    p = np.arange(64)
    p[0:8] = np.arange(8, 16)
    p[8:16] = np.arange(0, 8)
    return p


def rope_perm_B():
    p = np.arange(64)
    p[0:16] = np.arange(16, 32)
    p[16:32] = np.arange(0, 16)
    p[32:48] = np.arange(48, 64)
    p[48:64] = np.arange(32, 48)
    return p


def rope_tables(smax):
    t = np.arange(smax, dtype=np.float32)
    fa = (np.float32(500000.0) ** (-(np.arange(0, 16, 2, dtype=np.float32) / np.float32(16)))).astype(np.float32)
    ang = t[None, :] * fa[:, None]
    ca = np.ones((64, smax), np.float32)
    sa = np.zeros((64, smax), np.float32)
    ca[0:8] = np.cos(ang)
    ca[8:16] = np.cos(ang)
    sa[0:8] = -np.sin(ang)
    sa[8:16] = np.sin(ang)
    fb = (np.float32(10000.0) ** (-(np.arange(0, 32, 2, dtype=np.float32) / np.float32(32)))).astype(np.float32)
    row = np.floor(t / 64).astype(np.float32)
    col = (t - row * 64).astype(np.float32)
    ar = row[None, :] * fb[:, None]
    ac = col[None, :] * fb[:, None]
    cb = np.zeros((64, smax), np.float32)
    sbb = np.zeros((64, smax), np.float32)
    cb[0:16] = np.cos(ar)
    cb[16:32] = np.cos(ar)
    sbb[0:16] = -np.sin(ar)
    sbb[16:32] = np.sin(ar)
    cb[32:48] = np.cos(ac)
    cb[48:64] = np.cos(ac)
    sbb[32:48] = -np.sin(ac)
    sbb[48:64] = np.sin(ac)
    tile2 = lambda a: np.ascontiguousarray(np.concatenate([a, a], 0))
    return tile2(ca), tile2(sa), tile2(cb), tile2(sbb)


def dil_masks():
    m = np.zeros((20, 128, 512), np.float32)
    kk = np.arange(128)[:, None]
    qq = np.arange(512)[None, :]
    for i in range(20):
        dlt = (-1024 + 128 * i) + kk - qq
        a = np.abs(dlt)
        c = (a <= 64).astype(np.float32)
        c += ((dlt % 4 == 0) & (a <= 256)).astype(np.float32)
        c += ((dlt % 16 == 0) & (a <= 1024)).astype(np.float32)
        m[i] = c
    return np.ascontiguousarray(m.transpose(1, 0, 2)).astype(ml_dtypes.bfloat16)


def col128(v, nchunk):
    return np.ascontiguousarray(np.asarray(v, np.float32).reshape(nchunk, 128).T)


def ab_columns():
    A_W = 512
    qa = [h * 64 + np.arange(64) for h in range(8)]
    ka = [A_W + h * 64 + np.arange(64) for h in range(8)]
    o3 = 3 * A_W
    qb = [o3 + h * 64 + np.arange(64) for h in range(8)]
    o4 = o3 + 512
    kb = [o4 + h * 64 + np.arange(64) for h in range(2)]
    cols = []
    for c in range(4):
        cols += [qa[2 * c], qa[2 * c + 1]]
    for c in range(4):
        cols += [ka[2 * c], ka[2 * c + 1]]
    for c in range(4):
        cols += [qb[c], qb[4 + c]]
    cols += [kb[0], kb[1]]
    cols += [2 * A_W + np.arange(512)]
    o5 = o4 + 128
    cols += [o5 + np.arange(128)]
    return np.concatenate(cols)


NAB = 2304


class K:
    pass


def build(seqs, phases, debug=(), marks=None, feed=()):
    nc = bass.Bass("TRN2", target_bir_lowering=False)
    P = Prog(nc, marks)
    T = sum(seqs)
    SMAX = max(seqs)
    offs = [sum(seqs[:i]) for i in range(len(seqs))]
    k = K()
    k.nc, k.P, k.T, k.seqs, k.offs = nc, P, T, seqs, offs
    k.stq = P.eng[STORE_Q]

    def din(name, shape, dt=F32):
        return nc.dram_tensor(name, list(shape), dt, kind="ExternalInput").ap()

    def dscr(name, shape, dt):
        kind = "ExternalOutput" if name in debug else ("ExternalInput" if name in feed else "Internal")
        return nc.dram_tensor(name, list(shape), dt, kind=kind).ap()

    k.x = din("x", [T, D])
    k.w_ab = din("w_ab", [D, NAB])
    k.w_oab = din("w_oab", [D, D])
    k.w_c = din("w_c", [D, 5120])
    k.w_oc = din("w_oc", [D, D])
    k.w_up = [din("w_up%d" % l, [D, 2 * DFF]) for l in range(2)]
    k.w_dn = [din("w_dn%d" % l, [DFF, D]) for l in range(2)]
    k.t_ca = din("t_ca", [128, SMAX])
    k.t_sa = din("t_sa", [128, SMAX])
    k.t_cb = din("t_cb", [128, SMAX])
    k.t_sb = din("t_sb", [128, SMAX])
    k.masks_d = din("masks", [128, 20, 512], BF16)
    k.smallp = din("smallp", [128, NSMALL])
    k.lbrep_d = din("lbrep", [128, 4, 1024])
    k.consts_d = din("consts", [128, NCONST])
    k.y = nc.dram_tensor("y", [T, D], F32, kind="ExternalOutput").ap()
    k.xT = dscr("xT", [NCH, 128, T], F32)
    k.qaT = dscr("qaT", [4, 128, T], BF16)
    k.kaT = dscr("kaT", [4, 128, T], BF16)
    k.qbT = dscr("qbT", [4, 128, T], BF16)
    k.kbT = dscr("kbT", [1, 128, T], BF16)
    k.va = dscr("va", [T, 8 * 65], BF16)
    k.vb = dscr("vb", [T, 2 * 65], BF16)
    k.ycat = dscr("ycat", [NCH, 128, T], BF16)
    k.den = dscr("den", [16, T], F32)
    k.hq = dscr("hq", [NCH, 128, T], BF16)
    k.hv = dscr("hv", [T, 1024], BF16)
    k.x1T = dscr("x1T", [NCH, 128, T], F32)
    k.actT = dscr("actT", [NF, 128, T], BF16)
    k.x2T = dscr("x2T", [NCH, 128, T], F32)
    k.ofT = dscr("ofT", [NCH, 128, T], F32)
    k.x3T = dscr("x3T", [NCH, 128, T], F32)


    setup_consts(k)

    def run_phase(fn, *a, **kw):
        with contextlib.ExitStack() as st:
            ALLOC["stack"] = st
            if fn is not phase_att:
                k.ps = Ring(nc, "ps", 8, [128, 512], F32, psum=True)
            fn(*a, **kw)
            P.barrier()
        ALLOC["stack"] = None

    for name, fn, a, kw in phase_table(k):
        if name in phases:
            run_phase(fn, *a, **kw)
    m = P.finish()
    return nc, m


def phase_table(k):
    return [
        ("p1", phase_p1, (k,), {}),
        ("attA", phase_att, (k, "A"), {}),
        ("attB", phase_att, (k, "B"), {}),
        ("p3", phase_proj_ln, (k,), dict(src=k.ycat, resid=k.xT, dst=k.x1T, w_dram=k.w_oab, nk=8, lng=SP_LNMG0, lnb=SP_LNMB0, tag="p3", den=k.den)),
        ("f0a", phase_ffn_up, (k, 0, k.x1T), {}),
        ("f0b", phase_proj_ln, (k,), dict(src=k.actT, resid=k.x1T, dst=k.x2T, w_dram=k.w_dn[0], nk=NF, lng=SP_LNFG0, lnb=SP_LNFB0, tag="f0b")),
        ("h1", phase_hgrn, (k, True), {}),
        ("h2", phase_hgrn, (k, False), {}),
        ("p6", phase_proj_ln, (k,), dict(src=k.ycat, resid=k.x2T, dst=k.x3T, w_dram=k.w_oc, nk=8, lng=SP_LNMG1, lnb=SP_LNMB1, tag="p6")),
        ("f1a", phase_ffn_up, (k, 1, k.x3T), {}),
        ("f1b", phase_proj_ln, (k,), dict(src=k.actT, resid=k.x3T, dst=None, w_dram=k.w_dn[1], nk=NF, lng=SP_LNFG1, lnb=SP_LNFB1, tag="f1b")),
    ]


def build2(seqs, phases, debug=(), feed=()):
    _, marks = build(seqs, phases, debug, None, feed)
    nc, _ = build(seqs, phases, debug, marks, feed)
    return nc


SP_LNMG0, SP_LNMB0, SP_LNFG0, SP_LNFB0 = 0, 8, 16, 24
SP_LNMG1, SP_LNMB1, SP_LNFG1, SP_LNFB1 = 32, 40, 48, 56
SP_QN, SP_QNP, SP_KN, SP_KNP = 64, 65, 66, 67
SP_GNC = 68
SP_CW0 = 76
SP_CB0 = SP_CW0 + 66
SP_CW1 = SP_CB0 + 22
SP_CB1 = SP_CW1 + 66
SP_LBF = SP_CB1 + 22
SP_LBB = SP_LBF + 16
NSMALL = SP_LBB + 16

C_IDENT = 0
C_ONESBLK = 128
C_ONES = 256
C_TRIL = 384
C_TRIU_S = 512
C_TRIU = 640
C_TRIL_S = 768
C_ONESV = 896
C_PERMA = 1024
C_PERMB = 1152
NCONST = 1280


def host_consts():
    c = np.zeros((128, NCONST), np.float32)
    c[:, C_IDENT:C_IDENT + 128] = np.eye(128, dtype=np.float32)
    blk = np.zeros((128, 128), np.float32)
    blk[0:64, 0:64] = 1.0 / 64
    blk[64:128, 64:128] = 1.0 / 64
    c[:, C_ONESBLK:C_ONESBLK + 128] = blk
    c[:, C_ONES:C_ONES + 128] = 1.0
    c[:, C_ONESV:C_ONESV + 128] = 1.0 / 128
    for col, pm in ((C_PERMA, rope_perm_A()), (C_PERMB, rope_perm_B())):
        for m in range(128):
            c[64 * (m // 64) + pm[m % 64], col + m] = 1.0
    s = np.arange(128)[:, None]
    t = np.arange(128)[None, :]
    same = (s // 64) == (t // 64)
    c[:, C_TRIL:C_TRIL + 128] = (same & (s <= t))
    c[:, C_TRIU_S:C_TRIU_S + 128] = (same & (s > t))
    c[:, C_TRIU:C_TRIU + 128] = (same & (s >= t))
    c[:, C_TRIL_S:C_TRIL_S + 128] = (same & (s < t))
    return c


def setup_consts(k):
    nc, P = k.nc, k.P
    k.consts, k.consts_b = sb(nc, "consts", [128, NCONST], F32)
    k.small, k.small_b = sb(nc, "small", [128, NSMALL], F32)
    P.dma("sp", lambda: nc.sync.dma_start(out=k.consts, in_=k.consts_d), [], [k.consts_b])
    P.dma("sp", lambda: nc.sync.dma_start(out=k.small, in_=k.smallp), [], [k.small_b])
    k.cbf, k.cbf_b = sb(nc, "cbf", [128, NCONST], BF16)
    P.op("dve", lambda: nc.vector.tensor_copy(out=k.cbf, in_=k.consts), [k.consts_b], [k.cbf_b])
    k.onesd, k.onesd_b = sb(nc, "onesd", [128, 128], BF16)
    P.op("dve", lambda: nc.vector.memset(k.onesd, 1.0 / 1024), [], [k.onesd_b])
    k.epsr, k.epsr_b = sb(nc, "epsr", [128, 1], F32)
    P.op("dve", lambda: nc.vector.memset(k.epsr, RMS_EPS), [], [k.epsr_b])
    k.epsl, k.epsl_b = sb(nc, "epsl", [128, 1], F32)
    P.op("dve", lambda: nc.vector.memset(k.epsl, LN_EPS), [], [k.epsl_b])


def load_weight_bf16(k, name, w_dram, nk, ncols, col0=0, sw=2048):
    nc, P = k.nc, k.P
    nchunk = nk * (-(-ncols // sw))
    wt, wb = sb(nc, name, [128, nk, ncols], BF16, nsub=nchunk)
    wstage = Ring(nc, name + "_stg", 2, [128, sw], F32)
    i = 0
    for kk in range(nk):
        for c0 in range(0, ncols, sw):
            cw = min(sw, ncols - c0)
            st, stb = wstage.next()
            P.dma("sp", (lambda st=st, kk=kk, c0=c0, cw=cw: nc.sync.dma_start(
                out=st[:, 0:cw], in_=w_dram[kk * 128:(kk + 1) * 128, col0 + c0:col0 + c0 + cw])), [], [stb])
            if i % 2 == 0:
                P.op("act", (lambda st=st, kk=kk, c0=c0, cw=cw: nc.scalar.copy(
                    out=wt[:, kk, c0:c0 + cw], in_=st[:, 0:cw])), [stb], [wb[i]])
            else:
                P.op("dve", (lambda st=st, kk=kk, c0=c0, cw=cw: nc.vector.tensor_copy(
                    out=wt[:, kk, c0:c0 + cw], in_=st[:, 0:cw])), [stb], [wb[i]])
            i += 1
    return wt, wb


def tiles512(k):
    for si, (off, S) in enumerate(zip(k.offs, k.seqs)):
        for j in range(S // 512):
            yield si, off, S, j * 512, off + j * 512


def phase_p1(k):
    nc, P = k.nc, k.P
    W, Wb = load_weight_bf16(k, "w_ab_sb", k.w_ab, 8, NAB)
    hbR = Ring(nc, "p1hb", 3, [128, 512], BF16)
    permA = k.cbf[:, C_PERMA:C_PERMA + 128]
    permB = k.cbf[:, C_PERMB:C_PERMB + 128]
    xtok = Ring(nc, "xtok", 2, [128, 4, D], F32)
    xT32 = Ring(nc, "xT32", 1, [128, NCH, 512], F32, nsub=NCH)
    xTb = Ring(nc, "xTb", 2, [128, NCH, 512], BF16, nsub=NCH)
    tab = Ring(nc, "ropetab", 1, [128, 4, 512], F32)
    t1r = Ring(nc, "p1t1", 2, [128, 512], F32)
    t2r = Ring(nc, "p1t2", 2, [128, 512], F32)
    sqr = Ring(nc, "p1sq", 2, [128, 512], F32)
    rsr = Ring(nc, "p1rs", 2, [128, 512], F32)
    qkout = Ring(nc, "p1qk", 1, [128, 13, 512], BF16, nsub=13)
    vaug = Ring(nc, "p1va", 2, [128, 4, 8 * 65], BF16)
    vbug = Ring(nc, "p1vb", 2, [128, 4, 2 * 65], BF16)
    for r in (vaug, vbug):
        for ap, b in zip(r.aps, r.bufs):
            P.op("pool", (lambda ap=ap: nc.gpsimd.memset(ap, 1.0)), [], [b])
    ident = k.consts[:, C_IDENT:C_IDENT + 128]
    onesblk = k.consts[:, C_ONESBLK:C_ONESBLK + 128]
    sm = k.small
    for si, off, S, p0, g0 in tiles512(k):
        xt, xtb = xtok.next()
        P.dma("sp", (lambda xt=xt, g0=g0: nc.sync.dma_start(
            out=xt, in_=k.x[g0:g0 + 512, :].rearrange("(g p) d -> p g d", p=128))), [], [xtb])
        tb, tbb = tab.next()
        for i, src in enumerate((k.t_ca, k.t_sa, k.t_cb, k.t_sb)):
            P.dma("sp", (lambda tb=tb, i=i, src=src, p0=p0: nc.sync.dma_start(
                out=tb[:, i, :], in_=src[:, p0:p0 + 512])), [], [tbb])
        x32, x32b = xT32.next()
        xb, xbb = xTb.next()
        for c in range(NCH):
            ps, psb = k.ps.next()
            for g in range(4):
                P.op("pe", (lambda ps=ps, xt=xt, g=g, c=c: nc.tensor.transpose(
                    out=ps[:, g * 128:(g + 1) * 128], in_=xt[:, g, c * 128:(c + 1) * 128], identity=ident)),
                    [xtb, k.consts_b], [psb])
            P.op("act", (lambda ps=ps, x32=x32, c=c: nc.scalar.copy(out=x32[:, c, :], in_=ps)), [psb], [x32b[c]])
            P.op("dve", (lambda x32=x32, xb=xb, c=c: nc.vector.tensor_copy(out=xb[:, c, :], in_=x32[:, c, :])), [x32b[c]], [xbb[c]])
        P.dma(STORE_Q, (lambda x32=x32, g0=g0: k.stq.dma_start(
            out=k.xT[:, :, g0:g0 + 512].rearrange("c p t -> p c t"), in_=x32)),
            [x32b], [P.dbuf("xT", g0)])

        def proj(chunk):
            ps, psb = k.ps.next()
            for c in range(NCH):
                P.op("pe", (lambda ps=ps, c=c, chunk=chunk: nc.tensor.matmul(
                    ps, lhsT=W[:, c, chunk * 128:(chunk + 1) * 128], rhs=xb[:, c, :],
                    start=(c == 0), stop=(c == NCH - 1))), [Wb, xbb[c]], [psb])
            return ps, psb

        def rot(ps, psb, permM):
            hb, hbb = hbR.next()
            P.op("act", (lambda: nc.scalar.copy(out=hb, in_=ps)), [psb], [hbb])
            pp, ppb = k.ps.next()
            P.op("pe", (lambda: nc.tensor.matmul(pp, lhsT=permM, rhs=hb, start=True, stop=True)), [hbb, k.cbf_b], [ppb])
            return pp, ppb, hbb

        qo, qob = qkout.next()
        for which, base in ((0, 0), (1, 4)):
            for c in range(4):
                ps, psb = proj(base + c)
                pp, ppb, hbb = rot(ps, psb, permA)
                t1, t1b = t1r.next()
                t2, t2b = t2r.next()
                P.op("dve", (lambda t1=t1, ps=ps: nc.vector.tensor_tensor(
                    out=t1, in0=ps, in1=tb[:, 0, :], op=ALU.mult)), [psb, tbb, hbb], [t1b])
                P.op("dve", (lambda t2=t2, pp=pp: nc.vector.tensor_tensor(
                    out=t2, in0=pp, in1=tb[:, 1, :], op=ALU.mult)), [ppb, tbb], [t2b])
                slot = which * 4 + c
                P.op("pool", (lambda t1=t1, t2=t2, slot=slot: nc.gpsimd.tensor_tensor(
                    out=qo[:, slot, :], in0=t1, in1=t2, op=ALU.add)), [t1b, t2b], [qob[slot]])
        for which, base, n, gcol in ((0, 8, 4, SP_QN), (1, 12, 1, SP_KN)):
            for c in range(n):
                ps, psb = proj(base + c)
                pp, ppb, hbb = rot(ps, psb, permB)
                sq, sqb = sqr.next()
                P.op("act", (lambda sq=sq, ps=ps: nc.scalar.activation(out=sq, in_=ps, func=AF.Square)), [psb], [sqb])
                ms, msb = k.ps.next()
                P.op("pe", (lambda ms=ms, sq=sq: nc.tensor.matmul(ms, lhsT=onesblk, rhs=sq, start=True, stop=True)),
                     [sqb, k.consts_b], [msb])
                rs, rsb = rsr.next()
                P.op("act", (lambda rs=rs, ms=ms: nc.scalar.activation(
                    out=rs, in_=ms, func=AF.Ln, bias=k.epsr[:, 0:1], scale=1.0)), [msb, k.epsr_b], [rsb])
                P.op("act", (lambda rs=rs: nc.scalar.activation(out=rs, in_=rs, func=AF.Exp, scale=-0.5)), [rsb], [rsb])
                t1, t1b = t1r.next()
                t2, t2b = t2r.next()
                P.op("dve", (lambda t1=t1, ps=ps, gcol=gcol: nc.vector.scalar_tensor_tensor(
                    out=t1, in0=ps, scalar=sm[:, gcol:gcol + 1], in1=tb[:, 2, :], op0=ALU.mult, op1=ALU.mult)),
                    [psb, tbb, k.small_b, sqb, hbb], [t1b])
                P.op("dve", (lambda t2=t2, pp=pp, gcol=gcol: nc.vector.scalar_tensor_tensor(
                    out=t2, in0=pp, scalar=sm[:, gcol + 1:gcol + 2], in1=tb[:, 3, :], op0=ALU.mult, op1=ALU.mult)),
                    [ppb, tbb, k.small_b], [t2b])
                P.op("pool", (lambda t1=t1, t2=t2: nc.gpsimd.tensor_tensor(
                    out=t1, in0=t1, in1=t2, op=ALU.add)), [t1b, t2b], [t1b])
                slot = 8 + c if which == 0 else 12
                P.op("dve", (lambda t1=t1, rs=rs, slot=slot: nc.vector.tensor_tensor(
                    out=qo[:, slot, :], in0=t1, in1=rs, op=ALU.mult)), [t1b, rsb], [qob[slot]])
        for dst, s0, n in ((k.qaT, 0, 4), (k.kaT, 4, 4), (k.qbT, 8, 4), (k.kbT, 12, 1)):
            P.dma(STORE_Q, (lambda dst=dst, s0=s0, n=n, g0=g0: k.stq.dma_start(
                out=dst[:, :, g0:g0 + 512].rearrange("c p t -> p c t"), in_=qo[:, s0:s0 + n, :])),
                [qob], [P.dbuf("qk", id(dst), g0)])
        va, vab = vaug.next()
        vb, vbb = vbug.next()
        for g in range(4):
            ps, psb = k.ps.next()
            for c in range(NCH):
                P.op("pe", (lambda ps=ps, c=c, g=g: nc.tensor.matmul(
                    ps, lhsT=xb[:, c, g * 128:(g + 1) * 128], rhs=W[:, c, 1664:2176],
                    start=(c == 0), stop=(c == NCH - 1))), [Wb, xbb[c]], [psb])
            P.op("act", (lambda ps=ps, va=va, g=g: nc.scalar.copy(
                out=va[:, g, :].rearrange("p (h e) -> p h e", e=65)[:, :, 0:64],
                in_=ps.rearrange("p (h e) -> p h e", e=64))), [psb], [vab])
            ps2, ps2b = k.ps.next()
            for c in range(NCH):
                P.op("pe", (lambda ps2=ps2, c=c, g=g: nc.tensor.matmul(
                    ps2[:, 0:128], lhsT=xb[:, c, g * 128:(g + 1) * 128], rhs=W[:, c, 2176:2304],
                    start=(c == 0), stop=(c == NCH - 1))), [Wb, xbb[c]], [ps2b])
            P.op("act", (lambda ps2=ps2, vb=vb, g=g: nc.scalar.copy(
                out=vb[:, g, :].rearrange("p (h e) -> p h e", e=65)[:, :, 0:64],
                in_=ps2[:, 0:128].rearrange("p (h e) -> p h e", e=64))), [ps2b], [vbb])
        P.dma(STORE_Q, (lambda va=va, g0=g0: k.stq.dma_start(
            out=k.va[g0:g0 + 512, :].rearrange("(g p) f -> p g f", p=128), in_=va)), [vab], [P.dbuf("va", g0)])
        P.dma(STORE_Q, (lambda vb=vb, g0=g0: k.stq.dma_start(
            out=k.vb[g0:g0 + 512, :].rearrange("(g p) f -> p g f", p=128), in_=vb)), [vbb], [P.dbuf("vb", g0)])


def phase_att(k, which):
    nc, P = k.nc, k.P
    isA = which == "A"
    SMAX = max(k.seqs)
    nktm = SMAX // 128
    LOOK = 2
    sR = Ring(nc, "att_s", LOOK + 1, [128, 1024], F32, psum=True)
    oR = Ring(nc, "att_O", 2, [128, 512], F32, psum=True)
    qT = Ring(nc, "att_q", 2, [128, SMAX], BF16)
    kT = Ring(nc, "att_k", 2 if isA else 1, [128, SMAX], BF16)
    vR = Ring(nc, "att_v", 2 if isA else 1, [128, nktm, 130], BF16)
    pr = Ring(nc, "att_p", 4, [128, 1024], BF16)
    pm = Ring(nc, "att_pm", 4, [128, 1024], BF16, nsub=2) if isA else None
    ysb = Ring(nc, "att_y", 4, [64, 512], BF16)
    dsb = Ring(nc, "att_d", 4, [65, 512], F32)
    if isA:
        masks, masks_b = sb(nc, "masks", [128, 20, 512], BF16)
        P.dma("sp", lambda: nc.sync.dma_start(out=masks, in_=k.masks_d), [], [masks_b])
    for si, (off, S) in enumerate(zip(k.offs, k.seqs)):
        nkt = S // 128
        nqb = S // 512
        if not isA:
            kt_, ktb = kT.next()
            P.dma("sp", (lambda: nc.sync.dma_start(out=kt_[:, 0:S], in_=k.kbT[0, :, off:off + S])), [], [ktb])
            v_, vb_ = vR.next()
            P.dma("sp", (lambda: nc.sync.dma_start(
                out=v_[:, 0:nkt, :], in_=k.vb[off:off + S, :].rearrange("(t p) f -> p t f", p=128))), [], [vb_])
        for c in range(4):
            q_, qb_ = qT.next()
            qsrc = k.qaT if isA else k.qbT
            P.dma("sp", (lambda: nc.sync.dma_start(out=q_[:, 0:S], in_=qsrc[c, :, off:off + S])), [], [qb_])
            if isA:
                kt_, ktb = kT.next()
                P.dma("sp", (lambda: nc.sync.dma_start(out=kt_[:, 0:S], in_=k.kaT[c, :, off:off + S])), [], [ktb])
                v_, vb_ = vR.next()
                P.dma("sp", (lambda: nc.sync.dma_start(
                    out=v_[:, 0:nkt, :],
                    in_=k.va[off:off + S, c * 130:(c + 1) * 130].rearrange("(t p) f -> p t f", p=128))), [], [vb_])
            items = []
            for qi in range(nqb):
                q0 = qi * 512
                if isA:
                    kts = list(range(max(0, (q0 - 1024) // 128), min(nkt - 1, (q0 + 1535) // 128) + 1))
                else:
                    kts = list(range(nkt))
                for ii, kt in enumerate(kts):
                    items.append((qi, ii, kt, len(kts)))
            sq = {}

            def issue_qk(j):
                qi, ii, kt, n = items[j]
                s_, sb_ = sR.next()
                for hh in range(2):
                    pb = 64 * hh
                    P.op("pe", (lambda: nc.tensor.matmul(
                        s_[:, hh * 512:(hh + 1) * 512], lhsT=kt_[pb:pb + 64, kt * 128:kt * 128 + 128],
                        rhs=q_[pb:pb + 64, qi * 512:qi * 512 + 512], start=True, stop=True)), [ktb, qb_], [sb_])
                sq[j] = (s_, sb_)

            for j0 in range(min(LOOK, len(items))):
                issue_qk(j0)
            O = None
            for j, (qi, ii, kt, n) in enumerate(items):
                q0 = qi * 512
                k0 = kt * 128
                if ii == 0:
                    O = [oR.next(), oR.next()]
                s_, sb_ = sq.pop(j)
                p_, pb_ = pr.next()
                P.op("act", (lambda: nc.scalar.activation(out=p_, in_=s_, func=AF.Exp, scale=0.125)), [sb_], [pb_])
                if isA:
                    mi = (k0 - q0 + 1024) // 128
                    p2, p2b = pm.next()
                    for hh in range(2):
                        eng_, e_ = (("dve", nc.vector), ("pool", nc.gpsimd))[hh if MASK_SPLIT else 0]
                        P.op(eng_, (lambda: e_.tensor_tensor(
                            out=p2[:, hh * 512:(hh + 1) * 512], in0=p_[:, hh * 512:(hh + 1) * 512],
                            in1=masks[:, mi, :], op=ALU.mult)), [pb_, masks_b], [p2b[hh]])
                    p_, pb_ = p2, p2b
                if j + LOOK < len(items):
                    issue_qk(j + LOOK)
                for hh in range(2):
                    P.op("pe", (lambda: nc.tensor.matmul(
                        O[hh][0][0:65, :], lhsT=v_[:, kt, hh * 65:(hh + 1) * 65], rhs=p_[:, hh * 512:(hh + 1) * 512],
                        start=(ii == 0), stop=(ii == n - 1))), [vb_, (pb_[hh] if isinstance(pb_, list) else pb_)], [O[hh][1]])
                if ii == n - 1:
                    g0 = off + q0
                    for hh in range(2):
                        if isA:
                            head = 2 * c + hh
                            ochunk, obase, drow = c, 64 * hh, head
                        else:
                            head = c + 4 * hh
                            ochunk, obase, drow = 4 + head // 2, 64 * (head % 2), 8 + head
                        ops, opsb = O[hh]
                        y_, yb_ = ysb.next()
                        d_, db_ = dsb.next()
                        P.op("act", (lambda: nc.scalar.copy(out=y_, in_=ops[0:64, :])), [opsb], [yb_])
                        P.op("act", (lambda: nc.scalar.activation(out=d_[64:65, :], in_=ops[64:65, :], func=AF.Ln)),
                             [opsb], [db_])
                        P.op("act", (lambda: nc.scalar.activation(out=d_[64:65, :], in_=d_[64:65, :], func=AF.Exp,
                                                                  scale=-1.0)), [db_], [db_])
                        P.dma(STORE_Q, (lambda: k.stq.dma_start(
                            out=k.ycat[ochunk, obase:obase + 64, g0:g0 + 512], in_=y_)), [yb_], [P.dbuf("ycat", head, g0)])
                        P.dma(STORE_Q, (lambda: k.stq.dma_start(
                            out=k.den[drow:drow + 1, g0:g0 + 512], in_=d_[64:65, :])), [db_], [P.dbuf("den", drow, g0)])


def layer_norm_tile(k, r, rb_, lng, lnb, rings, out, outb):
    nc, P = k.nc, k.P
    sm = k.small
    rbf, r2f, stat = rings
    rb16, rb16b = rbf.next()
    r2, r2b = r2f.next()
    for n in range(NCH):
        P.op("act", (lambda n=n: nc.scalar.copy(out=rb16[:, n, :], in_=r[:, n, :])), [rb_[n]], [rb16b[n]])
        P.op("act", (lambda n=n: nc.scalar.activation(out=r2[:, n, :], in_=r[:, n, :], func=AF.Square)), [rb_[n]], [r2b[n]])
    mps, mpsb = k.ps.next()
    eps_, epsb = k.ps.next()
    for n in range(NCH):
        P.op("pe", (lambda n=n: nc.tensor.matmul(mps, lhsT=k.onesd, rhs=rb16[:, n, :], start=(n == 0), stop=(n == NCH - 1))),
             [k.onesd_b, rb16b[n]], [mpsb])
    for n in range(NCH):
        P.op("pe", (lambda n=n: nc.tensor.matmul(eps_, lhsT=k.onesd, rhs=r2[:, n, :], start=(n == 0), stop=(n == NCH - 1))),
             [k.onesd_b, r2b[n]], [epsb])
    m2, m2b = stat.next()
    P.op("act", (lambda: nc.scalar.activation(out=m2, in_=mps, func=AF.Square)), [mpsb], [m2b])
    P.op("dve", (lambda: nc.vector.tensor_tensor(out=m2, in0=eps_, in1=m2, op=ALU.subtract)), [epsb, m2b], [m2b])
    P.op("act", (lambda: nc.scalar.activation(out=m2, in_=m2, func=AF.Ln, bias=k.epsl[:, 0:1], scale=1.0)),
         [m2b, k.epsl_b], [m2b])
    P.op("act", (lambda: nc.scalar.activation(out=m2, in_=m2, func=AF.Exp, scale=-0.5)), [m2b], [m2b])
    for n in range(NCH):
        P.op("dve", (lambda n=n: nc.vector.tensor_tensor(out=r[:, n, :], in0=r[:, n, :], in1=mps, op=ALU.subtract)),
             [rb_[n], mpsb], [rb_[n]])
        P.op("dve", (lambda n=n: nc.vector.scalar_tensor_tensor(
            out=r[:, n, :], in0=r[:, n, :], scalar=sm[:, lng + n:lng + n + 1], in1=m2, op0=ALU.mult, op1=ALU.mult)),
            [rb_[n], m2b, k.small_b], [rb_[n]])
        P.op("act", (lambda n=n: nc.scalar.activation(
            out=out[:, n, :], in_=r[:, n, :], func=AF.Identity, bias=sm[:, lnb + n:lnb + n + 1], scale=1.0)),
            [rb_[n], k.small_b], [outb[n]])


def ln_rings(k, tag):
    nc = k.nc
    return (Ring(nc, tag + "_rb16", 1, [128, NCH, 512], BF16, nsub=NCH), Ring(nc, tag + "_r2", 1, [128, NCH, 512], BF16, nsub=NCH),
            Ring(nc, tag + "_stat", 2, [128, 512], F32))


def phase_proj_ln(k, src, resid, dst, w_dram, nk, lng, lnb, tag, den=None):
    nc, P = k.nc, k.P
    W, Wb = load_weight_bf16(k, "w_" + tag, w_dram, nk, D)
    srcR = Ring(nc, tag + "_src", 2, [128, nk, 512], BF16)
    resR = Ring(nc, tag + "_res", 1 if nk > 8 else 2, [128, NCH, 512], F32)
    rR = Ring(nc, tag + "_r", 2, [128, NCH, 512], F32, nsub=NCH)
    outR = Ring(nc, tag + "_out", 1 if nk > 8 else 2, [128, NCH, 512], F32, nsub=NCH)
    lnr = ln_rings(k, tag)
    if den is not None:
        dnR = Ring(nc, tag + "_dn", 1, [128, nk, 512], F32)
    if dst is None:
        ytok = Ring(nc, tag + "_ytok", 2, [128, D], F32)
        ident = k.consts[:, C_IDENT:C_IDENT + 128]
    for si, off, S, p0, g0 in tiles512(k):
        s_, sb_ = srcR.next()
        P.dma("sp", (lambda: nc.sync.dma_start(out=s_, in_=src[:, :, g0:g0 + 512].rearrange("c p t -> p c t"))), [], [sb_])
        if den is not None:
            dn, dnb = dnR.next()
            for half in range(2):
                P.dma("sp", (lambda: nc.sync.dma_start(
                    out=dn[64 * half:64 * half + 64, :, :],
                    in_=den[:, g0:g0 + 512].rearrange("(n h) t -> h n t", h=2)[half:half + 1].broadcast_to([64, nk, 512]))),
                    [], [dnb])
            P.op("dve", (lambda: nc.vector.tensor_tensor(
                out=s_.rearrange("p c t -> p (c t)"), in0=s_.rearrange("p c t -> p (c t)"),
                in1=dn.rearrange("p c t -> p (c t)"), op=ALU.mult)), [sb_, dnb], [sb_])
        x_, xb_ = resR.next()
        P.dma("sp", (lambda: nc.sync.dma_start(out=x_, in_=resid[:, :, g0:g0 + 512].rearrange("c p t -> p c t"))), [], [xb_])
        r, rb_ = rR.next()
        for n in range(NCH):
            ps, psb = k.ps.next()
            for kk in range(nk):
                P.op("pe", (lambda: nc.tensor.matmul(ps, lhsT=W[:, kk, n * 128:(n + 1) * 128], rhs=s_[:, kk, :],
                                                     start=(kk == 0), stop=(kk == nk - 1))), [Wb, sb_], [psb])
            P.op("dve", (lambda: nc.vector.scalar_tensor_tensor(
                out=r[:, n, :], in0=x_[:, n, :], scalar=ALPHA, in1=ps, op0=ALU.mult, op1=ALU.add)), [xb_, psb], [rb_[n]])
        o_, ob_ = outR.next()
        layer_norm_tile(k, r, rb_, lng, lnb, lnr, o_, ob_)
        if dst is not None:
            P.dma(STORE_Q, (lambda: k.stq.dma_start(out=dst[:, :, g0:g0 + 512].rearrange("c p t -> p c t"), in_=o_)),
                  [ob_], [P.dbuf(tag, g0)])
        else:
            for g in range(4):
                yt, ytb = ytok.next()
                for half in range(2):
                    ps, psb = k.ps.next()
                    for j in range(4):
                        n = half * 4 + j
                        P.op("pe", (lambda: nc.tensor.transpose(
                            out=ps[:, j * 128:(j + 1) * 128], in_=o_[:, n, g * 128:(g + 1) * 128], identity=ident)),
                            [ob_[n], k.consts_b], [psb])
                    if half == 0:
                        P.op("act", (lambda: nc.scalar.copy(out=yt[:, 0:512], in_=ps)), [psb], [ytb])
                    else:
                        P.op("dve", (lambda: nc.vector.tensor_copy(out=yt[:, 512:1024], in_=ps)), [psb], [ytb])
                r0 = g0 + g * 128
                P.dma(STORE_Q, (lambda: k.stq.dma_start(out=k.y[r0:r0 + 128, :], in_=yt)), [ytb],
                      [P.dbuf("y", r0)], is_out=True)


GELU_NATIVE = True
MASK_SPLIT = False


def phase_ffn_up(k, l, xsrc):
    nc, P = k.nc, k.P
    W, Wb = load_weight_bf16(k, "w_up%d" % l, k.w_up[l], 8, 2 * DFF)
    cw = SP_CW0 if l == 0 else SP_CW1
    cbc = SP_CB0 if l == 0 else SP_CB1
    sm = k.small
    stg = Ring(nc, "fu_stg", 3, [128, 512], F32)
    xbR = Ring(nc, "fu_xb", 2, [128, NCH, 512], BF16, nsub=NCH)
    tR = Ring(nc, "fu_t", 3, [128, 512], F32)
    gR = Ring(nc, "fu_g", 2, [128, 512], F32)
    actR = Ring(nc, "fu_act", 2, [128, NF, 512], BF16, nsub=NF)
    tl = []
    for si, (off, S) in enumerate(zip(k.offs, k.seqs)):
        nt = -(-S // 510)
        wbase = -(-S // nt)
        wbase += wbase % 2
        a = 0
        while a < S:
            w = min(wbase, S - a)
            tl.append((off, S, a, w))
            a += w
    xq = {}

    def load_x(ti):
        off, S, a, w = tl[ti]
        lo, hi = max(a - 1, 0), min(a + w + 1, S)
        dcol = lo - (a - 1)
        xb, xbb = xbR.next()
        if a == 0:
            P.op("pool", (lambda: nc.gpsimd.memset(xb[:, :, 0:1], 0.0)), [], [xbb])
        if a + w == S:
            P.op("pool", (lambda: nc.gpsimd.memset(xb[:, :, w + 1:w + 2], 0.0)), [], [xbb])
        for c in range(NCH):
            st, stb = stg.next()
            P.dma("sp", (lambda: nc.sync.dma_start(out=st[:, 0:hi - lo], in_=xsrc[c, :, off + lo:off + hi])), [], [stb])
            P.op("pool", (lambda: nc.gpsimd.tensor_copy(out=xb[:, c, dcol:dcol + hi - lo], in_=st[:, 0:hi - lo])),
                 [stb], [xbb[c]])
        xq[ti] = (xb, xbb)

    load_x(0)
    for ti, (off, S, a, w) in enumerate(tl):
        if True:
            xb, xbb = xq.pop(ti)
            if ti + 1 < len(tl):
                load_x(ti + 1)
            act, actb = actR.next()
            for f in range(NF):
                gps, gpsb = k.ps.next()
                for c in range(NCH):
                    P.op("pe", (lambda: nc.tensor.matmul(
                        gps[:, 0:w + 2], lhsT=W[:, c, DFF + f * 128:DFF + (f + 1) * 128], rhs=xb[:, c, 0:w + 2],
                        start=(c == 0), stop=(c == NCH - 1))), [Wb, xbb[c]], [gpsb])
                ups, upsb = k.ps.next()
                for c in range(NCH):
                    P.op("pe", (lambda: nc.tensor.matmul(
                        ups[:, 0:w], lhsT=W[:, c, f * 128:(f + 1) * 128], rhs=xb[:, c, 1:w + 1],
                        start=(c == 0), stop=(c == NCH - 1))), [Wb, xbb[c]], [upsb])
                t, tb_ = tR.next()
                P.op("act", (lambda: nc.scalar.activation(
                    out=t[:, 0:w], in_=gps[:, 1:w + 1], func=AF.Identity,
                    scale=sm[:, cw + NF + f:cw + NF + f + 1], bias=sm[:, cbc + f:cbc + f + 1])), [gpsb, k.small_b], [tb_])
                P.op("dve", (lambda: nc.vector.scalar_tensor_tensor(
                    out=t[:, 0:w], in0=gps[:, 0:w], scalar=sm[:, cw + f:cw + f + 1], in1=t[:, 0:w],
                    op0=ALU.mult, op1=ALU.add)), [gpsb, tb_, k.small_b], [tb_])
                P.op("dve", (lambda: nc.vector.scalar_tensor_tensor(
                    out=t[:, 0:w], in0=gps[:, 2:w + 2], scalar=sm[:, cw + 2 * NF + f:cw + 2 * NF + f + 1], in1=t[:, 0:w],
                    op0=ALU.mult, op1=ALU.add)), [gpsb, tb_, k.small_b], [tb_])
                ge, geb = gR.next()
                if GELU_NATIVE:
                    P.op("act", (lambda: nc.scalar.activation(out=ge[:, 0:w], in_=t[:, 0:w], func=AF.Gelu_apprx_tanh)),
                         [tb_], [geb])
                else:
                    P.op("act", (lambda: nc.scalar.activation(out=ge[:, 0:w], in_=t[:, 0:w], func=AF.Square)), [tb_], [geb])
                    P.op("pool", (lambda: nc.gpsimd.tensor_scalar(
                        out=ge[:, 0:w], in0=ge[:, 0:w], scalar1=0.044715, scalar2=1.0, op0=ALU.mult, op1=ALU.add)),
                        [geb], [geb])
                    P.op("pool", (lambda: nc.gpsimd.tensor_tensor(out=ge[:, 0:w], in0=ge[:, 0:w], in1=t[:, 0:w], op=ALU.mult)),
                         [geb, tb_], [geb])
                    P.op("act", (lambda: nc.scalar.activation(out=ge[:, 0:w], in_=ge[:, 0:w], func=AF.Sigmoid,
                                                              scale=1.5957691216057308)), [geb], [geb])
                    P.op("pool", (lambda: nc.gpsimd.tensor_tensor(out=ge[:, 0:w], in0=ge[:, 0:w], in1=t[:, 0:w], op=ALU.mult)),
                         [geb, tb_], [geb])
                P.op("dve", (lambda: nc.vector.tensor_tensor(out=act[:, f, 0:w], in0=ge[:, 0:w], in1=ups[:, 0:w], op=ALU.mult)),
                     [geb, upsb], [actb[f]])
            g0 = off + a
            P.dma(STORE_Q, (lambda: k.stq.dma_start(
                out=k.actT[:, :, g0:g0 + w].rearrange("f p t -> p f t"), in_=act[:, :, 0:w])), [actb], [P.dbuf("actT", g0)])


def phase_hgrn(k, fwd):
    nc, P = k.nc, k.P
    sm = k.small
    Wz, Wzb = load_weight_bf16(k, "hw_z", k.w_c, 8, 1024, col0=(1024 if fwd else 2048), sw=1024)
    if fwd:
        Wq, Wqb = load_weight_bf16(k, "hw_q", k.w_c, 8, 1024, col0=0, sw=1024)
        Wi, Wib = load_weight_bf16(k, "hw_i", k.w_c, 8, 1024, col0=3072, sw=1024)
    if not fwd:
        Wg, Wgb = load_weight_bf16(k, "hw_g", k.w_c, 8, 1024, col0=4096, sw=1024)
    C = k.consts
    tri_inc = C[:, C_TRIL:C_TRIL + 128] if fwd else C[:, C_TRIU:C_TRIU + 128]
    tri_suf = C[:, C_TRIU_S:C_TRIU_S + 128] if fwd else C[:, C_TRIL_S:C_TRIL_S + 128]
    onesv = C[:, C_ONESV:C_ONESV + 128]
    lc = 63 if fwd else 0
    lbcol, lbcolb = sb(nc, "lbcol", [128, 8], F32)
    omlcol, omlcolb = sb(nc, "omlcol", [128, 8], F32)
    spb = SP_LBF if fwd else SP_LBB
    P.op("dve", (lambda: nc.vector.tensor_tensor(out=lbcol, in0=sm[:, spb + 8:spb + 16], in1=sm[:, spb:spb + 8],
                                                 op=ALU.subtract)), [k.small_b], [lbcolb])
    P.op("act", (lambda: nc.scalar.activation(out=lbcol, in_=lbcol, func=AF.Sigmoid)), [lbcolb], [lbcolb])
    P.op("dve", (lambda: nc.vector.tensor_scalar(out=omlcol, in0=lbcol, scalar1=-1.0, scalar2=1.0,
                                                 op0=ALU.mult, op1=ALU.add)), [lbcolb], [omlcolb])
    lbrep, lbrepb = sb(nc, "lbrep", [128, 1024], F32)
    omlrep, omlrepb = sb(nc, "omlrep", [128, 1024], F32)
    r0 = 0 if fwd else 2
    P.dma("sp", (lambda: nc.sync.dma_start(out=lbrep, in_=k.lbrep_d[:, r0 + 1, :])), [], [lbrepb])
    P.dma("sp", (lambda: nc.sync.dma_start(out=omlrep, in_=k.lbrep_d[:, r0, :])), [], [omlrepb])
    P.op("dve", (lambda: nc.vector.tensor_tensor(out=lbrep, in0=lbrep, in1=omlrep, op=ALU.subtract)),
         [lbrepb, omlrepb], [lbrepb])
    P.op("act", (lambda: nc.scalar.activation(out=lbrep, in_=lbrep, func=AF.Sigmoid)), [lbrepb], [lbrepb])
    P.op("dve", (lambda: nc.vector.tensor_scalar(out=omlrep, in0=lbrep, scalar1=-1.0, scalar2=1.0,
                                                 op0=ALU.mult, op1=ALU.add)), [lbrepb], [omlrepb])
    mask4, mask4b = sb(nc, "mask4", [128, 4, 128], BF16)
    for g in range(4):
        P.op("dve", (lambda g=g: nc.vector.tensor_copy(out=mask4[:, g, :], in_=tri_inc)), [k.consts_b], [mask4b])
    S32, S32b = sb(nc, "hS32", [128, 8, 128], F32, nsub=8)
    S16, S16b = sb(nc, "hS16", [128, 8, 128], BF16, nsub=8)
    stg = Ring(nc, "h_stg", 2, [128, 512], F32)
    xbR = Ring(nc, "h_xb", 2, [128, NCH, 512], BF16, nsub=NCH)
    LFr = Ring(nc, "h_LF", 1, [128, 4, 512], F32, nsub=4)
    KDr = Ring(nc, "h_KD", 1, [128, 4, 512], BF16, nsub=4)
    Vr = Ring(nc, "h_V", 1, [128, 4, 512], BF16, nsub=4)
    tA = Ring(nc, "h_tA", 4, [128, 512], F32)
    tB = Ring(nc, "h_tB", 4, [128, 512], F32)
    tC = Ring(nc, "h_tC", 2, [128, 512], F32)
    qbR = Ring(nc, "h_qb", 1, [128, 4, 512], BF16, nsub=4)
    kbR = Ring(nc, "h_kb", 4, [128, 512], BF16)
    atR = Ring(nc, "h_at", 1, [128, 4, 512], BF16, nsub=4)
    ebl = Ring(nc, "h_ebl", 1, [128, 4, 8], F32, nsub=4)
    Or = Ring(nc, "h_O", 1, [128, 4, 512], F32, nsub=4)
    qrR = Ring(nc, "h_qr", 1, [128, 4, 512], BF16, nsub=4)
    KKr = Ring(nc, "h_KK", 1, [128, 4, 512], BF16, nsub=4)
    identb = k.cbf[:, C_IDENT:C_IDENT + 128]
    if not fwd:
        ofr = Ring(nc, "h_of", 1, [128, 4, 512], F32, nsub=4)
        ogr = Ring(nc, "h_og", 1, [128, 4, 512], BF16, nsub=4)

    tiles = []
    for si, (off, S) in enumerate(zip(k.offs, k.seqs)):
        tl = list(range(S // 512))
        if not fwd:
            tl = tl[::-1]
        for n_, j in enumerate(tl):
            tiles.append((off + j * 512, n_ == 0))
    xq = {}

    def load_x(ti):
        g0 = tiles[ti][0]
        xb, xbb = xbR.next()
        for c in range(NCH):
            st, stb = stg.next()
            P.dma("sp", (lambda: nc.sync.dma_start(out=st, in_=k.x2T[c, :, g0:g0 + 512])), [], [stb])
            P.op("pool", (lambda: nc.gpsimd.tensor_copy(out=xb[:, c, :], in_=st)), [stb], [xbb[c]])
        xq[ti] = (xb, xbb)

    load_x(0)
    for ti, (g0, first) in enumerate(tiles):
        if first:
            P.op("pool", (lambda: nc.gpsimd.memset(S32, 0.0)), [], [S32b])
            P.op("pool", (lambda: nc.gpsimd.memset(S16, 0.0)), [], [S16b])
        xb, xbb = xq.pop(ti)
        for hg in range(2):
            hs = slice(hg * 512, (hg + 1) * 512)
            LF, LFb = LFr.next()
            KD, KDb = KDr.next()
            V, Vb = Vr.next()
            KK, KKb = KKr.next()
            G4 = range(4)
            tsl = [slice(g * 128, (g + 1) * 128) for g in G4]
            zp = []
            for g in G4:
                zps, zpsb = k.ps.next()
                for c in range(NCH):
                    P.op("pe", (lambda: nc.tensor.matmul(zps, lhsT=xb[:, c, tsl[g]], rhs=Wz[:, c, hs],
                                                         start=(c == 0), stop=(c == NCH - 1))), [xbb[c], Wzb], [zpsb])
                zp.append((zps, zpsb))
            aa = []
            for g in G4:
                a_, ab_ = tA.next()
                zps, zpsb = zp[g]
                P.op("act", (lambda: nc.scalar.activation(out=a_, in_=zps, func=AF.Sigmoid)), [zpsb], [ab_])
                aa.append((a_, ab_))
            for g in G4:
                a_, ab_ = aa[g]
                P.op("dve", (lambda: nc.vector.tensor_tensor(out=a_, in0=a_, in1=omlrep[:, hs], op=ALU.mult)),
                     [ab_, omlrepb], [ab_])
                P.op("pool", (lambda: nc.gpsimd.tensor_tensor(out=a_, in0=a_, in1=lbrep[:, hs], op=ALU.add)),
                     [ab_, lbrepb], [ab_])
            for g in G4:
                a_, ab_ = aa[g]
                P.op("act", (lambda: nc.scalar.activation(out=LF[:, g, :], in_=a_, func=AF.Ln)), [ab_], [LFb[g]])
            sp_ = []
            for g in G4:
                a_, ab_ = aa[g]
                P.op("pool", (lambda: nc.gpsimd.tensor_scalar(out=KK[:, g, :], in0=a_, scalar1=-1.0, scalar2=1.0,
                                                              op0=ALU.mult, op1=ALU.add)), [ab_], [KKb[g]])
                sps, spsb = k.ps.next()
                P.op("pe", (lambda: nc.tensor.matmul(sps, lhsT=tri_suf, rhs=LF[:, g, :], start=True, stop=True)),
                     [LFb[g], k.consts_b], [spsb])
                sp_.append((sps, spsb))
            bb = []
            for g in G4:
                b_, bb_ = tB.next()
                sps, spsb = sp_[g]
                P.op("act", (lambda: nc.scalar.activation(out=b_, in_=sps, func=AF.Exp)), [spsb], [bb_])
                bb.append((b_, bb_))
            for g in G4:
                a_, ab_ = aa[g]
                b_, bb_ = bb[g]
                P.op("dve", (lambda: nc.vector.tensor_tensor(out=KD[:, g, :], in0=KK[:, g, :], in1=b_, op=ALU.mult)),
                     [KKb[g], bb_], [KDb[g]])
            qb, qbb = qbR.next()
            at, atb = atR.next()
            el, elb = ebl.next()
            H4 = range(4)
            fsl = [slice((hg * 4 + hl) * 128, (hg * 4 + hl + 1) * 128) for hl in H4]
            lsl = [slice(hl * 128, (hl + 1) * 128) for hl in H4]
            btp = []
            for hl in H4:
                bt, btb = k.ps.next()
                for g in G4:
                    P.op("pe", (lambda: nc.tensor.matmul(bt[:, g * 128:(g + 1) * 128], lhsT=LF[:, g, lsl[hl]], rhs=tri_inc,
                                                         start=True, stop=True)), [LFb[g], k.consts_b], [btb])
                btp.append((bt, btb))
            eb, enb = [], []
            for hl in H4:
                bt, btb = btp[hl]
                a_, ab_ = tA.next()
                P.op("act", (lambda: nc.scalar.activation(out=a_, in_=bt, func=AF.Exp)), [btb], [ab_])
                b_, bb_ = tB.next()
                P.op("act", (lambda: nc.scalar.activation(out=b_, in_=bt, func=AF.Exp, scale=-1.0)), [btb], [bb_])
                eb.append((a_, ab_))
                enb.append((b_, bb_))
            if fwd:
                ip = []
                for g in G4:
                    ips, ipsb = k.ps.next()
                    for c in range(NCH):
                        P.op("pe", (lambda: nc.tensor.matmul(ips, lhsT=xb[:, c, tsl[g]], rhs=Wi[:, c, hs],
                                                             start=(c == 0), stop=(c == NCH - 1))), [xbb[c], Wib], [ipsb])
                    ip.append((ips, ipsb))
                for g in G4:
                    ips, ipsb = ip[g]
                    P.op("act", (lambda: nc.scalar.activation(out=V[:, g, :], in_=ips, func=AF.Silu)), [ipsb], [Vb[g]])
                P.dma(STORE_Q, (lambda: k.stq.dma_start(
                    out=k.hv[g0:g0 + 512, hs].rearrange("(g p) f -> p g f", p=128), in_=V)), [Vb], [P.dbuf("hv", hg, g0)])
            else:
                P.dma("sp", (lambda: nc.sync.dma_start(
                    out=V, in_=k.hv[g0:g0 + 512, hs].rearrange("(g p) f -> p g f", p=128))), [], [Vb])
            for hl in H4:
                a_, ab_ = eb[hl]
                P.op("pool", (lambda: nc.gpsimd.tensor_copy(
                    out=el[:, hl, :], in_=a_.rearrange("p (c t) -> p c t", t=64)[:, :, lc])), [ab_], [elb[hl]])
            qr, qrb = qrR.next()
            if not fwd:
                P.dma("sp", (lambda: nc.sync.dma_start(
                    out=qr, in_=k.hq[hg * 4:(hg + 1) * 4, :, g0:g0 + 512].rearrange("c p t -> p c t"))), [], [qrb])
            for hl in H4:
                a_, ab_ = eb[hl]
                if fwd:
                    qps, qpsb = k.ps.next()
                    for c in range(NCH):
                        P.op("pe", (lambda: nc.tensor.matmul(qps, lhsT=Wq[:, c, fsl[hl]], rhs=xb[:, c, :],
                                                             start=(c == 0), stop=(c == NCH - 1))), [xbb[c], Wqb], [qpsb])
                    P.op("act", (lambda: nc.scalar.copy(out=qr[:, hl, :], in_=qps)), [qpsb], [qrb[hl]])
                P.op("dve", (lambda: nc.vector.tensor_tensor(out=qb[:, hl, :], in0=qr[:, hl, :], in1=a_, op=ALU.mult)),
                     [qrb[hl], ab_], [qbb[hl]])
            if fwd:
                P.dma(STORE_Q, (lambda: k.stq.dma_start(
                    out=k.hq[hg * 4:(hg + 1) * 4, :, g0:g0 + 512].rearrange("c p t -> p c t"), in_=qr)),
                    [qrb], [P.dbuf("hq", hg, g0)])
            ktp = []
            for hl in H4:
                kt_, ktb_ = k.ps.next()
                for g in G4:
                    P.op("pe", (lambda: nc.tensor.matmul(kt_[:, g * 128:(g + 1) * 128], lhsT=KK[:, g, lsl[hl]], rhs=identb,
                                                         start=True, stop=True)), [KKb[g], k.cbf_b], [ktb_])
                ktp.append((kt_, ktb_))
            kbs = []
            for hl in H4:
                kt_, ktb_ = ktp[hl]
                b_, bb_ = enb[hl]
                kb, kbb = kbR.next()
                P.op("dve", (lambda: nc.vector.tensor_tensor(out=kb, in0=kt_, in1=b_, op=ALU.mult)),
                     [ktb_, bb_], [kbb])
                kbs.append((kb, kbb))
            for hl in H4:
                kb, kbb = kbs[hl]
                aps, apsb = k.ps.next()
                for g in G4:
                    gs = tsl[g]
                    P.op("pe", (lambda: nc.tensor.matmul(aps[:, gs], lhsT=kb[:, gs], rhs=qb[:, hl, gs],
                                                         start=True, stop=True)), [kbb, qbb[hl]], [apsb])
                P.op("dve", (lambda: nc.vector.tensor_tensor(
                    out=at[:, hl, :], in0=aps, in1=mask4.rearrange("p g t -> p (g t)"), op=ALU.mult)),
                    [apsb, mask4b], [atb[hl]])
            if hg == 1 and ti + 1 < len(tiles):
                load_x(ti + 1)
            O, Ob = Or.next()
            order = list(range(8)) if fwd else list(range(7, -1, -1))
            for ci in order:
                g = ci // 2
                pb = 64 * (ci % 2)
                cs = slice(ci * 64, ci * 64 + 64)
                for hl in H4:
                    h = hg * 4 + hl
                    ls = lsl[hl]
                    ops_, opsb = k.ps.next()
                    P.op("pe", (lambda: nc.tensor.matmul(ops_[:, 0:64], lhsT=S16[:, h, :], rhs=qb[:, hl, cs],
                                                         start=True, stop=False)), [S16b[h], qbb[hl]], [opsb])
                    P.op("pe", (lambda: nc.tensor.matmul(ops_[:, 0:64], lhsT=V[:, g, ls], rhs=at[:, hl, cs],
                                                         start=False, stop=True)), [Vb[g], atb[hl]], [opsb])
                    P.op("act", (lambda: nc.scalar.copy(out=O[:, hl, cs], in_=ops_[:, 0:64])), [opsb], [Ob[hl]])
                    dps, dpsb = k.ps.next()
                    P.op("pe", (lambda: nc.tensor.matmul(dps[:, 0:128], lhsT=KD[pb:pb + 64, g, ls], rhs=V[pb:pb + 64, g, ls],
                                                         start=True, stop=True)), [KDb[g], Vb[g]], [dpsb])
                    P.op("dve", (lambda: nc.vector.scalar_tensor_tensor(
                        out=S32[:, h, :], in0=S32[:, h, :], scalar=el[:, hl, ci:ci + 1], in1=dps[:, 0:128],
                        op0=ALU.mult, op1=ALU.add)), [S32b[h], elb[hl], dpsb], [S32b[h]])
                    P.op("dve", (lambda: nc.vector.tensor_copy(out=S16[:, h, :], in_=S32[:, h, :])), [S32b[h]], [S16b[h]])
            if fwd:
                P.dma(STORE_Q, (lambda: k.stq.dma_start(
                    out=k.ofT[hg * 4:(hg + 1) * 4, :, g0:g0 + 512].rearrange("c p t -> p c t"), in_=O)),
                    [Ob], [P.dbuf("ofT", hg, g0)])
            else:
                of, ofb = ofr.next()
                P.dma("sp", (lambda: nc.sync.dma_start(
                    out=of, in_=k.ofT[hg * 4:(hg + 1) * 4, :, g0:g0 + 512].rearrange("c p t -> p c t"))), [], [ofb])
                og, ogb = ogr.next()
                sqs, gpl, rs_, sgl = [], [], [], []
                for hl in H4:
                    P.op("pool", (lambda: nc.gpsimd.tensor_tensor(out=of[:, hl, :], in0=of[:, hl, :], in1=O[:, hl, :],
                                                                  op=ALU.add)), [ofb[hl], Ob[hl]], [ofb[hl]])
                for hl in H4:
                    a_, ab_ = tA.next()
                    P.op("act", (lambda: nc.scalar.activation(out=a_, in_=of[:, hl, :], func=AF.Square)), [ofb[hl]], [ab_])
                    ms, msb = k.ps.next()
                    P.op("pe", (lambda: nc.tensor.matmul(ms, lhsT=onesv, rhs=a_, start=True, stop=True)),
                         [ab_, k.consts_b], [msb])
                    sqs.append((ms, msb))
                for hl in H4:
                    ms, msb = sqs[hl]
                    b_, bb_ = tB.next()
                    P.op("act", (lambda: nc.scalar.activation(out=b_, in_=ms, func=AF.Ln, bias=k.epsr[:, 0:1],
                                                              scale=1.0)), [msb, k.epsr_b], [bb_])
                    rs_.append((b_, bb_))
                for hl in H4:
                    b_, bb_ = rs_[hl]
                    P.op("act", (lambda: nc.scalar.activation(out=b_, in_=b_, func=AF.Exp, scale=-0.5)), [bb_], [bb_])
                for hl in H4:
                    gps, gpsb = k.ps.next()
                    for c in range(NCH):
                        P.op("pe", (lambda: nc.tensor.matmul(gps, lhsT=Wg[:, c, fsl[hl]], rhs=xb[:, c, :],
                                                             start=(c == 0), stop=(c == NCH - 1))), [xbb[c], Wgb], [gpsb])
                    gpl.append((gps, gpsb))
                for hl in H4:
                    gps, gpsb = gpl[hl]
                    c_, cb_ = tC.next()
                    P.op("act", (lambda: nc.scalar.activation(out=c_, in_=gps, func=AF.Silu)), [gpsb], [cb_])
                    h = hg * 4 + hl
                    b_, bb_ = rs_[hl]
                    P.op("dve", (lambda: nc.vector.scalar_tensor_tensor(
                        out=b_, in0=of[:, hl, :], scalar=sm[:, SP_GNC + h:SP_GNC + h + 1], in1=b_,
                        op0=ALU.mult, op1=ALU.mult)), [ofb[hl], bb_, k.small_b], [bb_])
                    P.op("dve", (lambda: nc.vector.tensor_tensor(out=og[:, hl, :], in0=b_, in1=c_, op=ALU.mult)),
                         [bb_, cb_], [ogb[hl]])
                P.dma(STORE_Q, (lambda: k.stq.dma_start(
                    out=k.ycat[hg * 4:(hg + 1) * 4, :, g0:g0 + 512].rearrange("c p t -> p c t"), in_=og)),
                    [ogb], [P.dbuf("og", hg, g0)])


def host_shared(inp, smax):
    d = {}
    f32 = lambda a: np.ascontiguousarray(np.asarray(a, np.float32))
    d["w_ab"] = f32(np.asarray(inp["w_in_ab"])[0][:, ab_columns()])
    d["w_oab"] = f32(np.asarray(inp["w_out_ab"])[0])
    d["w_c"] = f32(np.asarray(inp["w_in_c"])[0])
    d["w_oc"] = f32(np.asarray(inp["w_out_c"])[0])
    for l in range(2):
        d["w_up%d" % l] = f32(np.asarray(inp["ffn_w_up"])[l])
        d["w_dn%d" % l] = f32(np.asarray(inp["ffn_w_down"])[l])
    ca, sa, cb, sbb = rope_tables(smax)
    d["t_ca"], d["t_sa"], d["t_cb"], d["t_sb"] = ca, sa, cb, sbb
    d["masks"] = dil_masks()
    sp = np.zeros((128, NSMALL), np.float32)
    for l in range(2):
        sp[:, SP_LNMG0 + 32 * l:SP_LNMG0 + 32 * l + 8] = col128(np.asarray(inp["ln_mix_g"])[l], 8)
        sp[:, SP_LNMB0 + 32 * l:SP_LNMB0 + 32 * l + 8] = col128(np.asarray(inp["ln_mix_b"])[l], 8)
        sp[:, SP_LNFG0 + 32 * l:SP_LNFG0 + 32 * l + 8] = col128(np.asarray(inp["ln_ffn_g"])[l], 8)
        sp[:, SP_LNFB0 + 32 * l:SP_LNFB0 + 32 * l + 8] = col128(np.asarray(inp["ln_ffn_b"])[l], 8)
    pb = rope_perm_B()
    qn = np.asarray(inp["qn_ab"], np.float32)[0]
    kn = np.asarray(inp["kn_ab"], np.float32)[0]
    sp[:, SP_QN] = np.concatenate([qn, qn])
    sp[:, SP_QNP] = np.concatenate([qn[pb], qn[pb]])
    sp[:, SP_KN] = np.concatenate([kn, kn])
    sp[:, SP_KNP] = np.concatenate([kn[pb], kn[pb]])
    sp[:, SP_GNC:SP_GNC + 8] = col128(np.asarray(inp["gn_c"])[0], 8)
    for l, (cw, cbias) in enumerate(((SP_CW0, SP_CB0), (SP_CW1, SP_CB1))):
        w = np.asarray(inp["ffn_conv_w"], np.float32)[l]
        for j in range(3):
            sp[:, cw + j * NF:cw + (j + 1) * NF] = col128(w[j], NF)
        sp[:, cbias:cbias + NF] = col128(np.asarray(inp["ffn_conv_b"])[l], NF)
    lbf = np.asarray(inp["lb_fwd"], np.float32)
    lbb = np.asarray(inp["lb_bwd"], np.float32)
    for l in range(2):
        sp[:, SP_LBF + 8 * l:SP_LBF + 8 * l + 8] = col128(lbf[l], 8)
        sp[:, SP_LBB + 8 * l:SP_LBB + 8 * l + 8] = col128(lbb[l], 8)
    d["smallp"] = sp
    rep = np.zeros((128, 4, 1024), np.float32)
    rep[:, 0, :] = lbf[0][None, :]
    rep[:, 1, :] = lbf[1][None, :]
    rep[:, 2, :] = lbb[0][None, :]
    rep[:, 3, :] = lbb[1][None, :]
    d["lbrep"] = rep
    d["consts"] = host_consts()
    return d


ALL_PHASES = ("p1", "attA", "attB", "p3", "f0a", "f0b", "h1", "h2", "p6", "f1a", "f1b")


def kernel(**inputs):
    seqs = [2048, 8192]
    xp = np.asarray(inputs["x_prompt"], np.float32)
    xs = np.asarray(inputs["x_sample"], np.float32)
    shared = host_shared(inputs, max(seqs))
    nc = build2(seqs, ALL_PHASES)
    in_maps = []
    for c in range(8):
        m = dict(shared)
        m["x"] = np.ascontiguousarray(np.concatenate([xp[c], xs[c]], 0))
        in_maps.append(m)
    res = run_bass_kernel_spmd(nc, in_maps, core_ids=list(range(8)))
    yp = np.stack([res.results[c]["y"][0:2048] for c in range(8)], 0)
    ys = np.stack([res.results[c]["y"][2048:] for c in range(8)], 0)
    return (yp.astype(np.float32), ys.astype(np.float32))
```
